# Optimizing a Trainium2 kernel written in Bass

```python
import math
import jax
import jax.numpy as jnp
from jax import lax
import numpy as np

D_MODEL = 1024
BATCH = 8
SEQ = 2048
DEPTH = 1
DEC_BATCH = 128
DEC_SEQ = 4
PAST_LEN = 16384
PAGE_SIZE = 128

HG_HEADS = 4
HG_DK = 128
HG_DV = 128
HG_WIDTH = HG_HEADS * HG_DK
GDN_HEADS = 4
GDN_DK = 128
GDN_DV = 128
GDN_QK = GDN_HEADS * GDN_DK
GDN_V = GDN_HEADS * GDN_DV
CONV_W = 4
CONV_CH = 2 * GDN_QK + GDN_V
MEM_LEN = 256
MEM_HEADS = 4
MEM_HD = 128
MEM_WIDTH = MEM_HEADS * MEM_HD
N_BRANCH = 3
FF_RAW = -(-8 * D_MODEL // 3)
FFN_HIDDEN = -(-FF_RAW // 256) * 256
IN_WIDTH = 4 * HG_WIDTH + CONV_CH + GDN_V + 2 * GDN_HEADS + MEM_WIDTH + N_BRANCH * D_MODEL
CHUNK = 64
EPS = 1e-6

kernel_name = "hgrn2_gdn_memxattn_gated_parallel_decoder_step"

F32 = jnp.float32


def rmsnorm(x, g):
    xf = x.astype(F32)
    y = xf * lax.rsqrt(jnp.mean(xf * xf, axis=-1, keepdims=True) + EPS)
    return (y * g.astype(F32)).astype(x.dtype)


def l2norm(x):
    xf = x.astype(F32)
    return xf * lax.rsqrt(jnp.sum(xf * xf, axis=-1, keepdims=True) + EPS)


def _chunk_dims(T):
    c = min(CHUNK, T)
    return c, -(-T // c)


def _to_chunks(a, c, n):
    B, T = a.shape[0], a.shape[1]
    a = jnp.pad(a, [(0, 0), (0, n * c - T)] + [(0, 0)] * (a.ndim - 2))
    a = a.reshape((B, n, c) + a.shape[2:])
    return jnp.moveaxis(jnp.moveaxis(a, 1, 0), 2, 3)


def _from_chunks(o, T):
    n, B, H, c, d = o.shape
    return o.transpose(1, 0, 3, 2, 4).reshape(B, n * c, H, d)[:, :T]


def hgrn2_chunked(q, k, v, logf, s0):
    T = q.shape[1]
    c, n = _chunk_dims(T)
    tri = jnp.tril(jnp.ones((c, c), bool))

    def step(S, inp):
        qc, kc, vc, lc = inp
        G = jnp.cumsum(lc, axis=2)
        diff = G[:, :, :, None, :] - G[:, :, None, :, :]
        dec = jnp.exp(jnp.where(tri[:, :, None], diff, -jnp.inf))
        A = jnp.einsum('bhtd,bhsd,bhtsd->bhts', qc, kc, dec)
        o = (jnp.einsum('bhts,bhsv->bhtv', A, vc)
             + jnp.einsum('bhtd,bhdv->bhtv', qc * jnp.exp(G), S))
        GC = G[:, :, -1:]
        S = (jnp.exp(GC[:, :, 0])[..., None] * S
             + jnp.einsum('bhsd,bhsv->bhdv', kc * jnp.exp(GC - G), vc))
        return S, o

    xs = tuple(_to_chunks(a.astype(F32), c, n) for a in (q, k, v, logf))
    S, o = lax.scan(step, s0.astype(F32), xs)
    return _from_chunks(o, T), S


def gdn_chunked(q, k, v, beta, g, s0):
    T = q.shape[1]
    c, n = _chunk_dims(T)
    tri = jnp.tril(jnp.ones((c, c), bool))
    strict = jnp.tril(jnp.ones((c, c), bool), k=-1)
    eye = jnp.eye(c, dtype=F32)

    def step(S, inp):
        qc, kc, vc, bc, gc = inp
        G = jnp.cumsum(gc, axis=-1)
        L = jnp.exp(jnp.where(tri, G[..., :, None] - G[..., None, :], -jnp.inf))
        kb = kc * bc[..., None]
        M = jnp.where(strict, jnp.einsum('bhtd,bhsd->bhts', kb, kc) * L, 0.0)
        rhs = jnp.concatenate([vc * bc[..., None], kb * jnp.exp(G)[..., None]], axis=-1)
        X = lax.linalg.triangular_solve(eye + M, rhs, left_side=True, lower=True,
                                        unit_diagonal=True)
        u, w = X[..., :GDN_DV], X[..., GDN_DV:]
        v_new = u - jnp.einsum('bhtd,bhdv->bhtv', w, S)
        Aqk = jnp.where(tri, jnp.einsum('bhtd,bhsd->bhts', qc, kc) * L, 0.0)
        o = (jnp.einsum('bhtd,bhdv->bhtv', qc * jnp.exp(G)[..., None], S)
             + jnp.einsum('bhts,bhsv->bhtv', Aqk, v_new))
        GC = G[..., -1:]
        S = (jnp.exp(GC)[..., None] * S
             + jnp.einsum('bhsd,bhsv->bhdv', kc * jnp.exp(GC - G)[..., None], v_new))
        return S, o

    xs = (_to_chunks(q.astype(F32), c, n), _to_chunks(k.astype(F32), c, n),
          _to_chunks(v.astype(F32), c, n), _to_chunks(beta.astype(F32), c, n),
          _to_chunks(g.astype(F32), c, n))
    S, o = lax.scan(step, s0.astype(F32), xs)
    return _from_chunks(o, T), S


def causal_conv(u, buf, w):
    T = u.shape[1]
    up = jnp.concatenate([buf.astype(u.dtype), u], axis=1)
    out = up[:, 0:T] * w[0]
    for j in range(1, CONV_W):
        out = out + up[:, j:j + T] * w[j]
    return out, up[:, -(CONV_W - 1):]


def _split_in(proj):
    sizes = [HG_WIDTH] * 4 + [CONV_CH, GDN_V, GDN_HEADS, GDN_HEADS, MEM_WIDTH, N_BRANCH * D_MODEL]
    points = []
    acc = 0
    for s in sizes[:-1]:
        acc += s
        points.append(acc)
    return jnp.split(proj, points, axis=-1)


def _mem_kv(mem, g_mem, w_mem_kv):
    B, M = mem.shape[0], mem.shape[1]
    kv = rmsnorm(mem, g_mem) @ w_mem_kv
    k, v = jnp.split(kv, 2, axis=-1)
    return k.reshape(B, M, MEM_HEADS, MEM_HD), v.reshape(B, M, MEM_HEADS, MEM_HD)


def _layer(x, mem_k, mem_v, conv_buf, s_hg, s_gdn, lb,
           g_pre_mix, w_in, w_conv, a_log, dt_bias, g_hg_out, g_gdn_out,
           w_br_hg, w_br_gdn, w_br_mem, w_out, g_post_mix, g_pre_ffn,
           w_ffn_in, w_ffn_out, g_post_ffn):
    B, T, _ = x.shape
    dt = x.dtype
    xn = rmsnorm(x, g_pre_mix)
    (hg_q, hg_f, hg_i, hg_gate, gdn_qkv, gdn_z, gdn_a, gdn_b,
     mem_q, gates) = _split_in(xn @ w_in)

    f = lb + (1.0 - lb) * jax.nn.sigmoid(hg_f.astype(F32))
    logf = jnp.log(f)
    hk = 1.0 - f
    o_hg, s_hg_new = hgrn2_chunked(hg_q.reshape(B, T, HG_HEADS, HG_DK),
                                   hk.reshape(B, T, HG_HEADS, HG_DK),
                                   hg_i.reshape(B, T, HG_HEADS, HG_DV),
                                   logf.reshape(B, T, HG_HEADS, HG_DK), s_hg)
    o_hg = (rmsnorm(o_hg, g_hg_out.reshape(HG_HEADS, HG_DV))
            * jax.nn.silu(hg_gate.astype(F32)).reshape(B, T, HG_HEADS, HG_DV))
    o_hg = o_hg.reshape(B, T, HG_WIDTH).astype(dt)

    qkv, conv_new = causal_conv(gdn_qkv, conv_buf, w_conv)
    qkv = jax.nn.silu(qkv)
    gq, gk, gv = jnp.split(qkv, [GDN_QK, 2 * GDN_QK], axis=-1)
    gq = l2norm(gq.reshape(B, T, GDN_HEADS, GDN_DK)) * (GDN_DK ** -0.5)
    gk = l2norm(gk.reshape(B, T, GDN_HEADS, GDN_DK))
    gv = gv.reshape(B, T, GDN_HEADS, GDN_DV)
    beta = jax.nn.sigmoid(gdn_b.astype(F32))
    glog = -jnp.exp(a_log.astype(F32)) * jax.nn.softplus(gdn_a.astype(F32) + dt_bias.astype(F32))
    o_gdn, s_gdn_new = gdn_chunked(gq, gk, gv, beta, glog, s_gdn)
    o_gdn = (rmsnorm(o_gdn, g_gdn_out)
             * jax.nn.silu(gdn_z.astype(F32)).reshape(B, T, GDN_HEADS, GDN_DV))
    o_gdn = o_gdn.reshape(B, T, GDN_V).astype(dt)

    mq = mem_q.reshape(B, T, MEM_HEADS, MEM_HD).astype(F32)
    s = jnp.einsum('bthd,bmhd->bhtm', mq, mem_k.astype(F32)) * (MEM_HD ** -0.5)
    p = jax.nn.softmax(s, axis=-1)
    o_mem = jnp.einsum('bhtm,bmhd->bthd', p, mem_v.astype(F32)).reshape(B, T, MEM_WIDTH).astype(dt)

    g_hg, g_gdn, g_mem = jnp.split(jax.nn.sigmoid(gates), N_BRANCH, axis=-1)
    merged = g_hg * (o_hg @ w_br_hg) + g_gdn * (o_gdn @ w_br_gdn) + g_mem * (o_mem @ w_br_mem)
    h = x + rmsnorm(merged @ w_out, g_post_mix)

    gate, up = jnp.split(rmsnorm(h, g_pre_ffn) @ w_ffn_in, 2, axis=-1)
    ff = (jax.nn.silu(gate) * up) @ w_ffn_out
    y = h + rmsnorm(ff, g_post_ffn)
    return y.astype(dt), conv_new.astype(dt), s_hg_new.astype(dt), s_gdn_new.astype(dt)


def setup_inputs(seed: int = 0) -> dict:
    key = jax.random.key(seed)
    ks = iter(jax.random.split(key, 32))

    def nrm(shape, scale):
        return jax.random.normal(next(ks), shape, F32) * scale

    def gain(shape):
        return 1.0 + nrm(shape, 0.05)

    a_log = jnp.log(jax.random.uniform(next(ks), (DEPTH, GDN_HEADS), F32, 1.0, 16.0))
    dtv = jnp.exp(jax.random.uniform(next(ks), (DEPTH, GDN_HEADS), F32,
                                     math.log(1e-3), math.log(1e-1)))
    dt_bias = dtv + jnp.log(-jnp.expm1(-dtv))
    return {
        "x_prompt": nrm((BATCH, SEQ, D_MODEL), 1.0),
        "x_sample": nrm((DEC_BATCH, DEC_SEQ, D_MODEL), 1.0),
        "mem_prompt": nrm((BATCH, MEM_LEN, D_MODEL), 1.0),
        "cache_mem_k": nrm((DEPTH, DEC_BATCH, MEM_LEN, MEM_HEADS, MEM_HD), 1.0),
        "cache_mem_v": nrm((DEPTH, DEC_BATCH, MEM_LEN, MEM_HEADS, MEM_HD), 1.0),
        "state_hgrn": nrm((DEPTH, DEC_BATCH, HG_HEADS, HG_DK, HG_DV), 0.5),
        "state_gdn": nrm((DEPTH, DEC_BATCH, GDN_HEADS, GDN_DK, GDN_DV), 0.1),
        "state_gdn_conv": nrm((DEPTH, DEC_BATCH, CONV_W - 1, CONV_CH), 1.0),
        "hg_lb_logits": nrm((DEPTH + 1, HG_WIDTH), 0.5),
        "g_pre_mix": gain((DEPTH, D_MODEL)),
        "w_in": nrm((DEPTH, D_MODEL, IN_WIDTH), D_MODEL ** -0.5),
        "w_conv": nrm((DEPTH, CONV_W, CONV_CH), 0.5),
        "a_log": a_log,
        "dt_bias": dt_bias,
        "g_hg_out": gain((DEPTH, HG_WIDTH)),
        "g_gdn_out": gain((DEPTH, GDN_DV)),
        "g_mem": gain((DEPTH, D_MODEL)),
        "w_mem_kv": nrm((DEPTH, D_MODEL, 2 * MEM_WIDTH), D_MODEL ** -0.5),
        "w_br_hg": nrm((DEPTH, HG_WIDTH, D_MODEL), HG_WIDTH ** -0.5),
        "w_br_gdn": nrm((DEPTH, GDN_V, D_MODEL), GDN_V ** -0.5),
        "w_br_mem": nrm((DEPTH, MEM_WIDTH, D_MODEL), MEM_WIDTH ** -0.5),
        "w_out": nrm((DEPTH, D_MODEL, D_MODEL), D_MODEL ** -0.5),
        "g_post_mix": gain((DEPTH, D_MODEL)),
        "g_pre_ffn": gain((DEPTH, D_MODEL)),
        "w_ffn_in": nrm((DEPTH, D_MODEL, 2 * FFN_HIDDEN), D_MODEL ** -0.5),
        "w_ffn_out": nrm((DEPTH, FFN_HIDDEN, D_MODEL), FFN_HIDDEN ** -0.5),
        "g_post_ffn": gain((DEPTH, D_MODEL)),
    }


def reference(x_prompt, x_sample, mem_prompt, cache_mem_k, cache_mem_v, state_hgrn,
              state_gdn, state_gdn_conv, hg_lb_logits, g_pre_mix, w_in, w_conv, a_log,
              dt_bias, g_hg_out, g_gdn_out, g_mem, w_mem_kv, w_br_hg, w_br_gdn, w_br_mem,
              w_out, g_post_mix, g_pre_ffn, w_ffn_in, w_ffn_out, g_post_ffn):
    lb_all = jnp.cumsum(jax.nn.softmax(hg_lb_logits.astype(F32), axis=0), axis=0)
    yp, ys = x_prompt, x_sample
    Bp = x_prompt.shape[0]
    mk_p, mv_p, hg_p, gdn_p, conv_p = [], [], [], [], []
    hg_s, gdn_s, conv_s = [], [], []
    for l in range(DEPTH):
        lw = (g_pre_mix[l], w_in[l], w_conv[l], a_log[l], dt_bias[l], g_hg_out[l],
              g_gdn_out[l], w_br_hg[l], w_br_gdn[l], w_br_mem[l], w_out[l], g_post_mix[l],
              g_pre_ffn[l], w_ffn_in[l], w_ffn_out[l], g_post_ffn[l])
        mk, mv = _mem_kv(mem_prompt, g_mem[l], w_mem_kv[l])
        conv0 = jnp.zeros((Bp, CONV_W - 1, CONV_CH), x_prompt.dtype)
        hg0 = jnp.zeros((Bp, HG_HEADS, HG_DK, HG_DV), F32)
        gdn0 = jnp.zeros((Bp, GDN_HEADS, GDN_DK, GDN_DV), F32)
        yp, cb, sh, sg = _layer(yp, mk, mv, conv0, hg0, gdn0, lb_all[l], *lw)
        mk_p.append(mk.astype(x_prompt.dtype))
        mv_p.append(mv.astype(x_prompt.dtype))
        hg_p.append(sh)
        gdn_p.append(sg)
        conv_p.append(cb)
        ys, cb2, sh2, sg2 = _layer(ys, cache_mem_k[l], cache_mem_v[l], state_gdn_conv[l],
                                   state_hgrn[l], state_gdn[l], lb_all[l], *lw)
        hg_s.append(sh2)
        gdn_s.append(sg2)
        conv_s.append(cb2)
    return (yp, ys, jnp.stack(mk_p), jnp.stack(mv_p), jnp.stack(hg_p), jnp.stack(gdn_p),
            jnp.stack(conv_p), jnp.stack(hg_s), jnp.stack(gdn_s), jnp.stack(conv_s))
```

```python
import os
import numpy as np
import ml_dtypes
import concourse.bass as bass
import concourse.mybir as mybir
from concourse.ap import AP
from concourse.bass_utils import run_bass_kernel_spmd

F32 = mybir.dt.float32
BF16 = mybir.dt.bfloat16
AF = mybir.ActivationFunctionType
ALU = mybir.AluOpType

NCORES = 8
D = 1024
SEQ = 2048
NS = 16
TS = 4
NTOK = SEQ + NS * TS
MEM = 256
FFH = 2816
INW = 7688
EPS = 1e-6
SB_BASE = 16512
SB_TOP = 229344


class Op:
    __slots__ = ("eng", "idx", "build", "deps", "dma", "sem", "semval", "mark")


class Prog:
    ENGS = ("pe", "act", "dve", "pool", "sp")

    def __init__(self, nc):
        self.nc = nc
        self.streams = {e: [] for e in self.ENGS}
        self.last_w = {}
        self.readers = {}
        self.alias = {}
        self.keys_of = {}

    @staticmethod
    def _keys(aps):
        out = []
        for a in aps:
            if a is None or isinstance(a, (int, float)):
                continue
            if isinstance(a, (str, tuple)):
                out.append(a)
            else:
                out.append(getattr(a, "tensor", a).name)
        return out

    def emit(self, eng, build, reads, writes, dma=False):
        op = Op()
        op.eng = eng
        op.build = build
        op.dma = dma
        op.mark = False
        op.sem = None
        op.semval = 0
        st = self.streams[eng]
        op.idx = len(st)
        st.append(op)
        rk = self._keys(reads)
        wk = self._keys(writes)
        deps = {}

        def add(d, kind):
            if d is None or d is op:
                return
            if (not d.dma) and d.eng == eng and not dma:
                if eng == "pe":
                    return
            deps[id(d)] = d

        for k in rk + wk:
            base = k if isinstance(k, str) else k[0]
            self.keys_of.setdefault(base, set()).add(k)
        for k in rk:
            add(self.last_w.get(k), "raw")
            base = k if isinstance(k, str) else k[0]
            if isinstance(base, str) and base.startswith("psb"):
                for r in self.readers.get(k, ()):
                    if r.eng != eng:
                        add(r, "rar")
        for k in wk:
            add(self.last_w.get(k), "waw")
            for r in self.readers.get(k, ()):
                add(r, "war")
            base = k if isinstance(k, str) else k[0]
            for o in self.alias.get(base, ()):
                for kk in self.keys_of.get(o, ()):
                    add(self.last_w.get(kk), "waw")
                    for r in self.readers.get(kk, ()):
                        add(r, "war")
        op.deps = list(deps.values())
        for k in rk:
            lst = self.readers.setdefault(k, [])
            if not dma:
                lst[:] = [r for r in lst if r.dma or r.eng != eng]
            lst.append(op)
        for k in wk:
            self.last_w[k] = op
            self.readers[k] = []
        return op

    def lower(self):
        nc = self.nc
        eng_sem = {e: nc.alloc_semaphore("cs_" + e) for e in self.ENGS}
        ndma = {"sp": 14, "pool": 14, "act": 4, "pe": 1, "dve": 1}
        dma_sems = {e: [nc.alloc_semaphore("ds_%s%d" % (e, i)) for i in range(ndma[e])] for e in ("sp", "pool", "act")}
        for e in self.ENGS:
            for op in self.streams[e]:
                for d in op.deps:
                    if not d.dma:
                        d.mark = True
        finals = {}
        for e in self.ENGS:
            cnt = 0
            m = 0
            for op in self.streams[e]:
                if op.dma:
                    sems = dma_sems[e]
                    op.sem = sems[m % len(sems)]
                    op.semval = 16 * (m // len(sems) + 1)
                    finals[op.sem] = op.semval
                    m += 1
                else:
                    if op.mark:
                        cnt += 1
                    op.sem = eng_sem[e]
                    op.semval = cnt
        streams = self.streams

        def run(name, h):
            waited = {}
            for op in streams[name]:
                waits = {}
                for d in op.deps:
                    if waits.get(d.sem, 0) < d.semval:
                        waits[d.sem] = d.semval
                if op.dma and op.semval > 16:
                    if waits.get(op.sem, 0) < op.semval - 16:
                        waits[op.sem] = op.semval - 16
                for sem, val in waits.items():
                    if waited.get(sem, 0) >= val:
                        continue
                    h.wait_ge(sem, val)
                    waited[sem] = val
                ins = op.build(h)
                if op.dma:
                    ins.then_inc(op.sem, 16)
                elif op.mark:
                    ins.then_inc(op.sem, 1)
            if name == "sp":
                for sem, val in finals.items():
                    if waited.get(sem, 0) < val:
                        h.wait_ge(sem, val)

        with nc.Block() as block:
            @block.tensor
            def _(h):
                run("pe", h)

            @block.scalar
            def _(h):
                run("act", h)

            @block.vector
            def _(h):
                run("dve", h)

            @block.gpsimd
            def _(h):
                run("pool", h)

            @block.sync
            def _(h):
                run("sp", h)

    def dma(self, eng, out, in_, rk=None, wk=None, nc_ok=False):
        nc = self.nc

        def b(h):
            if nc_ok:
                with nc.allow_non_contiguous_dma(reason="small strided constant load"):
                    return h.dma_start(out=out, in_=in_)
            return h.dma_start(out=out, in_=in_)
        return self.emit(eng, b, rk if rk is not None else [in_], wk if wk is not None else [out], dma=True)

    def mm(self, out, lhsT, rhs, start=True, stop=True, rk=None, wk=None):
        return self.emit("pe", lambda h: h.matmul(out, lhsT=lhsT, rhs=rhs, start=start, stop=stop),
                         rk if rk is not None else [lhsT, rhs], wk if wk is not None else [out])

    def tr(self, out, in_, ident, rk=None, wk=None):
        return self.emit("pe", lambda h: h.transpose(out=out, in_=in_, identity=ident),
                         rk if rk is not None else [in_, ident], wk if wk is not None else [out])

    def act(self, out, in_, func, scale=None, bias=None, accum_out=None, rk=None, wk=None):
        kw = {}
        if scale is not None:
            kw["scale"] = scale
        if bias is not None:
            kw["bias"] = bias
        if accum_out is not None:
            kw["accum_out"] = accum_out
        rd = [in_]
        if isinstance(scale, AP):
            rd.append(scale)
        if isinstance(bias, AP):
            rd.append(bias)
        return self.emit("act", lambda h: h.activation(out=out, in_=in_, func=func, **kw),
                         rk if rk is not None else rd, wk if wk is not None else [out, accum_out])

    def tt(self, eng, out, in0, in1, op, rk=None, wk=None):
        return self.emit(eng, lambda h: h.tensor_tensor(out=out, in0=in0, in1=in1, op=op),
                         rk if rk is not None else [in0, in1], wk if wk is not None else [out])

    def ts(self, eng, out, in0, s1, s2, op0, op1=None, rk=None, wk=None):
        rd = [in0]
        if isinstance(s1, AP):
            rd.append(s1)
        if isinstance(s2, AP):
            rd.append(s2)

        def b(h):
            if op1 is None:
                return h.tensor_scalar(out=out, in0=in0, scalar1=s1, scalar2=None, op0=op0)
            return h.tensor_scalar(out=out, in0=in0, scalar1=s1, scalar2=s2, op0=op0, op1=op1)
        return self.emit(eng, b, rk if rk is not None else rd, wk if wk is not None else [out])

    def stt(self, eng, out, in0, scalar, in1, op0, op1, rk=None, wk=None):
        rd = [in0, in1]
        if isinstance(scalar, AP):
            rd.append(scalar)
        return self.emit(eng, lambda h: h.scalar_tensor_tensor(out=out, in0=in0, scalar=scalar, in1=in1, op0=op0, op1=op1),
                         rk if rk is not None else rd, wk if wk is not None else [out])

    def copy(self, eng, out, in_, rk=None, wk=None):
        if eng == "act":
            return self.act(out, in_, AF.Copy, rk=rk, wk=wk)
        return self.emit(eng, lambda h: h.tensor_copy(out=out, in_=in_),
                         rk if rk is not None else [in_], wk if wk is not None else [out])

    def memset(self, eng, out, val, wk=None):
        return self.emit(eng, lambda h: h.memset(out, val), [], wk if wk is not None else [out])

    def recip(self, out, in_, rk=None, wk=None):
        return self.emit("dve", lambda h: h.reciprocal(out=out, in_=in_),
                         rk if rk is not None else [in_], wk if wk is not None else [out])

    def scan(self, out, d0, d1, init, op0, op1, rk=None, wk=None):
        return self.emit("dve", lambda h: h.tensor_tensor_scan(out=out, data0=d0, data1=d1, initial=init, op0=op0, op1=op1),
                         rk if rk is not None else [d0, d1], wk if wk is not None else [out])


class Arena:
    def __init__(self, nc, prog):
        self.nc = nc
        self.prog = prog
        self.off = SB_BASE
        self.n = 0
        self.peak = SB_BASE
        self.ranges = []

    def _register(self, t, off, per):
        al = self.prog.alias
        for (nm, o, e) in self.ranges:
            if o < off + per and off < e:
                al.setdefault(t.name, set()).add(nm)
                al.setdefault(nm, set()).add(t.name)
        self.ranges.append((t.name, off, off + per))

    def alloc(self, name, shape, dtype):
        esz = 2 if dtype == BF16 else 4
        per = esz
        for s in shape[1:]:
            per *= s
        off = (self.off + 31) // 32 * 32
        assert off + per <= SB_TOP, "SBUF overflow allocating %s (%d bytes at %d)" % (name, per, off)
        self.n += 1
        t = self.nc.alloc_sbuf_tensor_at("%s_%d" % (name, self.n), list(shape), dtype, offset=off)
        self._register(t, off, per)
        self.off = off + per
        self.peak = max(self.peak, self.off)
        return t

    def alloc_at(self, name, shape, dtype, off):
        esz = 2 if dtype == BF16 else 4
        per = esz
        for x in shape[1:]:
            per *= x
        assert off % 32 == 0 and off >= SB_BASE and off + per <= SB_TOP, "bad alloc_at %s" % name
        self.n += 1
        t = self.nc.alloc_sbuf_tensor_at("%s_%d" % (name, self.n), list(shape), dtype, offset=off)
        self._register(t, off, per)
        return t

    def mark(self):
        return self.off

    def release(self, m):
        self.off = m


def interleave(gens, width, stagger=0):
    gens = list(gens)
    live = []
    nxt = 0
    while live or nxt < len(gens):
        while len(live) < width and nxt < len(gens) and (not live or live[-1][1] >= stagger):
            live.append([gens[nxt], 0])
            nxt += 1
        for e in list(live):
            try:
                next(e[0])
                e[1] += 1
            except StopIteration:
                live.remove(e)


def bc_free(t_ap, n):
    ap = [list(x) for x in t_ap.ap]
    if len(ap) >= 2 and ap[-1][1] == 1:
        ap = ap[:-1]
    return AP(t_ap.tensor, t_ap.offset, ap + [[0, n]])


def make_consts():
    c = {}
    s = np.arange(128)[:, None]
    t = np.arange(128)[None, :]
    c["maskH_p"] = ((s <= t) & (s // 64 == t // 64)).astype(np.float32)
    s6 = np.arange(64)[:, None]
    t6 = np.arange(64)[None, :]
    m_s = ((s6 <= t6) & (s6 // 4 == t6 // 4)).astype(np.float32)
    c["maskB_s"] = np.zeros((128, 64), np.float32)
    c["maskB_s"][:64] = m_s
    c["maskG_p"] = (s <= t).astype(np.float32)
    c["maskGs_p"] = (s < t).astype(np.float32)
    ms = ((s6 < t6) & (s6 // 4 == t6 // 4)).astype(np.float32)
    c["maskBs_s"] = np.zeros((128, 64), np.float32)
    c["maskBs_s"][:64] = ms
    c["nmaskGs_p"] = -c["maskGs_p"]
    c["nmaskBs_s"] = -c["maskBs_s"]
    c["negG_p"] = np.where(s <= t, 0.0, -30000.0).astype(np.float32)
    ng = np.where((s6 <= t6) & (s6 // 4 == t6 // 4), 0.0, -30000.0).astype(np.float32)
    c["negB_s"] = np.zeros((128, 64), np.float32)
    c["negB_s"][:64] = ng
    tt = np.arange(512)[None, :]
    c["cmneg"] = np.zeros((128, 16 * 64), np.float32)
    cmn = -(np.arange(64)[None, :] // 4 == np.arange(16)[:, None]).astype(np.float32)
    c["cmneg"][:] = cmn.reshape(1, 1024)
    c["scan64"] = np.broadcast_to((tt % 64 != 0).astype(np.float32), (128, 512)).copy()
    c["scan128"] = np.broadcast_to((tt % 128 != 0).astype(np.float32), (128, 512)).copy()
    c["scan4"] = np.broadcast_to((np.arange(64)[None, :] % 4 != 0).astype(np.float32), (128, 64)).copy()
    bm = (np.arange(128)[:, None] // 4 == np.arange(16)[None, :]).astype(np.float32)
    c["bm"] = bm
    sel = np.zeros((128, 4 * 128), np.float32)
    for h in range(4):
        sel[h, h * 128:(h + 1) * 128] = 1.0
    c["sel4"] = sel
    c["ident"] = np.eye(128, dtype=np.float32)
    c["ones"] = np.ones((128, 128), np.float32)
    names = list(c.keys())
    offs = {}
    o = 0
    for k in names:
        offs[k] = (o, c[k].shape[1])
        o += c[k].shape[1]
    arr = np.concatenate([c[k] for k in names], axis=1).astype(np.float32)
    return arr, offs


CONST_ARR, CONST_OFFS = make_consts()
NCONST = CONST_ARR.shape[1]


def make_level_masks():
    t = np.arange(128)[:, None]
    u = np.arange(128)[None, :]
    ms = []
    for k in range(7):
        m = ((t >> (k + 1)) == (u >> (k + 1))) & ((t >> k) != (u >> k)) & (t > u)
        ms.append(m.astype(np.float32) + np.eye(128, dtype=np.float32))
    ms.append(ms[0].T.copy())
    return np.concatenate(ms, axis=1).astype(ml_dtypes.bfloat16)


LMASK_ARR = make_level_masks()


def build_program(debug=False):
    nc = bass.Bass("TRN2", target_bir_lowering=False)
    stop = os.environ.get("KSTOP", "") if debug else ""
    P = Prog(nc)
    A = Arena(nc, P)

    def din(name, shape, dt=F32):
        return nc.dram_tensor(name, list(shape), dt, kind="ExternalInput").ap()

    def dout(name, shape, dt=F32):
        return nc.dram_tensor(name, list(shape), dt, kind="ExternalOutput").ap()

    x_prompt = din("x_prompt", [SEQ, D])
    x_sample = din("x_sample", [NS * TS, D])
    mem_prompt = din("mem_prompt", [MEM, D])
    cache_k = din("cache_mem_k", [NS, MEM, 512])
    cache_v = din("cache_mem_v", [NS, MEM, 512])
    st_hg_in = din("state_hgrn", [NS, 4, 128, 128])
    st_gdn_in = din("state_gdn", [NS, 4, 128, 128])
    st_conv_in = din("state_gdn_conv", [NS, 3, 1536])
    lb_logits = din("hg_lb_logits", [2, 512])
    g_pre_mix = din("g_pre_mix", [D])
    w_in = din("w_in", [D, INW])
    w_conv = din("w_conv", [4, 1536])
    a_log = din("a_log", [4])
    dt_bias = din("dt_bias", [4])
    g_hg_out = din("g_hg_out", [512])
    g_gdn_out = din("g_gdn_out", [128])
    g_mem = din("g_mem", [D])
    w_mem_kv = din("w_mem_kv", [D, 1024])
    w_br_hg = din("w_br_hg", [512, D])
    w_br_gdn = din("w_br_gdn", [512, D])
    w_br_mem = din("w_br_mem", [512, D])
    w_out = din("w_out", [D, D])
    g_post_mix = din("g_post_mix", [D])
    g_pre_ffn = din("g_pre_ffn", [D])
    w_ffn_in = din("w_ffn_in", [D, 2 * FFH])
    w_ffn_out = din("w_ffn_out", [FFH, D])
    g_post_ffn = din("g_post_ffn", [D])
    consts_d = din("consts", [128, NCONST])
    lmask_d = din("lmasks", [128, 1024], BF16)

    y_prompt = dout("y_prompt", [SEQ, D])
    y_sample = dout("y_sample", [NS * TS, D])
    o_mk = dout("new_mem_k", [MEM, 512])
    o_mv = dout("new_mem_v", [MEM, 512])
    o_hg_p = dout("new_hg_p", [4, 128, 128])
    o_gdn_p = dout("new_gdn_p", [4, 128, 128])
    o_conv_p = dout("new_conv_p", [3, 1536])
    o_hg_s = dout("new_hg_s", [NS, 4, 128, 128])
    o_gdn_s = dout("new_gdn_s", [NS, 4, 128, 128])
    o_conv_s = dout("new_conv_s", [NS, 3, 1536])
    dbg = {}
    if debug:
        dbg["ohgT"] = dout("dbg_ohgT", [128, 4, NTOK])
        dbg["xnT"] = dout("dbg_xnT", [128, 8, NTOK])
        dbg["ogdnT"] = dout("dbg_ogdnT", [128, 4, NTOK])
        dbg["omemT"] = dout("dbg_omemT", [128, 4, NTOK])
        dbg["mrgT"] = dout("dbg_mrgT", [128, 8, NTOK])
        dbg["h"] = dout("dbg_h", [NTOK, D])

    PS = [nc.alloc_psum_tensor("psb%d" % i, [128, 512], F32) for i in range(8)]
    PSB = [p.bitcast(BF16) for p in PS]

    cst = A.alloc("cst", [128, NCONST], F32)
    P.dma("sp", cst[:, :], consts_d)

    def C(name, rows=128):
        o, w = CONST_OFFS[name]
        return cst[0:rows, o:o + w]

    lmk = A.alloc("lmk", [128, 8, 128], BF16)
    P.dma("sp", lmk[:, :, :].rearrange("p k t -> p (k t)"), lmask_d)
    ident_bf = A.alloc("identbf", [128, 128], BF16)
    ones_bf = A.alloc("onesbf", [128, 128], BF16)
    P.copy("dve", ident_bf[:, :], C("ident"))
    P.copy("dve", ones_bf[:, :], C("ones"))

    vec = A.alloc("vec", [128, 160], F32)
    vrow = A.alloc("vrow", [128, 128], F32)
    ident32 = C("ident")
    P.memset("dve", vrow[:, :], 0.0)
    VC = {}
    vcol = [0]
    vparts = []

    def load_cols(name, src, n):
        c0 = vcol[0]
        vcol[0] += n
        P.dma("sp", vrow[c0:c0 + n, 0:src.shape[-1]], src, rk=[vrow], wk=[("vrowpart", name)])
        VC[name] = c0
        vparts.append(("vrowpart", name))
        return c0

    load_cols("g_pre_mix", g_pre_mix.rearrange("(k p) -> k p", p=128), 8)
    load_cols("g_mem", g_mem.rearrange("(k p) -> k p", p=128), 8)
    load_cols("g_hg", g_hg_out.rearrange("(k p) -> k p", p=128), 4)
    load_cols("g_gdn", g_gdn_out.rearrange("(k p) -> k p", p=128), 1)
    load_cols("l0", lb_logits[0].rearrange("(k p) -> k p", p=128), 4)
    load_cols("l1", lb_logits[1].rearrange("(k p) -> k p", p=128), 4)
    load_cols("g_post_mix", g_post_mix.rearrange("(k p) -> k p", p=128), 8)
    load_cols("g_pre_ffn", g_pre_ffn.rearrange("(k p) -> k p", p=128), 8)
    load_cols("g_post_ffn", g_post_ffn.rearrange("(k p) -> k p", p=128), 8)
    load_cols("w_conv", w_conv.rearrange("j (k p) -> (j k) p", p=128), 48)
    load_cols("a_log", a_log.rearrange("(o h) -> o h", o=1), 1)
    load_cols("dt_bias", dt_bias.rearrange("(o h) -> o h", o=1), 1)
    assert vcol[0] <= 128
    vcol[0] = 128
    P.tr(PS[7][:, 0:128], vrow[:, :], ident32, rk=[vrow, cst] + vparts)
    P.copy("dve", vec[:, 0:128], PS[7][:, 0:128])

    def vc(name, k=0, n=1):
        c0 = VC[name] + k
        return vec[:, c0:c0 + n]

    VC["lb"] = vcol[0]; vcol[0] += 4
    VC["oml"] = vcol[0]; vcol[0] += 4
    VC["tmp4"] = vcol[0]; vcol[0] += 4
    VC["halfoml"] = vcol[0]; vcol[0] += 4
    VC["fbias"] = vcol[0]; vcol[0] += 4
    VC["noml"] = vcol[0]; vcol[0] += 4
    P.tt("dve", vc("tmp4", 0, 4), vc("l1", 0, 4), vc("l0", 0, 4), ALU.subtract)
    P.act(vc("tmp4", 0, 4), vc("tmp4", 0, 4), AF.Exp)
    P.ts("dve", vc("tmp4", 0, 4), vc("tmp4", 0, 4), 1.0, None, ALU.add)
    P.recip(vc("lb", 0, 4), vc("tmp4", 0, 4))
    P.ts("dve", vc("oml", 0, 4), vc("lb", 0, 4), -1.0, 1.0, ALU.mult, ALU.add)
    P.ts("dve", vc("halfoml", 0, 4), vc("oml", 0, 4), 0.5, None, ALU.mult)
    P.tt("dve", vc("fbias", 0, 4), vc("lb", 0, 4), vc("halfoml", 0, 4), ALU.add)
    P.ts("dve", vc("noml", 0, 4), vc("halfoml", 0, 4), -1.0, None, ALU.mult)

    OFF_XNT = 36864
    OFF_OHG = OFF_XNT + 33792
    OFF_OGDN = OFF_OHG + 16896
    OFF_OMEM = OFF_OGDN + 16896
    OFF_MRG = OFF_OMEM + 16896
    OFF_MRG_END = OFF_MRG + 33792
    assert A.off <= OFF_XNT, A.off
    xnT = A.alloc_at("xnT", [128, 8, NTOK], BF16, OFF_XNT)
    A.off = OFF_OHG

    def norm_transpose(src_rows_fn, ntiles, rows_fn, gname, dstT, tag):
        m0 = A.mark()
        xb = [A.alloc("xt%d" % i, [128, D], F32) for i in range(3)]
        xs = [A.alloc("xs%d" % i, [128, D], BF16) for i in range(3)]
        junk = [A.alloc("junk%d" % i, [128, D], BF16) for i in range(3)]
        stat = A.alloc("stat", [128, 3 * 32], F32)
        P.memset("dve", stat[:, :], 0.0)
        def chain(ti):
            rows = rows_fn(ti)
            xt = xb[ti % 3]
            P.dma("sp", xt[0:rows, :], src_rows_fn(ti))
            P.act(junk[ti % 3][0:rows, :], xt[0:rows, :], AF.Square, accum_out=stat[0:rows, ti:ti + 1])
            yield
            P.act(stat[0:rows, 32 + ti:33 + ti], stat[0:rows, ti:ti + 1], AF.Ln, scale=1.0 / D, bias=EPS)
            P.act(stat[0:rows, 64 + ti:65 + ti], stat[0:rows, 32 + ti:33 + ti], AF.Exp, scale=-0.5)
            xsb = xs[ti % 3]
            P.act(xsb[0:rows, :], xt[0:rows, :], AF.Copy, scale=stat[0:rows, 64 + ti:65 + ti])
            yield
            pb = PSB[ti % 3]
            for k in range(8):
                P.tr(pb[:, k * 128:k * 128 + rows], xsb[0:rows, k * 128:(k + 1) * 128], ident_bf[0:rows, 0:rows])
            tok0 = 128 * ti
            src = pb[:, 0:1024].rearrange("p (k t) -> p k t", k=8)[:, :, 0:rows]
            P.tt("dve", dstT[:, :, tok0:tok0 + rows], src, bc_free(vc(gname, 0, 8), rows), ALU.mult)
            yield

        interleave([chain(ti) for ti in range(ntiles)], 3, 1)
        A.release(m0)

    def x_rows(ti):
        if ti < 16:
            return x_prompt[128 * ti:128 * ti + 128, :]
        return x_sample[:, :]

    norm_transpose(x_rows, 17, lambda ti: 128 if ti < 16 else 64, "g_pre_mix", xnT, "x")

    if debug:
        m0 = A.mark()
        dtmp = A.alloc("dbgtmp", [128, 8, NTOK], F32)
        P.copy("dve", dtmp[:, :, :], xnT[:, :, :])
        P.dma("sp", dbg["xnT"], dtmp[:, :, :])
        A.release(m0)


    w_in_v = w_in.rearrange("(k p) c -> p k c", p=128)
    ogdnT = A.alloc_at("ogdnT", [128, 4, NTOK], BF16, OFF_OHG)
    ohgT = A.alloc_at("ohgT", [128, 4, NTOK], BF16, OFF_OGDN)
    omemT = A.alloc_at("omemT", [128, 4, NTOK], BF16, OFF_OMEM)

    def KY(t, *idx):
        return (t.name,) + tuple(idx)

    STS = [(512 * i, 512, 4, 128, False) for i in range(4)] + [(SEQ, 64, 1, 64, True)]

    def phase_hgrn():
        A.off = OFF_OMEM
        m0 = A.mark()
        whg = A.alloc("whg", [128, 8, 2048], BF16)
        for k in range(8):
            P.dma("pool", whg[:, k, :], w_in_v[:, k, 0:2048], wk=[KY(whg, k)])
        wkeys = [KY(whg, k) for k in range(8)]
        WH = 256
        v_sb_2 = [A.alloc("hv%d" % i, [128, 2, 512], BF16) for i in range(2)]
        thf = A.alloc("hthf", [128, 4, WH], F32)
        logf = thf
        G = A.alloc("hG", [128, 4, WH], F32)
        eG_2 = [A.alloc("heG%d" % i, [128, 4, WH], F32) for i in range(2)]
        eNG = thf
        kk = A.alloc("hkk", [128, 4, WH], F32)
        qtT_2 = [A.alloc("hqtT%d" % i, [128, 4, WH], BF16) for i in range(2)]
        ktT_2 = [A.alloc("hktT%d" % i, [128, 4, WH], BF16) for i in range(2)]
        thg_2 = [A.alloc("hthg%d" % i, [128, 4, WH], F32) for i in range(2)]
        ATm_2 = [A.alloc("hATm%d" % i, [128, 4, 128], BF16) for i in range(2)]
        ktok_2 = [A.alloc("hktok%d" % i, [128, 4, 128], BF16) for i in range(2)]
        S32 = A.alloc("hS32", [128, 4, 128], F32)
        Sbf = A.alloc("hSbf", [128, 4, 128], BF16)
        tmpS = A.alloc("htmpS", [128, 4, 128], F32)
        sq = A.alloc("hsq", [128, 512], BF16)
        lnv = A.alloc("hlnv", [128, 512], F32)
        rstd = A.alloc("hrstd", [128, 512], F32)
        t1 = A.alloc("ht1", [128, 512], F32)
        ghalf = A.alloc("hghalf", [128, 4], F32)
        P.ts("dve", ghalf[:, :], vc("g_hg", 0, 4), 0.5, None, ALU.mult)
        P.memset("dve", S32[:, :, :], 0.0)
        P.memset("dve", Sbf[:, :, :], 0.0)
        pj = [0]

        def proj_fm(col0, tok0, W):
            ps = PS[pj[0] % 2]
            pj[0] += 1
            for k in range(8):
                P.mm(ps[:, 0:W], whg[:, k, col0:col0 + 128], xnT[:, k, tok0:tok0 + W], start=(k == 0), stop=(k == 7),
                     rk=[wkeys[k], xnT])
            return ps

        HSTS = [(WH * i, WH, 2, 128, False) for i in range(8)] + [(SEQ, 64, 1, 64, True)]

        def stageA(sti):
            (tok0, W, ntile, TT, is_s) = HSTS[sti]
            (v_sb, eG, qtT, ktT, thg) = [x[sti % 2] for x in (v_sb_2, eG_2, qtT_2, ktT_2, thg_2)]
            sgT = thg
            for ti in range(ntile):
                t0 = tok0 + ti * TT
                psv = PS[pj[0] % 2]
                pj[0] += 1
                for k in range(8):
                    P.mm(psv[0:TT, :], xnT[:, k, t0:t0 + TT], whg[:, k, 1024:1536], start=(k == 0), stop=(k == 7),
                         rk=[wkeys[k], xnT])
                P.copy("act", v_sb[0:TT, ti, :], psv[0:TT, :])
                yield
            for h in range(4):
                ps = proj_fm(512 + h * 128, tok0, W)
                P.act(thf[:, h, 0:W], ps[:, 0:W], AF.Tanh, scale=0.5, wk=[KY(thf, h)])
            for h in range(4):
                ps = proj_fm(1536 + h * 128, tok0, W)
                P.act(thg[:, h, 0:W], ps[:, 0:W], AF.Tanh, scale=0.5)
                P.stt("dve", thg[:, h, 0:W], thg[:, h, 0:W], 1.0, ps[:, 0:W], ALU.add, ALU.mult)
                P.ts("dve", sgT[:, h, 0:W], thg[:, h, 0:W], ghalf[:, h:h + 1], None, ALU.mult)
            yield
            scanm = C("scan4")[:, 0:W] if is_s else C("scan64")[:, 0:W]
            for h in range(4):
                P.ts("dve", kk[:, h, 0:W], thf[:, h, 0:W], vc("noml", h), vc("halfoml", h), ALU.mult, ALU.add, rk=[KY(thf, h), vec], wk=[KY(kk, h)])
            yield
            for h in range(4):
                P.act(logf[:, h, 0:W], thf[:, h, 0:W], AF.Ln, scale=vc("halfoml", h), bias=vc("fbias", h), rk=[KY(thf, h), KY(kk, h), vec], wk=[KY(thf, h)])
            yield
            for h in range(4):
                P.scan(G[:, h, 0:W], scanm, logf[:, h, 0:W], 0.0, ALU.mult, ALU.add, rk=[KY(thf, h), cst], wk=[KY(G, h)])
            yield
            for h in range(4):
                P.act(eG[:, h, 0:W], G[:, h, 0:W], AF.Exp, rk=[KY(G, h)], wk=[KY(eG, h), eG])
                P.act(eNG[:, h, 0:W], G[:, h, 0:W], AF.Exp, scale=-1.0, rk=[KY(G, h)], wk=[KY(thf, h)])
            yield
            for h in range(4):
                ps = proj_fm(h * 128, tok0, W)
                P.tt("dve", qtT[:, h, 0:W], ps[:, 0:W], eG[:, h, 0:W], ALU.mult, rk=[ps, KY(eG, h)], wk=[KY(qtT, h), qtT])
                P.tt("pool", ktT[:, h, 0:W], kk[:, h, 0:W], eNG[:, h, 0:W], ALU.mult, rk=[KY(kk, h), KY(thf, h)], wk=[KY(ktT, h), ktT])
                yield

        def stageB1(sti, ti):
            (tok0, W, ntile, TT, is_s) = HSTS[sti]
            (v_sb, eG, qtT, ktT, thg) = [x[sti % 2] for x in (v_sb_2, eG_2, qtT_2, ktT_2, thg_2)]
            sgT = thg
            if True:
                c0 = ti * TT
                t0 = tok0 + c0
                gt = 2 * sti + ti
                oT = PS[6 + (gt % 2)]
                ATm = ATm_2[gt % 2]
                ktok = ktok_2[gt % 2]
                mask = C("maskB_s", 64) if is_s else C("maskH_p")
                for h in range(4):
                    P.mm(PS[3][0:TT, h * 128:h * 128 + TT], ktT[:, h, c0:c0 + TT], qtT[:, h, c0:c0 + TT])
                maskb = AP(mask.tensor, mask.offset, [list(mask.ap[0]), [0, 4], list(mask.ap[1])])
                P.tt("dve", ATm[0:TT, :, 0:TT], PS[3][0:TT, :].rearrange("p (h t) -> p h t", h=4)[:, :, 0:TT], maskb, ALU.mult)
                yield
                for h in range(4):
                    P.tr(PSB[4][0:TT, h * 128:(h + 1) * 128], ktT[:, h, c0:c0 + TT], ident_bf[:, :])
                P.copy("act", ktok[0:TT, :, :], PSB[4][0:TT, 0:512].rearrange("p (h d) -> p h d", h=4))
                yield
                for h in range(4):
                    P.mm(oT[:, h * TT:(h + 1) * TT], v_sb[0:TT, ti, h * 128:(h + 1) * 128], ATm[0:TT, h, 0:TT], start=(h == 0), stop=False)
                yield

        def stageB2(sti, ti):
            (tok0, W, ntile, TT, is_s) = HSTS[sti]
            (v_sb, eG, qtT, ktT, thg) = [x[sti % 2] for x in (v_sb_2, eG_2, qtT_2, ktT_2, thg_2)]
            sgT = thg
            if True:
                c0 = ti * TT
                t0 = tok0 + c0
                gt = 2 * sti + ti
                oT = PS[6 + (gt % 2)]
                ATm = ATm_2[gt % 2]
                ktok = ktok_2[gt % 2]
                if not is_s:
                    for b in range(2):
                        r0 = 64 * b
                        for h in range(4):
                            P.mm(oT[:, h * TT + r0:h * TT + r0 + 64], Sbf[:, h, :], qtT[:, h, c0 + r0:c0 + r0 + 64], start=False, stop=(b == 1 and h == 3))
                        for h in range(4):
                            P.mm(PS[5][:, h * 128:(h + 1) * 128], ktok[r0:r0 + 64, h, :], v_sb[r0:r0 + 64, ti, h * 128:(h + 1) * 128])
                        egcv = AP(eG, c0 + r0 + 63, [[4 * WH, 128], [WH, 4], [0, 128]])
                        P.tt("dve", S32[:, :, :], S32[:, :, :], PS[5][:, :].rearrange("p (h v) -> p h v", h=4), ALU.add)
                        P.tt("dve", S32[:, :, :], S32[:, :, :], egcv, ALU.mult)
                        P.copy("act", Sbf[:, :, :], S32[:, :, :])
                        yield
                else:
                    mS = A.mark()
                    S0_2 = [A.alloc("hS0%d" % q_, [128, 4, 4, 128], F32) for q_ in range(2)]
                    S0bf_2 = [A.alloc("hS0bf%d" % q_, [128, 4, 4, 128], BF16) for q_ in range(2)]
                    vm_2 = [A.alloc("hvm%d" % q_, [128, 2, 4, 128], BF16) for q_ in range(2)]
                    for grp in range(4):
                        i0 = 4 * grp
                        S0 = S0_2[grp % 2]
                        S0bf = S0bf_2[grp % 2]
                        vm = vm_2[grp % 2]
                        P.dma("sp", S0[:, :, :, :], st_hg_in[i0:i0 + 4].rearrange("i h d v -> d i h v"))
                        P.copy("act", S0bf[:, :, :, :], S0[:, :, :, :])
                        for h in range(4):
                            for ii in range(4):
                                i = i0 + ii
                                P.mm(oT[:, h * TT + 4 * i:h * TT + 4 * i + 4], S0bf[:, ii, h, :], qtT[:, h, 4 * i:4 * i + 4], start=False,
                                     stop=(grp == 3 and h == 3 and ii == 3))
                        egc = AP(eG, 4 * i0 + 3, [[4 * WH, 128], [4, 4], [WH, 4], [0, 128]])
                        P.tt("dve", S0[:, :, :, :], S0[:, :, :, :], egc, ALU.mult, rk=[S0, S0bf, eG])
                        for h in range(4):
                            vin = AP(v_sb, h * 128, [[2 * 512, 64], [0, 4], [1, 128]])
                            bmv = bc_free(C("bm", 64)[:, i0:i0 + 4], 128)
                            vmh = vm[0:64, h % 2, :, :]
                            P.tt("dve", vmh, vin, bmv, ALU.mult, rk=[v_sb, cst], wk=[KY(vm, h % 2)])
                            for ii in range(4):
                                P.mm(PS[5][:, ii * 128:(ii + 1) * 128], ktok[0:64, h, :], vm[0:64, h % 2, ii, :], rk=[ktok, KY(vm, h % 2)])
                            egc2 = AP(eG, h * WH + 4 * i0 + 3, [[4 * WH, 128], [4, 4], [0, 128]])
                            P.tt("dve", tmpS[:, :, :], PS[5][:, :].rearrange("p (i v) -> p i v", i=4), egc2, ALU.mult)
                            P.tt("dve", S0[:, :, h, :], tmpS[:, :, :], S0[:, :, h, :], ALU.add)
                        P.dma("sp", o_hg_s[i0:i0 + 4].rearrange("i h d v -> d i h v"), S0[:, :, :, :])
                        yield
                    A.release(mS)
                n = 4 * TT
                P.act(sq[:, 0:n], oT[:, 0:n], AF.Square)
                P.mm(PS[2][:, 0:n], ones_bf[:, :], sq[:, 0:n])
                P.act(lnv[:, 0:n], PS[2][:, 0:n], AF.Ln, scale=1.0 / 128, bias=EPS)
                P.act(rstd[:, 0:n], lnv[:, 0:n], AF.Exp, scale=-0.5)
                P.tt("dve", t1[:, 0:n], oT[:, 0:n], rstd[:, 0:n], ALU.mult)
                P.tt("dve", ohgT[:, :, t0:t0 + TT], t1[:, 0:n].rearrange("p (h t) -> p h t", h=4), sgT[:, :, c0:c0 + TT], ALU.mult)
                yield

        def drive(gens):
            alive = list(gens)
            while alive:
                for g in list(alive):
                    try:
                        next(g)
                    except StopIteration:
                        alive.remove(g)

        tl = [(sti, ti) for sti in range(len(HSTS)) for ti in range(HSTS[sti][2])]
        drive([stageA(0)])
        drive([stageB1(*tl[0])])
        for n, (sti, ti) in enumerate(tl):
            gs = [stageB2(sti, ti)]
            if n + 1 < len(tl):
                gs.append(stageB1(*tl[n + 1]))
            if ti == 0 and sti + 1 < len(HSTS):
                gs.append(stageA(sti + 1))
            drive(gs)
        P.dma("sp", o_hg_p.rearrange("h d v -> d h v"), S32[:, :, :])
        A.release(m0)


    def phase_gdn():
        A.off = OFF_OGDN
        m0 = A.mark()
        WG = 128
        wgd = A.alloc("wgd", [128, 8, 2056], BF16)
        for k in range(8):
            P.dma("pool", wgd[:, k, :], w_in_v[:, k, 2048:4104], wk=[KY(wgd, k)])
        wkeys = [KY(wgd, k) for k in range(8)]
        cext = A.alloc("gcext", [128, 12, 3 + WG], BF16)
        cact = A.alloc("gcact", [128, 8, WG], F32)
        th = A.alloc("gth", [128, 12, WG], BF16)
        sqb = A.alloc("gsq", [128, 8, WG], BF16)
        lnv = A.alloc("glnv", [128, 8, WG], F32)
        dgw = A.alloc("gdgw", [128, 48, 128], BF16)
        qnT_2 = [A.alloc("gqnT%d" % i, [128, 4, WG], BF16) for i in range(2)]
        knT_2 = [A.alloc("gknT%d" % i, [128, 4, WG], BF16) for i in range(2)]
        qhT_2 = [A.alloc("gqhT%d" % i, [128, 4, WG], BF16) for i in range(2)]
        vTb_2 = [A.alloc("gvTb%d" % i, [128, 4, WG], BF16) for i in range(2)]
        Gbc_2 = [A.alloc("gGbc%d" % i, [128, 4, WG], F32) for i in range(2)]
        eGbc_2 = [A.alloc("geGbc%d" % i, [128, 4, WG], F32) for i in range(2)]
        bbc_2 = [A.alloc("gbbc%d" % i, [128, 4, WG], F32) for i in range(2)]
        gT = A.alloc("ggT", [4, WG], F32)
        GT_2 = [A.alloc("gGT%d" % i, [4, WG], F32) for i in range(2)]
        GmT_2 = [A.alloc("gGmT%d" % i, [4, WG], F32) for i in range(2)]
        bT_2 = [A.alloc("gbT%d" % i, [4, WG], F32) for i in range(2)]
        zsg_2 = [A.alloc("gzsg%d" % i, [128, 4, WG], F32) for i in range(2)]
        tok = A.alloc("gtok", [128, 12], F32)
        etok = A.alloc("getok", [128, 8], F32)
        hb = A.alloc("ghb", [128, 4], F32)
        beg = A.alloc("gbeg", [128, 4], F32)
        D1 = A.alloc("gD1", [128, 4, 128], F32)
        AqkT = A.alloc("gAqkT", [128, 4, 128], BF16)
        T1 = A.alloc("gT1", [128, 4, 128], BF16)
        ATb = [A.alloc("gAT%d" % i, [128, 4, 128], BF16) for i in range(1)]
        Tn = [A.alloc("gTn%d" % i, [128, 4, 128], BF16) for i in range(2)]
        Vb = A.alloc("gVb", [128, 4, 128], BF16)
        PTb = [A.alloc("gPT%d" % i, [128, 4, 128], BF16) for i in range(2)]
        kbg = A.alloc("gkbg", [128, 4, 128], BF16)
        khat = A.alloc("gkhat", [128, 4, 128], BF16)
        vbt = A.alloc("gvbt", [128, 4, 128], BF16)
        nwT = A.alloc("gnwT", [128, 4, 128], BF16)
        vnew = A.alloc("gvnew", [128, 4, 128], BF16)
        S32 = A.alloc("gS32", [128, 4, 128], F32)
        Sbf = A.alloc("gSbf", [128, 4, 128], BF16)
        sq2 = A.alloc("gsq2", [128, 512], BF16)
        lnv2 = A.alloc("glnv2", [128, 512], F32)
        t1 = A.alloc("gt1", [128, 512], F32)
        ghalf = A.alloc("gghalf", [128, 1], F32)
        wcv = A.alloc("gwcv", [128, 4, 12], F32)
        P.copy("dve", wcv[:, :, :].rearrange("p j k -> p (j k)"), vc("w_conv", 0, 48))
        P.tt("dve", dgw[:, :, :], AP(ident_bf, 0, [[128, 128], [0, 48], [1, 128]]), AP(wcv, 0, [[48, 128], [1, 48], [0, 128]]), ALU.mult)
        P.ts("dve", ghalf[:, :], vc("g_gdn", 0, 1), 0.5, None, ALU.mult)
        P.memset("dve", S32[:, :, :], 0.0)
        P.memset("dve", Sbf[:, :, :], 0.0)
        P.memset("dve", cext[:, :, :], 0.0)
        nA = A.alloc("gnA", [4, 1], F32)
        P.act(nA[:, :], vec[0:4, VC["a_log"]:VC["a_log"] + 1], AF.Exp)
        P.ts("dve", nA[:, :], nA[:, :], -1.0, None, ALU.mult)
        dtb = vec[0:4, VC["dt_bias"]:VC["dt_bias"] + 1]
        sel = C("sel4", 4)
        pj = [0]

        def proj_fm(col0, tok0, W, M=128):
            ps = PS[pj[0] % 2]
            pj[0] += 1
            for k in range(8):
                P.mm(ps[0:M, 0:W], wgd[:, k, col0:col0 + M], xnT[:, k, tok0:tok0 + W], start=(k == 0), stop=(k == 7),
                     rk=[wkeys[k], xnT])
            return ps

        GSTS = [(WG * i, WG, 1, 128, False) for i in range(16)] + [(SEQ, 64, 1, 64, True)]
        def stageA(sti):
            (tok0, W, ntile, TT, is_s) = GSTS[sti]
            (qnT, knT, qhT, vTb, Gbc, eGbc, bbc, GT, GmT, bT, zsg) = [x[sti % 2] for x in (qnT_2, knT_2, qhT_2, vTb_2, Gbc_2, eGbc_2, bbc_2, GT_2, GmT_2, bT_2, zsg_2)]
            if is_s:
                cx = AP(cext, 0, [[12 * (3 + WG), 128], [3 + WG, 12], [7, NS], [1, 7]])
                mcr = A.mark()
                crow = A.alloc("gcrow", [48, 1536], F32)
                P.dma("sp", crow[:, :], st_conv_in.rearrange("i j c -> (i j) c"))
                for half in range(2):
                    for cc in range(6):
                        c = 6 * half + cc
                        P.tr(PS[half][:, cc * 48:(cc + 1) * 48], crow[0:48, c * 128:(c + 1) * 128], C("ident", 48)[:, 0:48])
                    P.copy("dve", cx[:, 6 * half:6 * half + 6, :, 0:3], PS[half][:, 0:288].rearrange("p (c i j) -> p c i j", c=6, i=NS))
                A.release(mcr)
            else:
                if tok0 > 0:
                    P.copy("dve", cext[:, :, 0:3], cext[:, :, WG:WG + 3])
            for c in range(12):
                ps = proj_fm(c * 128, tok0, W)
                if is_s:
                    P.copy("act", cx[:, c, :, 3:7], ps[:, 0:64].rearrange("p (i t) -> p i t", i=NS))
                else:
                    P.copy("act", cext[:, c, 3:3 + W], ps[:, 0:W])
                yield
            if is_s or tok0 + W == SEQ:
                mcv = A.mark()
                cvo = A.alloc("gcvo", [64, 1536], F32)
                rows = slice(tok0, tok0 + 64) if is_s else slice(SEQ - 3, SEQ)
                nr = 64 if is_s else 3
                for cb in range(3):
                    for k in range(8):
                        P.mm(PS[cb % 2][0:nr, :], xnT[:, k, rows], wgd[:, k, cb * 512:(cb + 1) * 512], start=(k == 0), stop=(k == 7),
                             rk=[wkeys[k], xnT])
                    P.copy("act", cvo[0:nr, cb * 512:(cb + 1) * 512], PS[cb % 2][0:nr, :])
                if is_s:
                    for j in range(3):
                        src = AP(cvo, (1 + j) * 1536, [[4 * 1536, NS], [1, 1536]])
                        P.dma("sp", o_conv_s[:, j, :], src)
                else:
                    P.dma("sp", o_conv_p, cvo[0:3, :])
                A.release(mcv)
            for g in range(3):
                bank = PS[g % 2]
                for cc in range(4):
                    c = 4 * g + cc
                    for j in range(4):
                        rhs = cx[:, c, :, j:j + 4] if is_s else cext[:, c, j:j + W]
                        P.mm(bank[:, cc * 128:cc * 128 + W], dgw[:, j * 12 + c, :], rhs, start=(j == 0), stop=(j == 3))
                bv = bank[:, :].rearrange("p (c t) -> p c t", c=4)[:, :, 0:W]
                P.act(th[:, 4 * g:4 * g + 4, 0:W], bv, AF.Tanh, scale=0.5)
                dst = cact[:, 4 * g:4 * g + 4, 0:W] if g < 2 else vTb[:, :, 0:W]
                P.stt("dve", dst, th[:, 4 * g:4 * g + 4, 0:W], 1.0, bv, ALU.add, ALU.mult)
            for h in range(4):
                ps = proj_fm(1536 + h * 128, tok0, W)
                P.act(zsg[:, h, 0:W], ps[:, 0:W], AF.Tanh, scale=0.5)
                P.stt("dve", zsg[:, h, 0:W], zsg[:, h, 0:W], 1.0, ps[:, 0:W], ALU.add, ALU.mult)
            P.ts("dve", zsg[:, :, 0:W], zsg[:, :, 0:W], ghalf[:, 0:1], None, ALU.mult)
            psb_ = proj_fm(2052, tok0, W, M=4)
            P.act(bT[:, 0:W], psb_[0:4, 0:W], AF.Tanh, scale=0.5)
            P.ts("dve", bT[:, 0:W], bT[:, 0:W], 0.5, 0.5, ALU.mult, ALU.add)
            yield
            psa = proj_fm(2048, tok0, W, M=4)
            P.act(gT[:, 0:W], psa[0:4, 0:W], AF.Exp, bias=dtb)
            P.act(gT[:, 0:W], gT[:, 0:W], AF.Ln, bias=1.0)
            P.ts("dve", gT[:, 0:W], gT[:, 0:W], nA[:, 0:1], None, ALU.mult)
            scanm = (C("scan4")[0:4, 0:W] if is_s else C("scan128")[0:4, 0:W])
            P.scan(GT[:, 0:W], scanm, gT[:, 0:W], 0.0, ALU.mult, ALU.add)
            if is_s:
                gcv = AP(GT, 3, [[WG, 4], [4, NS], [0, 4]])
                P.tt("dve", GmT[:, 0:64].rearrange("p (i t) -> p i t", i=NS), gcv, GT[:, 0:64].rearrange("p (i t) -> p i t", i=NS), ALU.subtract)
            else:
                gcv = AP(GT, 127, [[WG, 4], [128, ntile], [0, 128]])
                P.tt("dve", GmT[:, 0:W].rearrange("p (i t) -> p i t", i=ntile), gcv, GT[:, 0:W].rearrange("p (i t) -> p i t", i=ntile), ALU.subtract)
            for h in range(4):
                P.mm(PS[0][:, h * 128:h * 128 + W], sel[:, h * 128:(h + 1) * 128], GT[:, 0:W])
            for h in range(4):
                P.mm(PS[1][:, h * 128:h * 128 + W], sel[:, h * 128:(h + 1) * 128], bT[:, 0:W])
            gv = PS[0][:, :].rearrange("p (h t) -> p h t", h=4)[:, :, 0:W]
            P.copy("act", Gbc[:, :, 0:W], gv)
            P.act(eGbc[:, :, 0:W], gv, AF.Exp)
            P.copy("dve", bbc[:, :, 0:W], PS[1][:, :].rearrange("p (h t) -> p h t", h=4)[:, :, 0:W])
            yield
            P.act(sqb[:, :, 0:W], cact[:, 0:8, 0:W], AF.Square)
            for qk in range(2):
                for h in range(4):
                    P.mm(PS[qk][:, h * 128:h * 128 + W], ones_bf[:, :], sqb[:, 4 * qk + h, 0:W])
                P.act(lnv[:, 4 * qk:4 * qk + 4, 0:W], PS[qk][:, :].rearrange("p (h t) -> p h t", h=4)[:, :, 0:W], AF.Ln, bias=4.0 * EPS)
            P.act(lnv[:, :, 0:W], lnv[:, :, 0:W], AF.Exp, scale=-0.5)
            yield
            P.stt("dve", qnT[:, :, 0:W], cact[:, 0:4, 0:W], 128.0 ** -0.5, lnv[:, 0:4, 0:W], ALU.mult, ALU.mult)
            P.tt("dve", knT[:, :, 0:W], cact[:, 4:8, 0:W], lnv[:, 4:8, 0:W], ALU.mult)
            P.tt("dve", qhT[:, :, 0:W], qnT[:, :, 0:W], eGbc[:, :, 0:W], ALU.mult)
            yield

        def stageB(sti):
            (tok0, W, ntile, TT, is_s) = GSTS[sti]
            K = 2 if is_s else 7
            (qnT, knT, qhT, vTb, Gbc, eGbc, bbc, GT, GmT, bT, zsg) = [x[sti % 2] for x in (qnT_2, knT_2, qhT_2, vTb_2, Gbc_2, eGbc_2, bbc_2, GT_2, GmT_2, bT_2, zsg_2)]
            for ti in range(ntile):
                c0 = ti * TT
                t0 = tok0 + c0
                oT = PS[7]
                negm = C("negB_s", 64) if is_s else C("negG_p")
                nsm = C("nmaskBs_s", 64) if is_s else C("nmaskGs_p")
                P.tr(PS[2][0:TT, 0:4], GT[0:4, c0:c0 + TT], C("ident", 4)[:, 0:4])
                P.tr(PS[2][0:TT, 4:8], GmT[0:4, c0:c0 + TT], C("ident", 4)[:, 0:4])
                P.tr(PS[2][0:TT, 8:12], bT[0:4, c0:c0 + TT], C("ident", 4)[:, 0:4])
                P.copy("dve", tok[0:TT, :], PS[2][0:TT, 0:12])
                P.act(etok[0:TT, :], tok[0:TT, 0:8], AF.Exp)
                P.ts("dve", hb[0:TT, :], tok[0:TT, 8:12], 0.5, None, ALU.mult)
                P.tt("dve", beg[0:TT, :], tok[0:TT, 8:12], etok[0:TT, 0:4], ALU.mult)
                yield
                P.tt("pool", D1[0:TT, :, 0:TT], Gbc[0:TT, :, c0:c0 + TT], bc_free(tok[0:TT, 0:4], TT), ALU.subtract)
                negb = AP(negm.tensor, negm.offset, [list(negm.ap[0]), [0, 4], list(negm.ap[1])])
                P.tt("pool", D1[0:TT, :, 0:TT], D1[0:TT, :, 0:TT], negb, ALU.add)
                P.act(D1[0:TT, :, 0:TT], D1[0:TT, :, 0:TT], AF.Exp)
                nsb = AP(nsm.tensor, nsm.offset, [list(nsm.ap[0]), [0, 4], list(nsm.ap[1])])
                P.tt("pool", T1[0:TT, :, 0:TT], D1[0:TT, :, 0:TT], bbc[0:TT, :, c0:c0 + TT], ALU.mult)
                P.tt("pool", T1[0:TT, :, 0:TT], T1[0:TT, :, 0:TT], nsb, ALU.mult)
                yield
                for h in range(4):
                    P.mm(PS[2][0:TT, h * 128:h * 128 + TT], knT[:, h, c0:c0 + TT], knT[:, h, c0:c0 + TT], start=True, stop=True)
                for h in range(4):
                    P.mm(PS[3][0:TT, h * 128:h * 128 + TT], knT[:, h, c0:c0 + TT], qnT[:, h, c0:c0 + TT], start=True, stop=True)
                kkv = PS[2][0:TT, :].rearrange("p (h t) -> p h t", h=4)[:, :, 0:TT]
                qkv = PS[3][0:TT, :].rearrange("p (h t) -> p h t", h=4)[:, :, 0:TT]
                NT = ATb[0]
                P.tt("dve", NT[0:TT, :, 0:TT], kkv, T1[0:TT, :, 0:TT], ALU.mult)
                P.tt("dve", AqkT[0:TT, :, 0:TT], qkv, D1[0:TT, :, 0:TT], ALU.mult)
                yield
                for h in range(4):
                    P.tr(PSB[5][0:TT, h * 128:(h + 1) * 128], knT[:, h, c0:c0 + TT], ident_bf[:, :])
                for h in range(4):
                    P.act(kbg[0:TT, h, :], PSB[5][0:TT, h * 128:(h + 1) * 128], AF.Copy, scale=beg[0:TT, h:h + 1])
                    P.act(khat[0:TT, h, :], PSB[5][0:TT, h * 128:(h + 1) * 128], AF.Copy, scale=etok[0:TT, 4 + h:5 + h])
                for h in range(4):
                    P.tr(PSB[6][0:TT, h * 128:(h + 1) * 128], vTb[:, h, c0:c0 + TT], ident_bf[:, :])
                P.tt("dve", vbt[0:TT, :, :], PSB[6][0:TT, 0:512].rearrange("p (h d) -> p h d", h=4), bc_free(hb[0:TT, 0:4], 128), ALU.mult)
                yield
                idt = ident_bf[0:TT, 0:TT]

                def lmI(k):
                    return AP(lmk, k * 128, [[8 * 128, TT], [0, 2], [1, TT]])
                for hp in range(2):
                    for hh in range(2):
                        h = 2 * hp + hh
                        P.mm(PS[4 + hp][0:TT, hh * 128:hh * 128 + TT], NT[0:TT, h, 0:TT], idt, start=True, stop=False)
                        P.mm(PS[4 + hp][0:TT, hh * 128:hh * 128 + TT], idt, idt, start=False, stop=True)
                        P.mm(PS[6 + hp][0:TT, hh * 128:hh * 128 + TT], idt, NT[0:TT, h, 0:TT], start=True, stop=False)
                        P.mm(PS[6 + hp][0:TT, hh * 128:hh * 128 + TT], idt, idt, start=False, stop=True)
                for hp in range(2):
                    P.tt("dve", Tn[0][0:TT, 2 * hp:2 * hp + 2, 0:TT], PS[4 + hp][0:TT, 0:256].rearrange("p (h t) -> p h t", h=2)[:, :, 0:TT], lmI(0),
                         ALU.mult, wk=[KY(Tn[0], hp)])
                    P.tt("dve", PTb[0][0:TT, 2 * hp:2 * hp + 2, 0:TT], PS[6 + hp][0:TT, 0:256].rearrange("p (h t) -> p h t", h=2)[:, :, 0:TT], lmI(7),
                         ALU.mult, wk=[KY(PTb[0], hp)])
                yield
                cur = 0
                for k in range(1, K):
                    nxt = 1 - cur
                    for hp in range(2):
                        vb_ = PS[2 + hp]
                        for hh in range(2):
                            h = 2 * hp + hh
                            P.mm(vb_[0:TT, hh * 128:hh * 128 + TT], NT[0:TT, h, 0:TT], Tn[cur][0:TT, h, 0:TT], start=True, stop=False,
                                 rk=[NT, KY(Tn[cur], hp)])
                            P.mm(vb_[0:TT, hh * 128:hh * 128 + TT], ident_bf[0:TT, 0:TT], ident_bf[0:TT, 0:TT], start=False, stop=True)
                    for hp in range(2):
                        P.tt("dve", Vb[0:TT, 2 * hp:2 * hp + 2, 0:TT], PS[2 + hp][0:TT, 0:256].rearrange("p (h t) -> p h t", h=2)[:, :, 0:TT], lmI(k),
                             ALU.mult, wk=[KY(Vb, hp)])
                    for hp in range(2):
                        if k < K - 1:
                            for hh in range(2):
                                h = 2 * hp + hh
                                P.mm(PS[4 + hp][0:TT, hh * 128:hh * 128 + TT], PTb[cur][0:TT, h, 0:TT], Vb[0:TT, h, 0:TT], start=True, stop=True,
                                     rk=[KY(PTb[cur], hp), KY(Vb, hp)])
                        for hh in range(2):
                            h = 2 * hp + hh
                            P.mm(PS[6 + hp][0:TT, hh * 128:hh * 128 + TT], Vb[0:TT, h, 0:TT], PTb[cur][0:TT, h, 0:TT], start=True, stop=True,
                                 rk=[KY(PTb[cur], hp), KY(Vb, hp)])
                    for hp in range(2):
                        if k < K - 1:
                            P.copy("act", Tn[nxt][0:TT, 2 * hp:2 * hp + 2, 0:TT], PS[4 + hp][0:TT, 0:256].rearrange("p (h t) -> p h t", h=2)[:, :, 0:TT],
                                   wk=[KY(Tn[nxt], hp)])
                        P.copy("act", PTb[nxt][0:TT, 2 * hp:2 * hp + 2, 0:TT],
                               PS[6 + hp][0:TT, 0:256].rearrange("p (h t) -> p h t", h=2)[:, :, 0:TT], wk=[KY(PTb[nxt], hp)])
                    cur = nxt
                    yield
                PT = PTb[cur]
                for h in range(4):
                    P.mm(PS[4][:, h * 128:h * 128 + TT], kbg[0:TT, h, :], PT[0:TT, h, 0:TT], start=True, stop=True, rk=[kbg, KY(PT, h // 2)])
                P.act(nwT[:, :, 0:TT], PS[4][:, :].rearrange("p (h t) -> p h t", h=4)[:, :, 0:TT], AF.Copy, scale=-1.0)
                yield
                if not is_s:
                    for h in range(4):
                        P.mm(PS[5][0:TT, h * 128:(h + 1) * 128], PT[0:TT, h, 0:TT], vbt[0:TT, h, :], start=True, stop=False, rk=[vbt, KY(PT, h // 2)])
                        P.mm(PS[5][0:TT, h * 128:(h + 1) * 128], nwT[:, h, 0:TT], Sbf[:, h, :], start=False, stop=True)
                    P.copy("act", vnew[0:TT, :, :], PS[5][0:TT, :].rearrange("p (h v) -> p h v", h=4))
                    yield
                    for h in range(4):
                        P.mm(oT[:, h * TT:(h + 1) * TT], vnew[0:TT, h, :], AqkT[0:TT, h, 0:TT], start=True, stop=False)
                        P.mm(oT[:, h * TT:(h + 1) * TT], Sbf[:, h, :], qhT[:, h, c0:c0 + TT], start=False, stop=True)
                    for h in range(4):
                        P.mm(PS[6][:, h * 128:(h + 1) * 128], khat[0:TT, h, :], vnew[0:TT, h, :], start=True, stop=True)
                    for h in range(4):
                        egc = eGbc[:, h, c0 + TT - 1:c0 + TT]
                        P.stt("dve", Sbf[:, h, :], S32[:, h, :], egc, PS[6][:, h * 128:(h + 1) * 128], ALU.mult, ALU.add)
                        P.stt("dve", S32[:, h, :], S32[:, h, :], egc, PS[6][:, h * 128:(h + 1) * 128], ALU.mult, ALU.add)
                else:
                    for h in range(4):
                        P.mm(PS[5][0:TT, h * 128:(h + 1) * 128], PT[0:TT, h, 0:TT], vbt[0:TT, h, :], start=True, stop=True, rk=[vbt, KY(PT, h // 2)])
                    wS = A.alloc("gwS", [64, 4, 128], F32)
                    mS = A.mark()
                    off_c = [o for (nm, o, e) in A.ranges if nm.startswith("gcact")][0]
                    end_c = [e for (nm, o, e) in A.ranges if nm.startswith("glnv_")][0]
                    assert end_c - off_c >= 12288 + 64, (off_c, end_c)
                    off_c = (off_c + 31) // 32 * 32
                    S0_2 = [A.alloc("gS0x", [128, 4, 4, 128], F32), A.alloc_at("gS0y", [128, 4, 4, 128], F32, off_c)]
                    S0bf_2 = [A.alloc("gS0bfx", [128, 4, 4, 128], BF16), A.alloc_at("gS0bfy", [128, 4, 4, 128], BF16, off_c + 8192)]
                    nwm = A.alloc("gnwm", [128, 4, 4, 64], BF16)
                    vnm = A.alloc("gvnm", [64, 4, 4, 128], BF16)
                    for quarter in range(4):
                        i0 = 4 * quarter
                        S0 = S0_2[quarter % 2]
                        S0bf = S0bf_2[quarter % 2]
                        P.dma("sp", S0[:, :, :, :], st_gdn_in[i0:i0 + 4].rearrange("i h d v -> d i h v"))
                        P.copy("act", S0bf[:, :, :, :], S0[:, :, :, :])
                        cmv = AP(cst, CONST_OFFS["cmneg"][0] + i0 * 64, [[NCONST, 128], [0, 4], [64, 4], [1, 64]])
                        nwv = AP(nwT, 0, [[4 * 128, 128], [128, 4], [0, 4], [1, 64]])
                        P.tt("dve", nwm[:, :, :, :], nwv, cmv, ALU.mult)
                        for h in range(4):
                            for ii in range(4):
                                i = i0 + ii
                                P.mm(PS[6][0:TT, h * 128:(h + 1) * 128], nwm[:, h, ii, :], S0bf[:, ii, h, :],
                                     start=(ii == 0), stop=(ii == 3))
                        if quarter == 0:
                            P.copy("dve", wS[0:TT, :, :], PS[6][0:TT, :].rearrange("p (h v) -> p h v", h=4))
                        else:
                            P.tt("dve", wS[0:TT, :, :], wS[0:TT, :, :], PS[6][0:TT, :].rearrange("p (h v) -> p h v", h=4), ALU.add)
                    P.tt("dve", vnew[0:TT, :, :], PS[5][0:TT, :].rearrange("p (h v) -> p h v", h=4), wS[0:TT, :, :], ALU.subtract)
                    for h in range(4):
                        P.mm(oT[:, h * TT:(h + 1) * TT], vnew[0:TT, h, :], AqkT[0:TT, h, 0:TT], start=(h == 0), stop=False)
                    for quarter in range(4):
                        i0 = 4 * quarter
                        S0 = S0_2[quarter % 2]
                        S0bf = S0bf_2[quarter % 2]
                        P.dma("sp", S0[:, :, :, :], st_gdn_in[i0:i0 + 4].rearrange("i h d v -> d i h v"))
                        P.copy("dve", S0bf[:, :, :, :], S0[:, :, :, :])
                        for h in range(4):
                            for ii in range(4):
                                i = i0 + ii
                                P.mm(oT[:, h * TT + 4 * i:h * TT + 4 * i + 4], S0bf[:, ii, h, :], qhT[:, h, 4 * i:4 * i + 4], start=False, stop=(quarter == 3 and h == 3 and ii == 3))
                        vnv = AP(vnew, 0, [[4 * 128, 64], [128, 4], [0, 4], [1, 128]])
                        bmv = AP(cst, CONST_OFFS["bm"][0] + i0, [[NCONST, 64], [0, 4], [1, 4], [0, 128]])
                        P.tt("dve", vnm[:, :, :, :], vnv, bmv, ALU.mult)
                        for ii in range(4):
                            i = i0 + ii
                            for h in range(4):
                                P.mm(PS[6][:, h * 128:(h + 1) * 128], khat[0:TT, h, :], vnm[:, h, ii, :], start=True, stop=True)
                            egc = AP(eGbc, 4 * i + 3, [[4 * WG, 128], [WG, 4], [0, 128]])
                            P.tt("dve", S0[:, ii, :, :], S0[:, ii, :, :], egc, ALU.mult)
                            P.tt("dve", S0[:, ii, :, :], S0[:, ii, :, :], PS[6][:, :].rearrange("p (h v) -> p h v", h=4), ALU.add)
                        P.dma("sp", o_gdn_s[i0:i0 + 4].rearrange("i h d v -> d i h v"), S0[:, :, :, :])
                    A.release(mS)
                n = 4 * TT
                P.act(sq2[:, 0:n], oT[:, 0:n], AF.Square)
                P.mm(PS[3][:, 0:n], ones_bf[:, :], sq2[:, 0:n])
                P.act(lnv2[:, 0:n], PS[3][:, 0:n], AF.Ln, scale=1.0 / 128, bias=EPS)
                P.act(lnv2[:, 0:n], lnv2[:, 0:n], AF.Exp, scale=-0.5)
                P.tt("dve", t1[:, 0:n], oT[:, 0:n], lnv2[:, 0:n], ALU.mult)
                P.tt("dve", ogdnT[:, :, t0:t0 + TT], t1[:, 0:n].rearrange("p (h t) -> p h t", h=4), zsg[:, :, c0:c0 + TT], ALU.mult)
            yield

        def drive(gens):
            alive = list(gens)
            while alive:
                for g in list(alive):
                    try:
                        next(g)
                    except StopIteration:
                        alive.remove(g)

        drive([stageA(0)])
        for sti in range(len(GSTS)):
            gs = [stageB(sti)]
            if sti + 1 < len(GSTS):
                gs.append(stageA(sti + 1))
            drive(gs)
        P.dma("sp", o_gdn_p.rearrange("h d v -> d h v"), S32[:, :, :])
        A.release(m0)

    phase_gdn()
    if debug:
        m0 = A.mark()
        dtmp = A.alloc("dbgtmp3", [128, 4, NTOK], F32)
        P.copy("dve", dtmp[:, :, :], ogdnT[:, :, :])
        P.dma("sp", dbg["ogdnT"], dtmp[:, :, :])
        A.release(m0)

    phase_hgrn()
    if debug:
        m0 = A.mark()
        dtmpx = A.alloc("dbgtmp2b", [128, 4, NTOK], F32)
        P.copy("dve", dtmpx[:, :, :], ohgT[:, :, :])
        P.dma("sp", dbg["ohgT"], dtmpx[:, :, :])
        A.release(m0)

    def phase_mem():
        A.off = OFF_MRG
        m0 = A.mark()
        memnT = A.alloc("memnT", [128, 8, MEM], BF16)
        norm_transpose(lambda ti: mem_prompt[128 * ti:128 * ti + 128, :], 2, lambda ti: 128, "g_mem", memnT, "m")
        wkv = A.alloc("wkv", [128, 8, 1024], BF16)
        w_kv_v = w_mem_kv.rearrange("(k p) c -> p k c", p=128)
        for k in range(8):
            P.dma("pool", wkv[:, k, :], w_kv_v[:, k, :], wk=[KY(wkv, k)])
        wq = A.alloc("wmq", [128, 8, 512], BF16)
        for k in range(8):
            P.dma("pool", wq[:, k, :], w_in_v[:, k, 4104:4616], wk=[KY(wq, k)])
        KT = A.alloc("mKT", [128, 4, MEM], BF16)
        Vsb = A.alloc("mVsb", [128, 2, 512], BF16)
        kvo = A.alloc("mkvo", [128, 2, 2, 512], F32)
        for h in range(4):
            for k in range(8):
                P.mm(PS[h % 2][:, 0:MEM], wkv[:, k, h * 128:(h + 1) * 128], memnT[:, k, :], start=(k == 0), stop=(k == 7),
                     rk=[KY(wkv, k), memnT])
            P.copy("act", KT[:, h, :], PS[h % 2][:, 0:MEM])
        for mt in range(2):
            for kv in range(2):
                ps = PS[2 + kv]
                for k in range(8):
                    P.mm(ps[:, :], memnT[:, k, mt * 128:(mt + 1) * 128], wkv[:, k, kv * 512:(kv + 1) * 512], start=(k == 0), stop=(k == 7),
                         rk=[KY(wkv, k), memnT])
                P.copy("act", kvo[:, kv, mt, :], ps[:, :])
                if kv == 1:
                    P.copy("dve", Vsb[:, mt, :], kvo[:, 1, mt, :])
        P.dma("sp", o_mk.rearrange("(mt p) c -> p mt c", p=128), kvo[:, 0, :, :])
        P.dma("sp", o_mv.rearrange("(mt p) c -> p mt c", p=128), kvo[:, 1, :, :])
        if stop == "mem1":
            return
        qT = A.alloc("mqT", [128, 2, 512], BF16)
        ET = A.alloc("mET", [128, 2, 512], BF16)
        rden = A.alloc("mrden", [128, 512], F32)
        qTs = A.alloc("mqTs", [128, 4, 64], BF16)
        cnt = [0]
        for (tok0, W, ntile, TT, is_s) in STS:
            for h in range(4):
                par = cnt[0] % 2
                cnt[0] += 1
                for k in range(8):
                    P.mm(PS[par][:, 0:W], wq[:, k, h * 128:(h + 1) * 128], xnT[:, k, tok0:tok0 + W], start=(k == 0), stop=(k == 7),
                         rk=[KY(wq, k), xnT])
                if is_s:
                    P.act(qTs[:, h, :], PS[par][:, 0:64], AF.Copy, scale=128.0 ** -0.5)
                    continue
                P.act(qT[:, par, :], PS[par][:, :], AF.Copy, scale=128.0 ** -0.5)
                for c in range(2):
                    P.mm(PS[2 + c][:, :], KT[:, h, c * 128:(c + 1) * 128], qT[:, par, :])
                    P.act(ET[:, c, :], PS[2 + c][:, :], AF.Exp)
                for c in range(2):
                    P.mm(PS[4 + par][:, :], Vsb[:, c, h * 128:(h + 1) * 128], ET[:, c, :], start=(c == 0), stop=(c == 1))
                for c in range(2):
                    P.mm(PS[6 + par][:, :], ones_bf[:, :], ET[:, c, :], start=(c == 0), stop=(c == 1))
                P.recip(rden[:, :], PS[6 + par][:, :])
                P.tt("dve", omemT[:, h, tok0:tok0 + W], PS[4 + par][:, :], rden[:, :], ALU.mult)
        if stop == "mem2":
            return
        ETs = A.alloc("mETs", [128, 2, 4, 64], BF16)
        ck_v = cache_k.rearrange("i (c p) f -> p i c f", p=128)
        cv_v = cache_v.rearrange("i (c p) f -> p i c f", p=128)
        first = [True]
        for quarter in range(4):
            mS = A.mark()
            i0 = 4 * quarter
            Kc = A.alloc("mKc", [128, 4, 2, 512], BF16)
            KcT = A.alloc("mKcT", [128, 4, 8, 128], BF16)
            P.dma("pool", Kc[:, :, :, :], ck_v[:, i0:i0 + 4, :, :])
            for ii in range(4):
                pb = PSB[ii % 2]
                for c in range(2):
                    for h in range(4):
                        P.tr(pb[:, (c * 4 + h) * 128:(c * 4 + h + 1) * 128], Kc[:, ii, c, h * 128:(h + 1) * 128], ident_bf[:, :])
                P.copy("act" if ii % 2 == 0 else "dve", KcT[:, ii, :, :], pb[:, 0:1024].rearrange("p (a m) -> p a m", a=8))
            for ii in range(4):
                i = i0 + ii
                for c in range(2):
                    for h in range(4):
                        col = (c * 4 + h) * 64 + 4 * i
                        P.mm(PS[2][:, col:col + 4], KcT[:, ii, c * 4 + h, :], qTs[:, h, 4 * i:4 * i + 4], start=first[0], stop=(i == 15 and c == 1 and h == 3))
                        first[0] = False
            A.release(mS)
        if stop == "mem3":
            return
        P.act(ETs[:, :, :, :].rearrange("p c h t -> p (c h t)"), PS[2][:, :], AF.Exp)
        first = [True]
        for quarter in range(4):
            mS = A.mark()
            i0 = 4 * quarter
            Vc = A.alloc("mVc", [128, 4, 2, 512], BF16)
            P.dma("pool", Vc[:, :, :, :], cv_v[:, i0:i0 + 4, :, :])
            for ii in range(4):
                i = i0 + ii
                for h in range(4):
                    for c in range(2):
                        P.mm(PS[3][:, h * 64 + 4 * i:h * 64 + 4 * i + 4], Vc[:, ii, c, h * 128:(h + 1) * 128], ETs[:, c, h, 4 * i:4 * i + 4],
                             start=first[0], stop=(i == 15 and c == 1 and h == 3))
                        first[0] = False
            A.release(mS)
        for c in range(2):
            P.mm(PS[4][:, 0:256], ones_bf[:, :], ETs[:, c, :, :].rearrange("p h t -> p (h t)"), start=(c == 0), stop=(c == 1))
        P.recip(rden[:, 0:256], PS[4][:, 0:256])
        P.tt("dve", omemT[:, :, SEQ:SEQ + 64], PS[3][:, 0:256].rearrange("p (h t) -> p h t", h=4), rden[:, 0:256].rearrange("p (h t) -> p h t", h=4), ALU.mult)
        A.release(m0)

    phase_mem()
    if stop in ("mem", "mem1", "mem2", "mem3"):
        P.lower()
        return nc
    if debug:
        m0 = A.mark()
        dtmp = A.alloc("dbgtmp4", [128, 4, NTOK], F32)
        P.copy("dve", dtmp[:, :, :], omemT[:, :, :])
        P.dma("sp", dbg["omemT"], dtmp[:, :, :])
        A.release(m0)

    mrgT = A.alloc_at("mrgT", [128, 8, NTOK], BF16, OFF_MRG)
    TGS = [(0, 512), (512, 512), (1024, 512), (1536, 512), (2048, 64)]

    def phase_merge():
        A.off = OFF_MRG_END
        m0 = A.mark()
        wbr = A.alloc("wbr", [128, 3, 4, D], BF16)
        for b, wsrc in enumerate((w_br_hg, w_br_gdn, w_br_mem)):
            P.dma("pool", wbr[:, b, :, :], wsrc.rearrange("(k p) c -> p k c", p=128), wk=[KY(wbr, b)])
        wg = [A.alloc("wgate%d" % i, [128, 8, 3, 128], BF16) for i in range(3)]
        thb = [A.alloc("mgth%d" % i, [128, 512], F32) for i in range(2)]
        acc = A.alloc("mgacc", [128, 512], F32)
        term = A.alloc("mgterm", [128, 512], F32)
        obs = (ohgT, ogdnT, omemT)
        wgv = w_in[:, 4616:7688].rearrange("(k p) (b f) -> p k b f", p=128, b=3)
        cnt = 0
        def load_gate_w(fc):
            wgb_ = wg[fc % 3]
            for b in range(3):
                P.dma("pool", wgb_[:, :, b, :], wgv[:, :, b, fc * 128:(fc + 1) * 128], wk=[KY(wgb_, b)])

        load_gate_w(0)
        load_gate_w(1)
        for fc in range(8):
            wgb = wg[fc % 3]
            if fc + 2 < 8:
                load_gate_w(fc + 2)
            for (t0, W) in TGS:
                for b in range(3):
                    par = cnt % 2
                    cnt += 1
                    gps = PS[par]
                    bps = PS[2 + par]
                    for k in range(8):
                        P.mm(gps[:, 0:W], wgb[:, k, b, :], xnT[:, k, t0:t0 + W], start=(k == 0), stop=(k == 7), rk=[KY(wgb, b), xnT])
                    P.act(thb[par][:, 0:W], gps[:, 0:W], AF.Tanh, scale=0.5)
                    for kc in range(4):
                        P.mm(bps[:, 0:W], wbr[:, b, kc, fc * 128:(fc + 1) * 128], obs[b][:, kc, t0:t0 + W], start=(kc == 0), stop=(kc == 3),
                             rk=[KY(wbr, b), obs[b]])
                    if b == 0:
                        P.stt("dve", acc[:, 0:W], thb[par][:, 0:W], 1.0, bps[:, 0:W], ALU.add, ALU.mult)
                    else:
                        P.stt("dve", term[:, 0:W], thb[par][:, 0:W], 1.0, bps[:, 0:W], ALU.add, ALU.mult)
                        P.tt("pool", acc[:, 0:W], acc[:, 0:W], term[:, 0:W], ALU.add)
                P.act(mrgT[:, fc, t0:t0 + W], acc[:, 0:W], AF.Copy, scale=0.5)
        A.release(m0)

    phase_merge()
    if stop == "merge":
        P.lower()
        return nc
    if debug:
        m0 = A.mark()
        dtmp = A.alloc("dbgtmp5", [128, 8, NTOK], F32)
        P.copy("dve", dtmp[:, :, :], mrgT[:, :, :])
        P.dma("sp", dbg["mrgT"], dtmp[:, :, :])
        A.release(m0)

    OFF_H = OFF_XNT
    OFF_H_END = OFF_H + 17 * 4096
    assert OFF_H_END <= OFF_MRG
    OFF_HNT = OFF_MRG_END
    OFF_HNT_END = OFF_HNT + 33792
    hall = A.alloc_at("hall", [128, 17, D], F32, OFF_H)
    hnT = A.alloc_at("hnT", [128, 8, NTOK], BF16, OFF_HNT)

    def tile_rows(ti):
        return 128 if ti < 16 else 64

    def phase_out():
        A.off = OFF_HNT_END
        m0 = A.mark()
        wo = A.alloc("wo", [128, 8, D], BF16)
        for k in range(8):
            P.dma("pool", wo[:, k, :], w_out.rearrange("(k p) c -> p k c", p=128)[:, k, :], wk=[KY(wo, k)])
        gpm = A.alloc("gpm", [128, D], F32)
        P.dma("sp", gpm[:, :], g_post_mix.partition_broadcast(128))
        xt = [A.alloc("oxt%d" % i, [128, D], F32) for i in range(2)]
        jk = A.alloc("ojk", [128, D], BF16)
        junk = A.alloc("ojunk", [128, 512], BF16)
        hs = [A.alloc("ohs%d" % i, [128, D], BF16) for i in range(2)]
        stat = A.alloc("ostat", [128, 100], F32)
        P.memset("dve", stat[:, :], 0.0)
        for ti in range(17):
            rows = tile_rows(ti)
            tk0 = 128 * ti
            for half in range(2):
                ps = PS[(2 * ti + half) % 4]
                for k in range(8):
                    P.mm(ps[0:rows, :], mrgT[:, k, tk0:tk0 + rows], wo[:, k, half * 512:(half + 1) * 512], start=(k == 0), stop=(k == 7),
                         rk=[mrgT, KY(wo, k)])
                P.act(junk[0:rows, :], ps[0:rows, :], AF.Square, accum_out=stat[0:rows, 2 * ti + half:2 * ti + half + 1])
                P.copy("dve", hall[0:rows, ti, half * 512:(half + 1) * 512], ps[0:rows, :], wk=[KY(hall, ti)])
        sv = stat[:, 0:34].rearrange("p (t h) -> p t h", h=2)
        P.tt("dve", stat[:, 40:57], sv[:, :, 0], sv[:, :, 1], ALU.add)
        P.act(stat[:, 40:57], stat[:, 40:57], AF.Ln, scale=1.0 / D, bias=EPS)
        P.act(stat[:, 40:57], stat[:, 40:57], AF.Exp, scale=-0.5)
        for ti in range(17):
            rows = tile_rows(ti)
            P.dma("sp", xt[ti % 2][0:rows, :], x_rows(ti))
            P.stt("dve", hall[0:rows, ti, :], hall[0:rows, ti, :], stat[0:rows, 40 + ti:41 + ti], gpm[0:rows, :], ALU.mult, ALU.mult,
                  rk=[KY(hall, ti), stat, gpm], wk=[KY(hall, ti)])
            P.tt("pool", hall[0:rows, ti, :], hall[0:rows, ti, :], xt[ti % 2][0:rows, :], ALU.add, rk=[KY(hall, ti), xt[ti % 2]], wk=[KY(hall, ti), hall])
            P.act(jk[0:rows, :], hall[0:rows, ti, :], AF.Square, accum_out=stat[0:rows, 60 + ti:61 + ti], rk=[KY(hall, ti)], wk=[jk, KY(stat, "b")])
        P.act(stat[:, 80:97], stat[:, 60:77], AF.Ln, scale=1.0 / D, bias=EPS, rk=[KY(stat, "b"), stat], wk=[KY(stat, "c")])
        P.act(stat[:, 80:97], stat[:, 80:97], AF.Exp, scale=-0.5, rk=[KY(stat, "c")], wk=[KY(stat, "c")])
        for ti in range(17):
            rows = tile_rows(ti)
            tk0 = 128 * ti
            hsb = hs[ti % 2]
            P.act(hsb[0:rows, :], hall[0:rows, ti, :], AF.Copy, scale=stat[0:rows, 80 + ti:81 + ti], rk=[KY(hall, ti), KY(stat, "c")])
            pb = PSB[4 + ti % 2]
            for k in range(8):
                P.tr(pb[:, k * 128:k * 128 + rows], hsb[0:rows, k * 128:(k + 1) * 128], ident_bf[0:rows, 0:rows])
            src = pb[:, 0:1024].rearrange("p (k t) -> p k t", k=8)[:, :, 0:rows]
            P.tt("dve", hnT[:, :, tk0:tk0 + rows], src, bc_free(vc("g_pre_ffn", 0, 8), rows), ALU.mult)
        A.release(m0)

    phase_out()
    if stop == "out":
        P.lower()
        return nc
    if debug:
        P.dma("sp", dbg["h"][0:SEQ, :].rearrange("(t p) d -> p t d", p=128), hall[:, 0:16, :])
        P.dma("sp", dbg["h"][SEQ:NTOK, :], hall[0:64, 16, :])

    def phase_ffn():
        wfo = A.alloc_at("wfo", [128, 22, D], BF16, OFF_H_END)
        assert OFF_H_END + 22 * D * 2 <= OFF_HNT
        for j in range(22):
            P.dma("pool", wfo[:, j, :], w_ffn_out[128 * j:128 * j + 128, :], wk=[KY(wfo, j)])
        A.off = OFF_HNT_END
        actT = A.alloc("actT", [128, 22, 576], BF16)
        wfi = [A.alloc("wfi%d" % i, [128, 8, 2, 128], BF16) for i in range(3)]
        s2 = A.alloc("fs2", [128, 512], F32)
        A.off = SB_BASE
        gpf = A.alloc("gpf", [128, D], F32)
        P.dma("sp", gpf[:, :], g_post_ffn.partition_broadcast(128))
        thf_ = A.alloc("fth", [128, 512], F32)
        yt = [A.alloc("fyt%d" % i, [128, D], F32) for i in range(2)]
        junk = A.alloc("fjunk", [128, 512], BF16)
        stat = A.alloc("fstat", [128, 8], F32)
        wfv = w_ffn_in.rearrange("(k p) (u f) -> p k u f", p=128, u=2)
        PASSES = [(0, 512), (512, 512), (1024, 512), (1536, 576)]
        nblk = 0
        ycnt = 0
        for (p0, PW) in PASSES:
            subs = [(0, 512)] if PW == 512 else [(0, 512), (512, 64)]
            for j in range(22):
                wb = wfi[nblk % 3]
                nblk += 1
                for u in range(2):
                    P.dma("pool", wb[:, :, u, :], wfv[:, :, u, 128 * j:128 * j + 128], wk=[KY(wb, u)])
                for (s0, SW) in subs:
                    par = (nblk + (s0 > 0)) % 2
                    gps = PS[2 * par]
                    ups = PS[2 * par + 1]
                    for k in range(8):
                        P.mm(gps[:, 0:SW], wb[:, k, 0, :], hnT[:, k, p0 + s0:p0 + s0 + SW], start=(k == 0), stop=(k == 7), rk=[KY(wb, 0), hnT])
                    for k in range(8):
                        P.mm(ups[:, 0:SW], wb[:, k, 1, :], hnT[:, k, p0 + s0:p0 + s0 + SW], start=(k == 0), stop=(k == 7), rk=[KY(wb, 1), hnT])
                    P.act(thf_[:, 0:SW], gps[:, 0:SW], AF.Tanh, scale=0.5)
                    P.stt("dve", s2[:, 0:SW], thf_[:, 0:SW], 1.0, gps[:, 0:SW], ALU.add, ALU.mult)
                    P.stt("dve", actT[:, j, s0:s0 + SW], s2[:, 0:SW], 0.5, ups[:, 0:SW], ALU.mult, ALU.mult)
            ntl = PW // 128 + (1 if PW % 128 else 0)
            for tl in range(ntl):
                ti = p0 // 128 + tl
                rows = tile_rows(ti)
                c0 = 128 * tl
                yb = yt[ycnt % 2]
                ycnt += 1
                P.memset("dve", stat[:, :], 0.0)
                for half in range(2):
                    ps = PS[4 + 2 * (tl % 2) + half]
                    for j in range(22):
                        P.mm(ps[0:rows, :], actT[:, j, c0:c0 + rows], wfo[:, j, half * 512:(half + 1) * 512], start=(j == 0), stop=(j == 21),
                             rk=[actT, KY(wfo, j)])
                    P.act(junk[0:rows, :], ps[0:rows, :], AF.Square, accum_out=stat[0:rows, half:half + 1])
                P.tt("dve", stat[0:rows, 2:3], stat[0:rows, 0:1], stat[0:rows, 1:2], ALU.add)
                P.act(stat[0:rows, 3:4], stat[0:rows, 2:3], AF.Ln, scale=1.0 / D, bias=EPS)
                P.act(stat[0:rows, 3:4], stat[0:rows, 3:4], AF.Exp, scale=-0.5)
                for half in range(2):
                    ps = PS[4 + 2 * (tl % 2) + half]
                    P.stt("dve", yb[0:rows, half * 512:(half + 1) * 512], ps[0:rows, :], stat[0:rows, 3:4], gpf[0:rows, half * 512:(half + 1) * 512],
                          ALU.mult, ALU.mult)
                P.tt("pool", yb[0:rows, :], yb[0:rows, :], hall[0:rows, ti, :], ALU.add)
                if ti < 16:
                    P.dma("sp", y_prompt[128 * ti:128 * ti + 128, :], yb[:, :])
                else:
                    P.dma("sp", y_sample[:, :], yb[0:64, :])

    phase_ffn()

    P.lower()
    return nc


_CACHE = {}


def _get_prog(debug=False):
    if debug not in _CACHE:
        _CACHE[debug] = build_program(debug)
    return _CACHE[debug]


def kernel(**inputs):
    return run(inputs, debug=False)


def run(inputs, debug=False):
    nc = _get_prog(debug)
    f = lambda a: np.ascontiguousarray(np.asarray(a, dtype=np.float32))
    in_maps = []
    for c in range(NCORES):
        sl = slice(NS * c, NS * c + NS)
        m = {
            "x_prompt": f(inputs["x_prompt"][c]),
            "x_sample": f(inputs["x_sample"][sl]).reshape(NS * TS, D),
            "mem_prompt": f(inputs["mem_prompt"][c]),
            "cache_mem_k": f(inputs["cache_mem_k"][0, sl]).reshape(NS, MEM, 512),
            "cache_mem_v": f(inputs["cache_mem_v"][0, sl]).reshape(NS, MEM, 512),
            "state_hgrn": f(inputs["state_hgrn"][0, sl]),
            "state_gdn": f(inputs["state_gdn"][0, sl]),
            "state_gdn_conv": f(inputs["state_gdn_conv"][0, sl]),
            "hg_lb_logits": f(inputs["hg_lb_logits"]),
            "g_pre_mix": f(inputs["g_pre_mix"][0]),
            "w_in": f(inputs["w_in"][0]),
            "w_conv": f(inputs["w_conv"][0]),
            "a_log": f(inputs["a_log"][0]),
            "dt_bias": f(inputs["dt_bias"][0]),
            "g_hg_out": f(inputs["g_hg_out"][0]),
            "g_gdn_out": f(inputs["g_gdn_out"][0]),
            "g_mem": f(inputs["g_mem"][0]),
            "w_mem_kv": f(inputs["w_mem_kv"][0]),
            "w_br_hg": f(inputs["w_br_hg"][0]),
            "w_br_gdn": f(inputs["w_br_gdn"][0]),
            "w_br_mem": f(inputs["w_br_mem"][0]),
            "w_out": f(inputs["w_out"][0]),
            "g_post_mix": f(inputs["g_post_mix"][0]),
            "g_pre_ffn": f(inputs["g_pre_ffn"][0]),
            "w_ffn_in": f(inputs["w_ffn_in"][0]),
            "w_ffn_out": f(inputs["w_ffn_out"][0]),
            "g_post_ffn": f(inputs["g_post_ffn"][0]),
            "consts": CONST_ARR,
            "lmasks": LMASK_ARR,
        }
        in_maps.append(m)
    res = run_bass_kernel_spmd(nc, in_maps, core_ids=list(range(NCORES)))
    R = res.results
    if debug:
        return R
    yp = np.stack([R[c]["y_prompt"] for c in range(NCORES)], 0)
    ys = np.concatenate([R[c]["y_sample"].reshape(NS, TS, D) for c in range(NCORES)], 0)
    mk = np.stack([R[c]["new_mem_k"].reshape(MEM, 4, 128) for c in range(NCORES)], 0)[None]
    mv = np.stack([R[c]["new_mem_v"].reshape(MEM, 4, 128) for c in range(NCORES)], 0)[None]
    hgp = np.stack([R[c]["new_hg_p"] for c in range(NCORES)], 0)[None]
    gdp = np.stack([R[c]["new_gdn_p"] for c in range(NCORES)], 0)[None]
    cvp = np.stack([R[c]["new_conv_p"] for c in range(NCORES)], 0)[None]
    hgs = np.concatenate([R[c]["new_hg_s"] for c in range(NCORES)], 0)[None]
    gds = np.concatenate([R[c]["new_gdn_s"] for c in range(NCORES)], 0)[None]
    cvs = np.concatenate([R[c]["new_conv_s"] for c in range(NCORES)], 0)[None]
    return (yp, ys, mk, mv, hgp, gdp, cvp, hgs, gds, cvs)
```

```python
import os
import numpy as np
import ml_dtypes
import concourse.bass as bass
import concourse.mybir as mybir
from concourse.ap import AP
from concourse.bass_utils import run_bass_kernel_spmd

F32 = mybir.dt.float32
BF16 = mybir.dt.bfloat16
AF = mybir.ActivationFunctionType
ALU = mybir.AluOpType

NCORES = 8
D = 1024
SEQ = 2048
NS = 16
TS = 4
NTOK = SEQ + NS * TS
MEM = 256
FFH = 2816
INW = 7688
EPS = 1e-6
SB_BASE = 16512
SB_TOP = 229344


class Op:
    __slots__ = ("eng", "idx", "build", "deps", "dma", "sem", "semval", "mark")


class Prog:
    ENGS = ("pe", "act", "dve", "pool", "sp")

    def __init__(self, nc):
        self.nc = nc
        self.streams = {e: [] for e in self.ENGS}
        self.last_w = {}
        self.readers = {}
        self.alias = {}
        self.keys_of = {}

    @staticmethod
    def _keys(aps):
        out = []
        for a in aps:
            if a is None or isinstance(a, (int, float)):
                continue
            if isinstance(a, (str, tuple)):
                out.append(a)
            else:
                out.append(getattr(a, "tensor", a).name)
        return out

    def emit(self, eng, build, reads, writes, dma=False):
        op = Op()
        op.eng = eng
        op.build = build
        op.dma = dma
        op.mark = False
        op.sem = None
        op.semval = 0
        st = self.streams[eng]
        op.idx = len(st)
        st.append(op)
        rk = self._keys(reads)
        wk = self._keys(writes)
        deps = {}

        def add(d, kind):
            if d is None or d is op:
                return
            if (not d.dma) and d.eng == eng and not dma:
                if eng == "pe":
                    return
            deps[id(d)] = d

        for k in rk + wk:
            base = k if isinstance(k, str) else k[0]
            self.keys_of.setdefault(base, set()).add(k)
        for k in rk:
            add(self.last_w.get(k), "raw")
            base = k if isinstance(k, str) else k[0]
            if isinstance(base, str) and base.startswith("psb"):
                for r in self.readers.get(k, ()):
                    if r.eng != eng:
                        add(r, "rar")
        for k in wk:
            add(self.last_w.get(k), "waw")
            for r in self.readers.get(k, ()):
                add(r, "war")
            base = k if isinstance(k, str) else k[0]
            for o in self.alias.get(base, ()):
                for kk in self.keys_of.get(o, ()):
                    add(self.last_w.get(kk), "waw")
                    for r in self.readers.get(kk, ()):
                        add(r, "war")
        op.deps = list(deps.values())
        for k in rk:
            lst = self.readers.setdefault(k, [])
            if not dma:
                lst[:] = [r for r in lst if r.dma or r.eng != eng]
            lst.append(op)
        for k in wk:
            self.last_w[k] = op
            self.readers[k] = []
        return op

    def lower(self):
        nc = self.nc
        eng_sem = {e: nc.alloc_semaphore("cs_" + e) for e in self.ENGS}
        ndma = {"sp": 14, "pool": 14, "act": 4, "pe": 1, "dve": 1}
        dma_sems = {e: [nc.alloc_semaphore("ds_%s%d" % (e, i)) for i in range(ndma[e])] for e in ("sp", "pool", "act")}
        for e in self.ENGS:
            for op in self.streams[e]:
                for d in op.deps:
                    if not d.dma:
                        d.mark = True
        finals = {}
        for e in self.ENGS:
            cnt = 0
            m = 0
            for op in self.streams[e]:
                if op.dma:
                    sems = dma_sems[e]
                    op.sem = sems[m % len(sems)]
                    op.semval = 16 * (m // len(sems) + 1)
                    finals[op.sem] = op.semval
                    m += 1
                else:
                    if op.mark:
                        cnt += 1
                    op.sem = eng_sem[e]
                    op.semval = cnt
        streams = self.streams

        def run(name, h):
            waited = {}
            for op in streams[name]:
                waits = {}
                for d in op.deps:
                    if waits.get(d.sem, 0) < d.semval:
                        waits[d.sem] = d.semval
                if op.dma and op.semval > 16:
                    if waits.get(op.sem, 0) < op.semval - 16:
                        waits[op.sem] = op.semval - 16
                for sem, val in waits.items():
                    if waited.get(sem, 0) >= val:
                        continue
                    h.wait_ge(sem, val)
                    waited[sem] = val
                ins = op.build(h)
                if op.dma:
                    ins.then_inc(op.sem, 16)
                elif op.mark:
                    ins.then_inc(op.sem, 1)
            if name == "sp":
                for sem, val in finals.items():
                    if waited.get(sem, 0) < val:
                        h.wait_ge(sem, val)

        with nc.Block() as block:
            @block.tensor
            def _(h):
                run("pe", h)

            @block.scalar
            def _(h):
                run("act", h)

            @block.vector
            def _(h):
                run("dve", h)

            @block.gpsimd
            def _(h):
                run("pool", h)

            @block.sync
            def _(h):
                run("sp", h)

    def dma(self, eng, out, in_, rk=None, wk=None, nc_ok=False):
        nc = self.nc

        def b(h):
            if nc_ok:
                with nc.allow_non_contiguous_dma(reason="small strided constant load"):
                    return h.dma_start(out=out, in_=in_)
            return h.dma_start(out=out, in_=in_)
        return self.emit(eng, b, rk if rk is not None else [in_], wk if wk is not None else [out], dma=True)

    def mm(self, out, lhsT, rhs, start=True, stop=True, rk=None, wk=None):
        return self.emit("pe", lambda h: h.matmul(out, lhsT=lhsT, rhs=rhs, start=start, stop=stop),
                         rk if rk is not None else [lhsT, rhs], wk if wk is not None else [out])

    def tr(self, out, in_, ident, rk=None, wk=None):
        return self.emit("pe", lambda h: h.transpose(out=out, in_=in_, identity=ident),
                         rk if rk is not None else [in_, ident], wk if wk is not None else [out])

    def act(self, out, in_, func, scale=None, bias=None, accum_out=None, rk=None, wk=None):
        kw = {}
        if scale is not None:
            kw["scale"] = scale
        if bias is not None:
            kw["bias"] = bias
        if accum_out is not None:
            kw["accum_out"] = accum_out
        rd = [in_]
        if isinstance(scale, AP):
            rd.append(scale)
        if isinstance(bias, AP):
            rd.append(bias)
        return self.emit("act", lambda h: h.activation(out=out, in_=in_, func=func, **kw),
                         rk if rk is not None else rd, wk if wk is not None else [out, accum_out])

    def tt(self, eng, out, in0, in1, op, rk=None, wk=None):
        return self.emit(eng, lambda h: h.tensor_tensor(out=out, in0=in0, in1=in1, op=op),
                         rk if rk is not None else [in0, in1], wk if wk is not None else [out])

    def ts(self, eng, out, in0, s1, s2, op0, op1=None, rk=None, wk=None):
        rd = [in0]
        if isinstance(s1, AP):
            rd.append(s1)
        if isinstance(s2, AP):
            rd.append(s2)

        def b(h):
            if op1 is None:
                return h.tensor_scalar(out=out, in0=in0, scalar1=s1, scalar2=None, op0=op0)
            return h.tensor_scalar(out=out, in0=in0, scalar1=s1, scalar2=s2, op0=op0, op1=op1)
        return self.emit(eng, b, rk if rk is not None else rd, wk if wk is not None else [out])

    def stt(self, eng, out, in0, scalar, in1, op0, op1, rk=None, wk=None):
        rd = [in0, in1]
        if isinstance(scalar, AP):
            rd.append(scalar)
        return self.emit(eng, lambda h: h.scalar_tensor_tensor(out=out, in0=in0, scalar=scalar, in1=in1, op0=op0, op1=op1),
                         rk if rk is not None else rd, wk if wk is not None else [out])

    def copy(self, eng, out, in_, rk=None, wk=None):
        if eng == "act":
            return self.act(out, in_, AF.Copy, rk=rk, wk=wk)
        return self.emit(eng, lambda h: h.tensor_copy(out=out, in_=in_),
                         rk if rk is not None else [in_], wk if wk is not None else [out])

    def memset(self, eng, out, val, wk=None):
        return self.emit(eng, lambda h: h.memset(out, val), [], wk if wk is not None else [out])

    def recip(self, out, in_, rk=None, wk=None):
        return self.emit("dve", lambda h: h.reciprocal(out=out, in_=in_),
                         rk if rk is not None else [in_], wk if wk is not None else [out])

    def scan(self, out, d0, d1, init, op0, op1, rk=None, wk=None):
        return self.emit("dve", lambda h: h.tensor_tensor_scan(out=out, data0=d0, data1=d1, initial=init, op0=op0, op1=op1),
                         rk if rk is not None else [d0, d1], wk if wk is not None else [out])


class Arena:
    def __init__(self, nc, prog):
        self.nc = nc
        self.prog = prog
        self.off = SB_BASE
        self.n = 0
        self.peak = SB_BASE
        self.ranges = []

    def _register(self, t, off, per):
        al = self.prog.alias
        for (nm, o, e) in self.ranges:
            if o < off + per and off < e:
                al.setdefault(t.name, set()).add(nm)
                al.setdefault(nm, set()).add(t.name)
        self.ranges.append((t.name, off, off + per))

    def alloc(self, name, shape, dtype):
        esz = 2 if dtype == BF16 else 4
        per = esz
        for s in shape[1:]:
            per *= s
        off = (self.off + 31) // 32 * 32
        assert off + per <= SB_TOP, "SBUF overflow allocating %s (%d bytes at %d)" % (name, per, off)
        self.n += 1
        t = self.nc.alloc_sbuf_tensor_at("%s_%d" % (name, self.n), list(shape), dtype, offset=off)
        self._register(t, off, per)
        self.off = off + per
        self.peak = max(self.peak, self.off)
        return t

    def alloc_at(self, name, shape, dtype, off):
        esz = 2 if dtype == BF16 else 4
        per = esz
        for x in shape[1:]:
            per *= x
        assert off % 32 == 0 and off >= SB_BASE and off + per <= SB_TOP, "bad alloc_at %s" % name
        self.n += 1
        t = self.nc.alloc_sbuf_tensor_at("%s_%d" % (name, self.n), list(shape), dtype, offset=off)
        self._register(t, off, per)
        return t

    def mark(self):
        return self.off

    def release(self, m):
        self.off = m


def interleave(gens, width, stagger=0):
    gens = list(gens)
    live = []
    nxt = 0
    while live or nxt < len(gens):
        while len(live) < width and nxt < len(gens) and (not live or live[-1][1] >= stagger):
            live.append([gens[nxt], 0])
            nxt += 1
        for e in list(live):
            try:
                next(e[0])
                e[1] += 1
            except StopIteration:
                live.remove(e)


def bc_free(t_ap, n):
    ap = [list(x) for x in t_ap.ap]
    if len(ap) >= 2 and ap[-1][1] == 1:
        ap = ap[:-1]
    return AP(t_ap.tensor, t_ap.offset, ap + [[0, n]])


def make_consts():
    c = {}
    s = np.arange(128)[:, None]
    t = np.arange(128)[None, :]
    c["maskH_p"] = ((s <= t) & (s // 64 == t // 64)).astype(np.float32)
    s6 = np.arange(64)[:, None]
    t6 = np.arange(64)[None, :]
    m_s = ((s6 <= t6) & (s6 // 4 == t6 // 4)).astype(np.float32)
    c["maskB_s"] = np.zeros((128, 64), np.float32)
    c["maskB_s"][:64] = m_s
    c["maskG_p"] = (s <= t).astype(np.float32)
    c["maskGs_p"] = (s < t).astype(np.float32)
    ms = ((s6 < t6) & (s6 // 4 == t6 // 4)).astype(np.float32)
    c["maskBs_s"] = np.zeros((128, 64), np.float32)
    c["maskBs_s"][:64] = ms
    c["nmaskGs_p"] = -c["maskGs_p"]
    c["nmaskBs_s"] = -c["maskBs_s"]
    c["negG_p"] = np.where(s <= t, 0.0, -30000.0).astype(np.float32)
    ng = np.where((s6 <= t6) & (s6 // 4 == t6 // 4), 0.0, -30000.0).astype(np.float32)
    c["negB_s"] = np.zeros((128, 64), np.float32)
    c["negB_s"][:64] = ng
    tt = np.arange(512)[None, :]
    c["cmneg"] = np.zeros((128, 16 * 64), np.float32)
    cmn = -(np.arange(64)[None, :] // 4 == np.arange(16)[:, None]).astype(np.float32)
    c["cmneg"][:] = cmn.reshape(1, 1024)
    c["scan64"] = np.broadcast_to((tt % 64 != 0).astype(np.float32), (128, 512)).copy()
    c["scan128"] = np.broadcast_to((tt % 128 != 0).astype(np.float32), (128, 512)).copy()
    c["scan4"] = np.broadcast_to((np.arange(64)[None, :] % 4 != 0).astype(np.float32), (128, 64)).copy()
    bm = (np.arange(128)[:, None] // 4 == np.arange(16)[None, :]).astype(np.float32)
    c["bm"] = bm
    sel = np.zeros((128, 4 * 128), np.float32)
    for h in range(4):
        sel[h, h * 128:(h + 1) * 128] = 1.0
    c["sel4"] = sel
    c["ident"] = np.eye(128, dtype=np.float32)
    c["ones"] = np.ones((128, 128), np.float32)
    names = list(c.keys())
    offs = {}
    o = 0
    for k in names:
        offs[k] = (o, c[k].shape[1])
        o += c[k].shape[1]
    arr = np.concatenate([c[k] for k in names], axis=1).astype(np.float32)
    return arr, offs


CONST_ARR, CONST_OFFS = make_consts()
NCONST = CONST_ARR.shape[1]


def make_level_masks():
    t = np.arange(128)[:, None]
    u = np.arange(128)[None, :]
    ms = []
    for k in range(7):
        m = ((t >> (k + 1)) == (u >> (k + 1))) & ((t >> k) != (u >> k)) & (t > u)
        ms.append(m.astype(np.float32) + np.eye(128, dtype=np.float32))
    ms.append(ms[0].T.copy())
    return np.concatenate(ms, axis=1).astype(ml_dtypes.bfloat16)


LMASK_ARR = make_level_masks()


def build_program(debug=False):
    nc = bass.Bass("TRN2", target_bir_lowering=False)
    stop = os.environ.get("KSTOP", "") if debug else ""
    P = Prog(nc)
    A = Arena(nc, P)

    def din(name, shape, dt=F32):
        return nc.dram_tensor(name, list(shape), dt, kind="ExternalInput").ap()

    def dout(name, shape, dt=F32):
        return nc.dram_tensor(name, list(shape), dt, kind="ExternalOutput").ap()

    x_prompt = din("x_prompt", [SEQ, D])
    x_sample = din("x_sample", [NS * TS, D])
    mem_prompt = din("mem_prompt", [MEM, D])
    cache_k = din("cache_mem_k", [NS, MEM, 512])
    cache_v = din("cache_mem_v", [NS, MEM, 512])
    st_hg_in = din("state_hgrn", [NS, 4, 128, 128])
    st_gdn_in = din("state_gdn", [NS, 4, 128, 128])
    st_conv_in = din("state_gdn_conv", [NS, 3, 1536])
    lb_logits = din("hg_lb_logits", [2, 512])
    g_pre_mix = din("g_pre_mix", [D])
    w_in = din("w_in", [D, INW])
    w_conv = din("w_conv", [4, 1536])
    a_log = din("a_log", [4])
    dt_bias = din("dt_bias", [4])
    g_hg_out = din("g_hg_out", [512])
    g_gdn_out = din("g_gdn_out", [128])
    g_mem = din("g_mem", [D])
    w_mem_kv = din("w_mem_kv", [D, 1024])
    w_br_hg = din("w_br_hg", [512, D])
    w_br_gdn = din("w_br_gdn", [512, D])
    w_br_mem = din("w_br_mem", [512, D])
    w_out = din("w_out", [D, D])
    g_post_mix = din("g_post_mix", [D])
    g_pre_ffn = din("g_pre_ffn", [D])
    w_ffn_in = din("w_ffn_in", [D, 2 * FFH])
    w_ffn_out = din("w_ffn_out", [FFH, D])
    g_post_ffn = din("g_post_ffn", [D])
    consts_d = din("consts", [128, NCONST])
    lmask_d = din("lmasks", [128, 1024], BF16)

    y_prompt = dout("y_prompt", [SEQ, D])
    y_sample = dout("y_sample", [NS * TS, D])
    o_mk = dout("new_mem_k", [MEM, 512])
    o_mv = dout("new_mem_v", [MEM, 512])
    o_hg_p = dout("new_hg_p", [4, 128, 128])
    o_gdn_p = dout("new_gdn_p", [4, 128, 128])
    o_conv_p = dout("new_conv_p", [3, 1536])
    o_hg_s = dout("new_hg_s", [NS, 4, 128, 128])
    o_gdn_s = dout("new_gdn_s", [NS, 4, 128, 128])
    o_conv_s = dout("new_conv_s", [NS, 3, 1536])
    dbg = {}
    if debug:
        dbg["ohgT"] = dout("dbg_ohgT", [128, 4, NTOK])
        dbg["xnT"] = dout("dbg_xnT", [128, 8, NTOK])
        dbg["ogdnT"] = dout("dbg_ogdnT", [128, 4, NTOK])
        dbg["omemT"] = dout("dbg_omemT", [128, 4, NTOK])
        dbg["mrgT"] = dout("dbg_mrgT", [128, 8, NTOK])
        dbg["h"] = dout("dbg_h", [NTOK, D])

    PS = [nc.alloc_psum_tensor("psb%d" % i, [128, 512], F32) for i in range(8)]
    PSB = [p.bitcast(BF16) for p in PS]

    cst = A.alloc("cst", [128, NCONST], F32)
    P.dma("sp", cst[:, :], consts_d)

    def C(name, rows=128):
        o, w = CONST_OFFS[name]
        return cst[0:rows, o:o + w]

    lmk = A.alloc("lmk", [128, 8, 128], BF16)
    P.dma("sp", lmk[:, :, :].rearrange("p k t -> p (k t)"), lmask_d)
    ident_bf = A.alloc("identbf", [128, 128], BF16)
    ones_bf = A.alloc("onesbf", [128, 128], BF16)
    P.copy("dve", ident_bf[:, :], C("ident"))
    P.copy("dve", ones_bf[:, :], C("ones"))

    vec = A.alloc("vec", [128, 160], F32)
    vrow = A.alloc("vrow", [128, 128], F32)
    ident32 = C("ident")
    P.memset("dve", vrow[:, :], 0.0)
    VC = {}
    vcol = [0]
    vparts = []

    def load_cols(name, src, n):
        c0 = vcol[0]
        vcol[0] += n
        P.dma("sp", vrow[c0:c0 + n, 0:src.shape[-1]], src, rk=[vrow], wk=[("vrowpart", name)])
        VC[name] = c0
        vparts.append(("vrowpart", name))
        return c0

    load_cols("g_pre_mix", g_pre_mix.rearrange("(k p) -> k p", p=128), 8)
    load_cols("g_mem", g_mem.rearrange("(k p) -> k p", p=128), 8)
    load_cols("g_hg", g_hg_out.rearrange("(k p) -> k p", p=128), 4)
    load_cols("g_gdn", g_gdn_out.rearrange("(k p) -> k p", p=128), 1)
    load_cols("l0", lb_logits[0].rearrange("(k p) -> k p", p=128), 4)
    load_cols("l1", lb_logits[1].rearrange("(k p) -> k p", p=128), 4)
    load_cols("g_post_mix", g_post_mix.rearrange("(k p) -> k p", p=128), 8)
    load_cols("g_pre_ffn", g_pre_ffn.rearrange("(k p) -> k p", p=128), 8)
    load_cols("g_post_ffn", g_post_ffn.rearrange("(k p) -> k p", p=128), 8)
    load_cols("w_conv", w_conv.rearrange("j (k p) -> (j k) p", p=128), 48)
    load_cols("a_log", a_log.rearrange("(o h) -> o h", o=1), 1)
    load_cols("dt_bias", dt_bias.rearrange("(o h) -> o h", o=1), 1)
    assert vcol[0] <= 128
    vcol[0] = 128
    P.tr(PS[7][:, 0:128], vrow[:, :], ident32, rk=[vrow, cst] + vparts)
    P.copy("dve", vec[:, 0:128], PS[7][:, 0:128])

    def vc(name, k=0, n=1):
        c0 = VC[name] + k
        return vec[:, c0:c0 + n]

    VC["lb"] = vcol[0]; vcol[0] += 4
    VC["oml"] = vcol[0]; vcol[0] += 4
    VC["tmp4"] = vcol[0]; vcol[0] += 4
    VC["halfoml"] = vcol[0]; vcol[0] += 4
    VC["fbias"] = vcol[0]; vcol[0] += 4
    VC["noml"] = vcol[0]; vcol[0] += 4
    P.tt("dve", vc("tmp4", 0, 4), vc("l1", 0, 4), vc("l0", 0, 4), ALU.subtract)
    P.act(vc("tmp4", 0, 4), vc("tmp4", 0, 4), AF.Exp)
    P.ts("dve", vc("tmp4", 0, 4), vc("tmp4", 0, 4), 1.0, None, ALU.add)
    P.recip(vc("lb", 0, 4), vc("tmp4", 0, 4))
    P.ts("dve", vc("oml", 0, 4), vc("lb", 0, 4), -1.0, 1.0, ALU.mult, ALU.add)
    P.ts("dve", vc("halfoml", 0, 4), vc("oml", 0, 4), 0.5, None, ALU.mult)
    P.tt("dve", vc("fbias", 0, 4), vc("lb", 0, 4), vc("halfoml", 0, 4), ALU.add)
    P.ts("dve", vc("noml", 0, 4), vc("halfoml", 0, 4), -1.0, None, ALU.mult)

    OFF_XNT = 36864
    OFF_OHG = OFF_XNT + 33792
    OFF_OGDN = OFF_OHG + 16896
    OFF_OMEM = OFF_OGDN + 16896
    OFF_MRG = OFF_OMEM + 16896
    OFF_MRG_END = OFF_MRG + 33792
    assert A.off <= OFF_XNT, A.off
    xnT = A.alloc_at("xnT", [128, 8, NTOK], BF16, OFF_XNT)
    A.off = OFF_OHG

    def norm_transpose(src_rows_fn, ntiles, rows_fn, gname, dstT, tag):
        m0 = A.mark()
        xb = [A.alloc("xt%d" % i, [128, D], F32) for i in range(3)]
        xs = [A.alloc("xs%d" % i, [128, D], BF16) for i in range(3)]
        junk = [A.alloc("junk%d" % i, [128, D], BF16) for i in range(3)]
        stat = A.alloc("stat", [128, 3 * 32], F32)
        P.memset("dve", stat[:, :], 0.0)
        def chain(ti):
            rows = rows_fn(ti)
            xt = xb[ti % 3]
            P.dma("sp", xt[0:rows, :], src_rows_fn(ti))
            P.act(junk[ti % 3][0:rows, :], xt[0:rows, :], AF.Square, accum_out=stat[0:rows, ti:ti + 1])
            yield
            P.act(stat[0:rows, 32 + ti:33 + ti], stat[0:rows, ti:ti + 1], AF.Ln, scale=1.0 / D, bias=EPS)
            P.act(stat[0:rows, 64 + ti:65 + ti], stat[0:rows, 32 + ti:33 + ti], AF.Exp, scale=-0.5)
            xsb = xs[ti % 3]
            P.act(xsb[0:rows, :], xt[0:rows, :], AF.Copy, scale=stat[0:rows, 64 + ti:65 + ti])
            yield
            pb = PSB[ti % 3]
            for k in range(8):
                P.tr(pb[:, k * 128:k * 128 + rows], xsb[0:rows, k * 128:(k + 1) * 128], ident_bf[0:rows, 0:rows])
            tok0 = 128 * ti
            src = pb[:, 0:1024].rearrange("p (k t) -> p k t", k=8)[:, :, 0:rows]
            P.tt("dve", dstT[:, :, tok0:tok0 + rows], src, bc_free(vc(gname, 0, 8), rows), ALU.mult)
            yield

        interleave([chain(ti) for ti in range(ntiles)], 3, 1)
        A.release(m0)

    def x_rows(ti):
        if ti < 16:
            return x_prompt[128 * ti:128 * ti + 128, :]
        return x_sample[:, :]

    norm_transpose(x_rows, 17, lambda ti: 128 if ti < 16 else 64, "g_pre_mix", xnT, "x")

    if debug:
        m0 = A.mark()
        dtmp = A.alloc("dbgtmp", [128, 8, NTOK], F32)
        P.copy("dve", dtmp[:, :, :], xnT[:, :, :])
        P.dma("sp", dbg["xnT"], dtmp[:, :, :])
        A.release(m0)


    w_in_v = w_in.rearrange("(k p) c -> p k c", p=128)
    ogdnT = A.alloc_at("ogdnT", [128, 4, NTOK], BF16, OFF_OHG)
    ohgT = A.alloc_at("ohgT", [128, 4, NTOK], BF16, OFF_OGDN)
    omemT = A.alloc_at("omemT", [128, 4, NTOK], BF16, OFF_OMEM)

    def KY(t, *idx):
        return (t.name,) + tuple(idx)

    STS = [(512 * i, 512, 4, 128, False) for i in range(4)] + [(SEQ, 64, 1, 64, True)]

    def phase_hgrn():
        A.off = OFF_OMEM
        m0 = A.mark()
        whg = A.alloc("whg", [128, 8, 2048], BF16)
        for k in range(8):
            P.dma("pool", whg[:, k, :], w_in_v[:, k, 0:2048], wk=[KY(whg, k)])
        wkeys = [KY(whg, k) for k in range(8)]
        WH = 256
        v_sb_2 = [A.alloc("hv%d" % i, [128, 2, 512], BF16) for i in range(2)]
        thf = A.alloc("hthf", [128, 4, WH], F32)
        logf = thf
        G = A.alloc("hG", [128, 4, WH], F32)
        eG_2 = [A.alloc("heG%d" % i, [128, 4, WH], F32) for i in range(2)]
        eNG = thf
        kk = A.alloc("hkk", [128, 4, WH], F32)
        qtT_2 = [A.alloc("hqtT%d" % i, [128, 4, WH], BF16) for i in range(2)]
        ktT_2 = [A.alloc("hktT%d" % i, [128, 4, WH], BF16) for i in range(2)]
        thg_2 = [A.alloc("hthg%d" % i, [128, 4, WH], F32) for i in range(2)]
        ATm_2 = [A.alloc("hATm%d" % i, [128, 4, 128], BF16) for i in range(2)]
        ktok_2 = [A.alloc("hktok%d" % i, [128, 4, 128], BF16) for i in range(2)]
        S32 = A.alloc("hS32", [128, 4, 128], F32)
        Sbf = A.alloc("hSbf", [128, 4, 128], BF16)
        tmpS = A.alloc("htmpS", [128, 4, 128], F32)
        sq = A.alloc("hsq", [128, 512], BF16)
        lnv = A.alloc("hlnv", [128, 512], F32)
        rstd = A.alloc("hrstd", [128, 512], F32)
        t1 = A.alloc("ht1", [128, 512], F32)
        ghalf = A.alloc("hghalf", [128, 4], F32)
        P.ts("dve", ghalf[:, :], vc("g_hg", 0, 4), 0.5, None, ALU.mult)
        P.memset("dve", S32[:, :, :], 0.0)
        P.memset("dve", Sbf[:, :, :], 0.0)
        pj = [0]

        def proj_fm(col0, tok0, W):
            ps = PS[pj[0] % 2]
            pj[0] += 1
            for k in range(8):
                P.mm(ps[:, 0:W], whg[:, k, col0:col0 + 128], xnT[:, k, tok0:tok0 + W], start=(k == 0), stop=(k == 7),
                     rk=[wkeys[k], xnT])
            return ps

        HSTS = [(WH * i, WH, 2, 128, False) for i in range(8)] + [(SEQ, 64, 1, 64, True)]

        def stageA(sti):
            (tok0, W, ntile, TT, is_s) = HSTS[sti]
            (v_sb, eG, qtT, ktT, thg) = [x[sti % 2] for x in (v_sb_2, eG_2, qtT_2, ktT_2, thg_2)]
            sgT = thg
            for ti in range(ntile):
                t0 = tok0 + ti * TT
                psv = PS[pj[0] % 2]
                pj[0] += 1
                for k in range(8):
                    P.mm(psv[0:TT, :], xnT[:, k, t0:t0 + TT], whg[:, k, 1024:1536], start=(k == 0), stop=(k == 7),
                         rk=[wkeys[k], xnT])
                P.copy("act", v_sb[0:TT, ti, :], psv[0:TT, :])
                yield
            for h in range(4):
                ps = proj_fm(512 + h * 128, tok0, W)
                P.act(thf[:, h, 0:W], ps[:, 0:W], AF.Tanh, scale=0.5, wk=[KY(thf, h)])
            for h in range(4):
                ps = proj_fm(1536 + h * 128, tok0, W)
                P.act(thg[:, h, 0:W], ps[:, 0:W], AF.Tanh, scale=0.5)
                P.stt("dve", thg[:, h, 0:W], thg[:, h, 0:W], 1.0, ps[:, 0:W], ALU.add, ALU.mult)
                P.ts("dve", sgT[:, h, 0:W], thg[:, h, 0:W], ghalf[:, h:h + 1], None, ALU.mult)
            yield
            scanm = C("scan4")[:, 0:W] if is_s else C("scan64")[:, 0:W]
            for h in range(4):
                P.ts("dve", kk[:, h, 0:W], thf[:, h, 0:W], vc("noml", h), vc("halfoml", h), ALU.mult, ALU.add, rk=[KY(thf, h), vec], wk=[KY(kk, h)])
            yield
            for h in range(4):
                P.act(logf[:, h, 0:W], thf[:, h, 0:W], AF.Ln, scale=vc("halfoml", h), bias=vc("fbias", h), rk=[KY(thf, h), KY(kk, h), vec], wk=[KY(thf, h)])
            yield
            for h in range(4):
                P.scan(G[:, h, 0:W], scanm, logf[:, h, 0:W], 0.0, ALU.mult, ALU.add, rk=[KY(thf, h), cst], wk=[KY(G, h)])
            yield
            for h in range(4):
                P.act(eG[:, h, 0:W], G[:, h, 0:W], AF.Exp, rk=[KY(G, h)], wk=[KY(eG, h), eG])
                P.act(eNG[:, h, 0:W], G[:, h, 0:W], AF.Exp, scale=-1.0, rk=[KY(G, h)], wk=[KY(thf, h)])
            yield
            for h in range(4):
                ps = proj_fm(h * 128, tok0, W)
                P.tt("dve", qtT[:, h, 0:W], ps[:, 0:W], eG[:, h, 0:W], ALU.mult, rk=[ps, KY(eG, h)], wk=[KY(qtT, h), qtT])
                P.tt("pool", ktT[:, h, 0:W], kk[:, h, 0:W], eNG[:, h, 0:W], ALU.mult, rk=[KY(kk, h), KY(thf, h)], wk=[KY(ktT, h), ktT])
                yield

        def stageB1(sti, ti):
            (tok0, W, ntile, TT, is_s) = HSTS[sti]
            (v_sb, eG, qtT, ktT, thg) = [x[sti % 2] for x in (v_sb_2, eG_2, qtT_2, ktT_2, thg_2)]
            sgT = thg
            if True:
                c0 = ti * TT
                t0 = tok0 + c0
                gt = 2 * sti + ti
                oT = PS[6 + (gt % 2)]
                ATm = ATm_2[gt % 2]
                ktok = ktok_2[gt % 2]
                mask = C("maskB_s", 64) if is_s else C("maskH_p")
                for h in range(4):
                    P.mm(PS[3][0:TT, h * 128:h * 128 + TT], ktT[:, h, c0:c0 + TT], qtT[:, h, c0:c0 + TT])
                maskb = AP(mask.tensor, mask.offset, [list(mask.ap[0]), [0, 4], list(mask.ap[1])])
                P.tt("dve", ATm[0:TT, :, 0:TT], PS[3][0:TT, :].rearrange("p (h t) -> p h t", h=4)[:, :, 0:TT], maskb, ALU.mult)
                yield
                for h in range(4):
                    P.tr(PSB[4][0:TT, h * 128:(h + 1) * 128], ktT[:, h, c0:c0 + TT], ident_bf[:, :])
                P.copy("act", ktok[0:TT, :, :], PSB[4][0:TT, 0:512].rearrange("p (h d) -> p h d", h=4))
                yield
                for h in range(4):
                    P.mm(oT[:, h * TT:(h + 1) * TT], v_sb[0:TT, ti, h * 128:(h + 1) * 128], ATm[0:TT, h, 0:TT], start=(h == 0), stop=False)
                yield

        def stageB2(sti, ti):
            (tok0, W, ntile, TT, is_s) = HSTS[sti]
            (v_sb, eG, qtT, ktT, thg) = [x[sti % 2] for x in (v_sb_2, eG_2, qtT_2, ktT_2, thg_2)]
            sgT = thg
            if True:
                c0 = ti * TT
                t0 = tok0 + c0
                gt = 2 * sti + ti
                oT = PS[6 + (gt % 2)]
                ATm = ATm_2[gt % 2]
                ktok = ktok_2[gt % 2]
                if not is_s:
                    for b in range(2):
                        r0 = 64 * b
                        for h in range(4):
                            P.mm(oT[:, h * TT + r0:h * TT + r0 + 64], Sbf[:, h, :], qtT[:, h, c0 + r0:c0 + r0 + 64], start=False, stop=(b == 1 and h == 3))
                        for h in range(4):
                            P.mm(PS[5][:, h * 128:(h + 1) * 128], ktok[r0:r0 + 64, h, :], v_sb[r0:r0 + 64, ti, h * 128:(h + 1) * 128])
                        egcv = AP(eG, c0 + r0 + 63, [[4 * WH, 128], [WH, 4], [0, 128]])
                        P.tt("dve", S32[:, :, :], S32[:, :, :], PS[5][:, :].rearrange("p (h v) -> p h v", h=4), ALU.add)
                        P.tt("dve", Sbf[:, :, :], S32[:, :, :], egcv, ALU.mult)
                        P.tt("dve", S32[:, :, :], S32[:, :, :], egcv, ALU.mult)
                        yield
                else:
                    mS = A.mark()
                    S0_2 = [A.alloc("hS0%d" % q_, [128, 4, 4, 128], F32) for q_ in range(2)]
                    S0bf_2 = [A.alloc("hS0bf%d" % q_, [128, 4, 4, 128], BF16) for q_ in range(2)]
                    vm_2 = [A.alloc("hvm%d" % q_, [128, 2, 4, 128], BF16) for q_ in range(2)]
                    for grp in range(4):
                        i0 = 4 * grp
                        S0 = S0_2[grp % 2]
                        S0bf = S0bf_2[grp % 2]
                        vm = vm_2[grp % 2]
                        P.dma("sp", S0[:, :, :, :], st_hg_in[i0:i0 + 4].rearrange("i h d v -> d i h v"))
                        P.copy("act", S0bf[:, :, :, :], S0[:, :, :, :])
                        for h in range(4):
                            for ii in range(4):
                                i = i0 + ii
                                P.mm(oT[:, h * TT + 4 * i:h * TT + 4 * i + 4], S0bf[:, ii, h, :], qtT[:, h, 4 * i:4 * i + 4], start=False,
                                     stop=(grp == 3 and h == 3 and ii == 3))
                        egc = AP(eG, 4 * i0 + 3, [[4 * WH, 128], [4, 4], [WH, 4], [0, 128]])
                        P.tt("dve", S0[:, :, :, :], S0[:, :, :, :], egc, ALU.mult, rk=[S0, S0bf, eG])
                        for h in range(4):
                            vin = AP(v_sb, h * 128, [[2 * 512, 64], [0, 4], [1, 128]])
                            bmv = bc_free(C("bm", 64)[:, i0:i0 + 4], 128)
                            vmh = vm[0:64, h % 2, :, :]
                            P.tt("dve", vmh, vin, bmv, ALU.mult, rk=[v_sb, cst], wk=[KY(vm, h % 2)])
                            for ii in range(4):
                                P.mm(PS[5][:, ii * 128:(ii + 1) * 128], ktok[0:64, h, :], vm[0:64, h % 2, ii, :], rk=[ktok, KY(vm, h % 2)])
                            egc2 = AP(eG, h * WH + 4 * i0 + 3, [[4 * WH, 128], [4, 4], [0, 128]])
                            P.tt("dve", tmpS[:, :, :], PS[5][:, :].rearrange("p (i v) -> p i v", i=4), egc2, ALU.mult)
                            P.tt("dve", S0[:, :, h, :], tmpS[:, :, :], S0[:, :, h, :], ALU.add)
                        P.dma("sp", o_hg_s[i0:i0 + 4].rearrange("i h d v -> d i h v"), S0[:, :, :, :])
                        yield
                    A.release(mS)
                n = 4 * TT
                P.act(sq[:, 0:n], oT[:, 0:n], AF.Square)
                P.mm(PS[2][:, 0:n], ones_bf[:, :], sq[:, 0:n])
                P.act(lnv[:, 0:n], PS[2][:, 0:n], AF.Ln, scale=1.0 / 128, bias=EPS)
                P.act(rstd[:, 0:n], lnv[:, 0:n], AF.Exp, scale=-0.5)
                P.tt("dve", t1[:, 0:n], oT[:, 0:n], rstd[:, 0:n], ALU.mult)
                P.tt("dve", ohgT[:, :, t0:t0 + TT], t1[:, 0:n].rearrange("p (h t) -> p h t", h=4), sgT[:, :, c0:c0 + TT], ALU.mult)
                yield

        def drive(gens):
            alive = list(gens)
            while alive:
                for g in list(alive):
                    try:
                        next(g)
                    except StopIteration:
                        alive.remove(g)

        tl = [(sti, ti) for sti in range(len(HSTS)) for ti in range(HSTS[sti][2])]
        drive([stageA(0)])
        drive([stageB1(*tl[0])])
        for n, (sti, ti) in enumerate(tl):
            gs = [stageB2(sti, ti)]
            if n + 1 < len(tl):
                gs.append(stageB1(*tl[n + 1]))
            if ti == 0 and sti + 1 < len(HSTS):
                gs.append(stageA(sti + 1))
            drive(gs)
        P.dma("sp", o_hg_p.rearrange("h d v -> d h v"), S32[:, :, :])
        A.release(m0)


    def phase_gdn():
        A.off = OFF_OGDN
        m0 = A.mark()
        WG = 128
        wgd = A.alloc("wgd", [128, 8, 2056], BF16)
        for k in range(8):
            P.dma("pool", wgd[:, k, :], w_in_v[:, k, 2048:4104], wk=[KY(wgd, k)])
        wkeys = [KY(wgd, k) for k in range(8)]
        cext = A.alloc("gcext", [128, 12, 3 + WG], BF16)
        cact = A.alloc("gcact", [128, 8, WG], F32)
        th = A.alloc("gth", [128, 12, WG], BF16)
        sqb = A.alloc("gsq", [128, 8, WG], BF16)
        lnv = A.alloc("glnv", [128, 8, WG], F32)
        dgw = A.alloc("gdgw", [128, 48, 128], BF16)
        qnT_2 = [A.alloc("gqnT%d" % i, [128, 4, WG], BF16) for i in range(2)]
        knT_2 = [A.alloc("gknT%d" % i, [128, 4, WG], BF16) for i in range(2)]
        qhT_2 = [A.alloc("gqhT%d" % i, [128, 4, WG], BF16) for i in range(2)]
        vTb_2 = [A.alloc("gvTb%d" % i, [128, 4, WG], BF16) for i in range(2)]
        Gbc_2 = [A.alloc("gGbc%d" % i, [128, 4, WG], F32) for i in range(2)]
        eGbc_2 = [A.alloc("geGbc%d" % i, [128, 4, WG], F32) for i in range(2)]
        bbc_2 = [A.alloc("gbbc%d" % i, [128, 4, WG], F32) for i in range(2)]
        gT = A.alloc("ggT", [4, WG], F32)
        GT_2 = [A.alloc("gGT%d" % i, [4, WG], F32) for i in range(2)]
        GmT_2 = [A.alloc("gGmT%d" % i, [4, WG], F32) for i in range(2)]
        bT_2 = [A.alloc("gbT%d" % i, [4, WG], F32) for i in range(2)]
        zsg_2 = [A.alloc("gzsg%d" % i, [128, 4, WG], F32) for i in range(2)]
        tok = A.alloc("gtok", [128, 12], F32)
        etok = A.alloc("getok", [128, 8], F32)
        hb = A.alloc("ghb", [128, 4], F32)
        beg = A.alloc("gbeg", [128, 4], F32)
        D1 = A.alloc("gD1", [128, 4, 128], F32)
        AqkT = A.alloc("gAqkT", [128, 4, 128], BF16)
        T1 = A.alloc("gT1", [128, 4, 128], BF16)
        ATb = [A.alloc("gAT%d" % i, [128, 4, 128], BF16) for i in range(1)]
        Tn = [A.alloc("gTn%d" % i, [128, 4, 128], BF16) for i in range(2)]
        Vb = A.alloc("gVb", [128, 4, 128], BF16)
        PTb = [A.alloc("gPT%d" % i, [128, 4, 128], BF16) for i in range(2)]
        kbg = A.alloc("gkbg", [128, 4, 128], BF16)
        khat = A.alloc("gkhat", [128, 4, 128], BF16)
        vbt = A.alloc("gvbt", [128, 4, 128], BF16)
        nwT = A.alloc("gnwT", [128, 4, 128], BF16)
        vnew = A.alloc("gvnew", [128, 4, 128], BF16)
        S32 = A.alloc("gS32", [128, 4, 128], F32)
        Sbf = A.alloc("gSbf", [128, 4, 128], BF16)
        sq2 = A.alloc("gsq2", [128, 512], BF16)
        lnv2 = A.alloc("glnv2", [128, 512], F32)
        t1 = A.alloc("gt1", [128, 512], F32)
        ghalf = A.alloc("gghalf", [128, 1], F32)
        wcv = A.alloc("gwcv", [128, 4, 12], F32)
        P.copy("dve", wcv[:, :, :].rearrange("p j k -> p (j k)"), vc("w_conv", 0, 48))
        P.tt("dve", dgw[:, :, :], AP(ident_bf, 0, [[128, 128], [0, 48], [1, 128]]), AP(wcv, 0, [[48, 128], [1, 48], [0, 128]]), ALU.mult)
        P.ts("dve", ghalf[:, :], vc("g_gdn", 0, 1), 0.5, None, ALU.mult)
        P.memset("dve", S32[:, :, :], 0.0)
        P.memset("dve", Sbf[:, :, :], 0.0)
        P.memset("dve", cext[:, :, :], 0.0)
        nA = A.alloc("gnA", [4, 1], F32)
        P.act(nA[:, :], vec[0:4, VC["a_log"]:VC["a_log"] + 1], AF.Exp)
        P.ts("dve", nA[:, :], nA[:, :], -1.0, None, ALU.mult)
        dtb = vec[0:4, VC["dt_bias"]:VC["dt_bias"] + 1]
        sel = C("sel4", 4)
        pj = [0]

        def proj_fm(col0, tok0, W, M=128):
            ps = PS[pj[0] % 2]
            pj[0] += 1
            for k in range(8):
                P.mm(ps[0:M, 0:W], wgd[:, k, col0:col0 + M], xnT[:, k, tok0:tok0 + W], start=(k == 0), stop=(k == 7),
                     rk=[wkeys[k], xnT])
            return ps

        GSTS = [(WG * i, WG, 1, 128, False) for i in range(16)] + [(SEQ, 64, 1, 64, True)]
        def stageA(sti):
            (tok0, W, ntile, TT, is_s) = GSTS[sti]
            (qnT, knT, qhT, vTb, Gbc, eGbc, bbc, GT, GmT, bT, zsg) = [x[sti % 2] for x in (qnT_2, knT_2, qhT_2, vTb_2, Gbc_2, eGbc_2, bbc_2, GT_2, GmT_2, bT_2, zsg_2)]
            if is_s:
                cx = AP(cext, 0, [[12 * (3 + WG), 128], [3 + WG, 12], [7, NS], [1, 7]])
                mcr = A.mark()
                crow = A.alloc("gcrow", [48, 1536], F32)
                P.dma("sp", crow[:, :], st_conv_in.rearrange("i j c -> (i j) c"))
                for half in range(2):
                    for cc in range(6):
                        c = 6 * half + cc
                        P.tr(PS[half][:, cc * 48:(cc + 1) * 48], crow[0:48, c * 128:(c + 1) * 128], C("ident", 48)[:, 0:48])
                    P.copy("dve", cx[:, 6 * half:6 * half + 6, :, 0:3], PS[half][:, 0:288].rearrange("p (c i j) -> p c i j", c=6, i=NS))
                A.release(mcr)
            else:
                if tok0 > 0:
                    P.copy("dve", cext[:, :, 0:3], cext[:, :, WG:WG + 3])
            for c in range(12):
                ps = proj_fm(c * 128, tok0, W)
                if is_s:
                    P.copy("act", cx[:, c, :, 3:7], ps[:, 0:64].rearrange("p (i t) -> p i t", i=NS))
                else:
                    P.copy("act", cext[:, c, 3:3 + W], ps[:, 0:W])
                yield
            if is_s or tok0 + W == SEQ:
                mcv = A.mark()
                cvo = A.alloc("gcvo", [64, 1536], F32)
                rows = slice(tok0, tok0 + 64) if is_s else slice(SEQ - 3, SEQ)
                nr = 64 if is_s else 3
                for cb in range(3):
                    for k in range(8):
                        P.mm(PS[cb % 2][0:nr, :], xnT[:, k, rows], wgd[:, k, cb * 512:(cb + 1) * 512], start=(k == 0), stop=(k == 7),
                             rk=[wkeys[k], xnT])
                    P.copy("act", cvo[0:nr, cb * 512:(cb + 1) * 512], PS[cb % 2][0:nr, :])
                if is_s:
                    for j in range(3):
                        src = AP(cvo, (1 + j) * 1536, [[4 * 1536, NS], [1, 1536]])
                        P.dma("sp", o_conv_s[:, j, :], src)
                else:
                    P.dma("sp", o_conv_p, cvo[0:3, :])
                A.release(mcv)
            for g in range(3):
                bank = PS[g % 2]
                for cc in range(4):
                    c = 4 * g + cc
                    for j in range(4):
                        rhs = cx[:, c, :, j:j + 4] if is_s else cext[:, c, j:j + W]
                        P.mm(bank[:, cc * 128:cc * 128 + W], dgw[:, j * 12 + c, :], rhs, start=(j == 0), stop=(j == 3))
                bv = bank[:, :].rearrange("p (c t) -> p c t", c=4)[:, :, 0:W]
                P.act(th[:, 4 * g:4 * g + 4, 0:W], bv, AF.Tanh, scale=0.5)
                dst = cact[:, 4 * g:4 * g + 4, 0:W] if g < 2 else vTb[:, :, 0:W]
                P.stt("dve", dst, th[:, 4 * g:4 * g + 4, 0:W], 1.0, bv, ALU.add, ALU.mult)
            for h in range(4):
                ps = proj_fm(1536 + h * 128, tok0, W)
                P.act(zsg[:, h, 0:W], ps[:, 0:W], AF.Tanh, scale=0.5)
                P.stt("dve", zsg[:, h, 0:W], zsg[:, h, 0:W], 1.0, ps[:, 0:W], ALU.add, ALU.mult)
            P.ts("dve", zsg[:, :, 0:W], zsg[:, :, 0:W], ghalf[:, 0:1], None, ALU.mult)
            psb_ = proj_fm(2052, tok0, W, M=4)
            P.act(bT[:, 0:W], psb_[0:4, 0:W], AF.Tanh, scale=0.5)
            P.ts("dve", bT[:, 0:W], bT[:, 0:W], 0.5, 0.5, ALU.mult, ALU.add)
            yield
            psa = proj_fm(2048, tok0, W, M=4)
            P.act(gT[:, 0:W], psa[0:4, 0:W], AF.Exp, bias=dtb)
            P.act(gT[:, 0:W], gT[:, 0:W], AF.Ln, bias=1.0)
            P.ts("dve", gT[:, 0:W], gT[:, 0:W], nA[:, 0:1], None, ALU.mult)
            scanm = (C("scan4")[0:4, 0:W] if is_s else C("scan128")[0:4, 0:W])
            P.scan(GT[:, 0:W], scanm, gT[:, 0:W], 0.0, ALU.mult, ALU.add)
            if is_s:
                gcv = AP(GT, 3, [[WG, 4], [4, NS], [0, 4]])
                P.tt("dve", GmT[:, 0:64].rearrange("p (i t) -> p i t", i=NS), gcv, GT[:, 0:64].rearrange("p (i t) -> p i t", i=NS), ALU.subtract)
            else:
                gcv = AP(GT, 127, [[WG, 4], [128, ntile], [0, 128]])
                P.tt("dve", GmT[:, 0:W].rearrange("p (i t) -> p i t", i=ntile), gcv, GT[:, 0:W].rearrange("p (i t) -> p i t", i=ntile), ALU.subtract)
            for h in range(4):
                P.mm(PS[0][:, h * 128:h * 128 + W], sel[:, h * 128:(h + 1) * 128], GT[:, 0:W])
            for h in range(4):
                P.mm(PS[1][:, h * 128:h * 128 + W], sel[:, h * 128:(h + 1) * 128], bT[:, 0:W])
            gv = PS[0][:, :].rearrange("p (h t) -> p h t", h=4)[:, :, 0:W]
            P.copy("act", Gbc[:, :, 0:W], gv)
            P.act(eGbc[:, :, 0:W], gv, AF.Exp)
            P.copy("dve", bbc[:, :, 0:W], PS[1][:, :].rearrange("p (h t) -> p h t", h=4)[:, :, 0:W])
            yield
            P.act(sqb[:, :, 0:W], cact[:, 0:8, 0:W], AF.Square)
            for qk in range(2):
                for h in range(4):
                    P.mm(PS[qk][:, h * 128:h * 128 + W], ones_bf[:, :], sqb[:, 4 * qk + h, 0:W])
                P.act(lnv[:, 4 * qk:4 * qk + 4, 0:W], PS[qk][:, :].rearrange("p (h t) -> p h t", h=4)[:, :, 0:W], AF.Ln, bias=4.0 * EPS)
            P.act(lnv[:, :, 0:W], lnv[:, :, 0:W], AF.Exp, scale=-0.5)
            yield
            P.stt("dve", qnT[:, :, 0:W], cact[:, 0:4, 0:W], 128.0 ** -0.5, lnv[:, 0:4, 0:W], ALU.mult, ALU.mult)
            P.tt("dve", knT[:, :, 0:W], cact[:, 4:8, 0:W], lnv[:, 4:8, 0:W], ALU.mult)
            P.tt("dve", qhT[:, :, 0:W], qnT[:, :, 0:W], eGbc[:, :, 0:W], ALU.mult)
            yield

        def stageB(sti):
            (tok0, W, ntile, TT, is_s) = GSTS[sti]
            K = 2 if is_s else 7
            (qnT, knT, qhT, vTb, Gbc, eGbc, bbc, GT, GmT, bT, zsg) = [x[sti % 2] for x in (qnT_2, knT_2, qhT_2, vTb_2, Gbc_2, eGbc_2, bbc_2, GT_2, GmT_2, bT_2, zsg_2)]
            for ti in range(ntile):
                c0 = ti * TT
                t0 = tok0 + c0
                oT = PS[7]
                negm = C("negB_s", 64) if is_s else C("negG_p")
                nsm = C("nmaskBs_s", 64) if is_s else C("nmaskGs_p")
                P.tr(PS[2][0:TT, 0:4], GT[0:4, c0:c0 + TT], C("ident", 4)[:, 0:4])
                P.tr(PS[2][0:TT, 4:8], GmT[0:4, c0:c0 + TT], C("ident", 4)[:, 0:4])
                P.tr(PS[2][0:TT, 8:12], bT[0:4, c0:c0 + TT], C("ident", 4)[:, 0:4])
                P.copy("dve", tok[0:TT, :], PS[2][0:TT, 0:12])
                P.act(etok[0:TT, :], tok[0:TT, 0:8], AF.Exp)
                P.ts("dve", hb[0:TT, :], tok[0:TT, 8:12], 0.5, None, ALU.mult)
                P.tt("dve", beg[0:TT, :], tok[0:TT, 8:12], etok[0:TT, 0:4], ALU.mult)
                yield
                P.tt("pool", D1[0:TT, :, 0:TT], Gbc[0:TT, :, c0:c0 + TT], bc_free(tok[0:TT, 0:4], TT), ALU.subtract)
                negb = AP(negm.tensor, negm.offset, [list(negm.ap[0]), [0, 4], list(negm.ap[1])])
                P.tt("pool", D1[0:TT, :, 0:TT], D1[0:TT, :, 0:TT], negb, ALU.add)
                P.act(D1[0:TT, :, 0:TT], D1[0:TT, :, 0:TT], AF.Exp)
                nsb = AP(nsm.tensor, nsm.offset, [list(nsm.ap[0]), [0, 4], list(nsm.ap[1])])
                P.tt("pool", T1[0:TT, :, 0:TT], D1[0:TT, :, 0:TT], bbc[0:TT, :, c0:c0 + TT], ALU.mult)
                P.tt("pool", T1[0:TT, :, 0:TT], T1[0:TT, :, 0:TT], nsb, ALU.mult)
                yield
                for h in range(4):
                    P.mm(PS[2][0:TT, h * 128:h * 128 + TT], knT[:, h, c0:c0 + TT], knT[:, h, c0:c0 + TT], start=True, stop=True)
                for h in range(4):
                    P.mm(PS[3][0:TT, h * 128:h * 128 + TT], knT[:, h, c0:c0 + TT], qnT[:, h, c0:c0 + TT], start=True, stop=True)
                kkv = PS[2][0:TT, :].rearrange("p (h t) -> p h t", h=4)[:, :, 0:TT]
                qkv = PS[3][0:TT, :].rearrange("p (h t) -> p h t", h=4)[:, :, 0:TT]
                NT = ATb[0]
                P.tt("dve", NT[0:TT, :, 0:TT], kkv, T1[0:TT, :, 0:TT], ALU.mult)
                P.tt("dve", AqkT[0:TT, :, 0:TT], qkv, D1[0:TT, :, 0:TT], ALU.mult)
                yield
                for h in range(4):
                    P.tr(PSB[5][0:TT, h * 128:(h + 1) * 128], knT[:, h, c0:c0 + TT], ident_bf[:, :])
                for h in range(4):
                    P.act(kbg[0:TT, h, :], PSB[5][0:TT, h * 128:(h + 1) * 128], AF.Copy, scale=beg[0:TT, h:h + 1])
                    P.act(khat[0:TT, h, :], PSB[5][0:TT, h * 128:(h + 1) * 128], AF.Copy, scale=etok[0:TT, 4 + h:5 + h])
                for h in range(4):
                    P.tr(PSB[6][0:TT, h * 128:(h + 1) * 128], vTb[:, h, c0:c0 + TT], ident_bf[:, :])
                P.tt("dve", vbt[0:TT, :, :], PSB[6][0:TT, 0:512].rearrange("p (h d) -> p h d", h=4), bc_free(hb[0:TT, 0:4], 128), ALU.mult)
                yield
                idt = ident_bf[0:TT, 0:TT]

                def lmI(k):
                    return AP(lmk, k * 128, [[8 * 128, TT], [0, 2], [1, TT]])
                for hp in range(2):
                    for hh in range(2):
                        h = 2 * hp + hh
                        P.mm(PS[4 + hp][0:TT, hh * 128:hh * 128 + TT], NT[0:TT, h, 0:TT], idt, start=True, stop=False)
                        P.mm(PS[4 + hp][0:TT, hh * 128:hh * 128 + TT], idt, idt, start=False, stop=True)
                        P.mm(PS[6 + hp][0:TT, hh * 128:hh * 128 + TT], idt, NT[0:TT, h, 0:TT], start=True, stop=False)
                        P.mm(PS[6 + hp][0:TT, hh * 128:hh * 128 + TT], idt, idt, start=False, stop=True)
                for hp in range(2):
                    P.tt("dve", Tn[0][0:TT, 2 * hp:2 * hp + 2, 0:TT], PS[4 + hp][0:TT, 0:256].rearrange("p (h t) -> p h t", h=2)[:, :, 0:TT], lmI(0),
                         ALU.mult, wk=[KY(Tn[0], hp)])
                    P.tt("dve", PTb[0][0:TT, 2 * hp:2 * hp + 2, 0:TT], PS[6 + hp][0:TT, 0:256].rearrange("p (h t) -> p h t", h=2)[:, :, 0:TT], lmI(7),
                         ALU.mult, wk=[KY(PTb[0], hp)])
                yield
                cur = 0
                for k in range(1, K):
                    nxt = 1 - cur
                    for hp in range(2):
                        vb_ = PS[2 + hp]
                        for hh in range(2):
                            h = 2 * hp + hh
                            P.mm(vb_[0:TT, hh * 128:hh * 128 + TT], NT[0:TT, h, 0:TT], Tn[cur][0:TT, h, 0:TT], start=True, stop=False,
                                 rk=[NT, KY(Tn[cur], hp)])
                            P.mm(vb_[0:TT, hh * 128:hh * 128 + TT], ident_bf[0:TT, 0:TT], ident_bf[0:TT, 0:TT], start=False, stop=True)
                    for hp in range(2):
                        P.tt("dve", Vb[0:TT, 2 * hp:2 * hp + 2, 0:TT], PS[2 + hp][0:TT, 0:256].rearrange("p (h t) -> p h t", h=2)[:, :, 0:TT], lmI(k),
                             ALU.mult, wk=[KY(Vb, hp)])
                    for hp in range(2):
                        if k < K - 1:
                            for hh in range(2):
                                h = 2 * hp + hh
                                P.mm(PS[4 + hp][0:TT, hh * 128:hh * 128 + TT], PTb[cur][0:TT, h, 0:TT], Vb[0:TT, h, 0:TT], start=True, stop=True,
                                     rk=[KY(PTb[cur], hp), KY(Vb, hp)])
                        for hh in range(2):
                            h = 2 * hp + hh
                            P.mm(PS[6 + hp][0:TT, hh * 128:hh * 128 + TT], Vb[0:TT, h, 0:TT], PTb[cur][0:TT, h, 0:TT], start=True, stop=True,
                                 rk=[KY(PTb[cur], hp), KY(Vb, hp)])
                    for hp in range(2):
                        if k < K - 1:
                            P.copy("act", Tn[nxt][0:TT, 2 * hp:2 * hp + 2, 0:TT], PS[4 + hp][0:TT, 0:256].rearrange("p (h t) -> p h t", h=2)[:, :, 0:TT],
                                   wk=[KY(Tn[nxt], hp)])
                        P.copy("act", PTb[nxt][0:TT, 2 * hp:2 * hp + 2, 0:TT],
                               PS[6 + hp][0:TT, 0:256].rearrange("p (h t) -> p h t", h=2)[:, :, 0:TT], wk=[KY(PTb[nxt], hp)])
                    cur = nxt
                    yield
                PT = PTb[cur]
                for h in range(4):
                    P.mm(PS[4][:, h * 128:h * 128 + TT], kbg[0:TT, h, :], PT[0:TT, h, 0:TT], start=True, stop=True, rk=[kbg, KY(PT, h // 2)])
                P.act(nwT[:, :, 0:TT], PS[4][:, :].rearrange("p (h t) -> p h t", h=4)[:, :, 0:TT], AF.Copy, scale=-1.0)
                yield
                if not is_s:
                    for h in range(4):
                        P.mm(PS[5][0:TT, h * 128:(h + 1) * 128], PT[0:TT, h, 0:TT], vbt[0:TT, h, :], start=True, stop=False, rk=[vbt, KY(PT, h // 2)])
                        P.mm(PS[5][0:TT, h * 128:(h + 1) * 128], nwT[:, h, 0:TT], Sbf[:, h, :], start=False, stop=True)
                    P.copy("act", vnew[0:TT, :, :], PS[5][0:TT, :].rearrange("p (h v) -> p h v", h=4))
                    yield
                    for h in range(4):
                        P.mm(oT[:, h * TT:(h + 1) * TT], vnew[0:TT, h, :], AqkT[0:TT, h, 0:TT], start=True, stop=False)
                        P.mm(oT[:, h * TT:(h + 1) * TT], Sbf[:, h, :], qhT[:, h, c0:c0 + TT], start=False, stop=True)
                    for h in range(4):
                        P.mm(PS[6][:, h * 128:(h + 1) * 128], khat[0:TT, h, :], vnew[0:TT, h, :], start=True, stop=True)
                    egcv = AP(eGbc, c0 + TT - 1, [[4 * WG, 128], [WG, 4], [0, 128]])
                    P.tt("dve", S32[:, :, :], S32[:, :, :], egcv, ALU.mult)
                    P.tt("dve", Sbf[:, :, :], S32[:, :, :], PS[6][:, :].rearrange("p (h v) -> p h v", h=4), ALU.add)
                    P.tt("dve", S32[:, :, :], S32[:, :, :], PS[6][:, :].rearrange("p (h v) -> p h v", h=4), ALU.add)
                else:
                    for h in range(4):
                        P.mm(PS[5][0:TT, h * 128:(h + 1) * 128], PT[0:TT, h, 0:TT], vbt[0:TT, h, :], start=True, stop=True, rk=[vbt, KY(PT, h // 2)])
                    wS = A.alloc("gwS", [64, 4, 128], F32)
                    for quarter in range(4):
                        mS = A.mark()
                        i0 = 4 * quarter
                        S0 = A.alloc("gS0", [128, 4, 4, 128], F32)
                        S0bf = A.alloc("gS0bf", [128, 4, 4, 128], BF16)
                        nwm = A.alloc("gnwm", [128, 4, 4, 64], BF16)
                        vnm = A.alloc("gvnm", [64, 4, 4, 128], BF16)
                        P.dma("sp", S0[:, :, :, :], st_gdn_in[i0:i0 + 4].rearrange("i h d v -> d i h v"))
                        P.copy("act", S0bf[:, :, :, :], S0[:, :, :, :])
                        cmv = AP(cst, CONST_OFFS["cmneg"][0] + i0 * 64, [[NCONST, 128], [0, 4], [64, 4], [1, 64]])
                        nwv = AP(nwT, 0, [[4 * 128, 128], [128, 4], [0, 4], [1, 64]])
                        P.tt("dve", nwm[:, :, :, :], nwv, cmv, ALU.mult)
                        for h in range(4):
                            for ii in range(4):
                                i = i0 + ii
                                P.mm(PS[6][0:TT, h * 128:(h + 1) * 128], nwm[:, h, ii, :], S0bf[:, ii, h, :],
                                     start=(ii == 0), stop=(ii == 3))
                        if quarter == 0:
                            P.copy("dve", wS[0:TT, :, :], PS[6][0:TT, :].rearrange("p (h v) -> p h v", h=4))
                        else:
                            P.tt("dve", wS[0:TT, :, :], wS[0:TT, :, :], PS[6][0:TT, :].rearrange("p (h v) -> p h v", h=4), ALU.add)
                        A.release(mS)
                    P.tt("dve", vnew[0:TT, :, :], PS[5][0:TT, :].rearrange("p (h v) -> p h v", h=4), wS[0:TT, :, :], ALU.subtract)
                    for h in range(4):
                        P.mm(oT[:, h * TT:(h + 1) * TT], vnew[0:TT, h, :], AqkT[0:TT, h, 0:TT], start=(h == 0), stop=False)
                    for quarter in range(4):
                        mS = A.mark()
                        i0 = 4 * quarter
                        S0 = A.alloc("gS0b", [128, 4, 4, 128], F32)
                        S0bf = A.alloc("gS0bfb", [128, 4, 4, 128], BF16)
                        vnm = A.alloc("gvnm", [64, 4, 4, 128], BF16)
                        P.dma("sp", S0[:, :, :, :], st_gdn_in[i0:i0 + 4].rearrange("i h d v -> d i h v"))
                        P.copy("dve", S0bf[:, :, :, :], S0[:, :, :, :])
                        for h in range(4):
                            for ii in range(4):
                                i = i0 + ii
                                P.mm(oT[:, h * TT + 4 * i:h * TT + 4 * i + 4], S0bf[:, ii, h, :], qhT[:, h, 4 * i:4 * i + 4], start=False, stop=(quarter == 3 and h == 3 and ii == 3))
                        vnv = AP(vnew, 0, [[4 * 128, 64], [128, 4], [0, 4], [1, 128]])
                        bmv = AP(cst, CONST_OFFS["bm"][0] + i0, [[NCONST, 64], [0, 4], [1, 4], [0, 128]])
                        P.tt("dve", vnm[:, :, :, :], vnv, bmv, ALU.mult)
                        for ii in range(4):
                            i = i0 + ii
                            for h in range(4):
                                P.mm(PS[6][:, h * 128:(h + 1) * 128], khat[0:TT, h, :], vnm[:, h, ii, :], start=True, stop=True)
                            egc = AP(eGbc, 4 * i + 3, [[4 * WG, 128], [WG, 4], [0, 128]])
                            P.tt("dve", S0[:, ii, :, :], S0[:, ii, :, :], egc, ALU.mult)
                            P.tt("dve", S0[:, ii, :, :], S0[:, ii, :, :], PS[6][:, :].rearrange("p (h v) -> p h v", h=4), ALU.add)
                        P.dma("sp", o_gdn_s[i0:i0 + 4].rearrange("i h d v -> d i h v"), S0[:, :, :, :])
                        A.release(mS)
                n = 4 * TT
                P.act(sq2[:, 0:n], oT[:, 0:n], AF.Square)
                P.mm(PS[3][:, 0:n], ones_bf[:, :], sq2[:, 0:n])
                P.act(lnv2[:, 0:n], PS[3][:, 0:n], AF.Ln, scale=1.0 / 128, bias=EPS)
                P.act(lnv2[:, 0:n], lnv2[:, 0:n], AF.Exp, scale=-0.5)
                P.tt("dve", t1[:, 0:n], oT[:, 0:n], lnv2[:, 0:n], ALU.mult)
                P.tt("dve", ogdnT[:, :, t0:t0 + TT], t1[:, 0:n].rearrange("p (h t) -> p h t", h=4), zsg[:, :, c0:c0 + TT], ALU.mult)
            yield

        def drive(gens):
            alive = list(gens)
            while alive:
                for g in list(alive):
                    try:
                        next(g)
                    except StopIteration:
                        alive.remove(g)

        drive([stageA(0)])
        for sti in range(len(GSTS)):
            gs = [stageB(sti)]
            if sti + 1 < len(GSTS):
                gs.append(stageA(sti + 1))
            drive(gs)
        P.dma("sp", o_gdn_p.rearrange("h d v -> d h v"), S32[:, :, :])
        A.release(m0)

    phase_gdn()
    if debug:
        m0 = A.mark()
        dtmp = A.alloc("dbgtmp3", [128, 4, NTOK], F32)
        P.copy("dve", dtmp[:, :, :], ogdnT[:, :, :])
        P.dma("sp", dbg["ogdnT"], dtmp[:, :, :])
        A.release(m0)

    phase_hgrn()
    if debug:
        m0 = A.mark()
        dtmpx = A.alloc("dbgtmp2b", [128, 4, NTOK], F32)
        P.copy("dve", dtmpx[:, :, :], ohgT[:, :, :])
        P.dma("sp", dbg["ohgT"], dtmpx[:, :, :])
        A.release(m0)

    def phase_mem():
        A.off = OFF_MRG
        m0 = A.mark()
        memnT = A.alloc("memnT", [128, 8, MEM], BF16)
        norm_transpose(lambda ti: mem_prompt[128 * ti:128 * ti + 128, :], 2, lambda ti: 128, "g_mem", memnT, "m")
        wkv = A.alloc("wkv", [128, 8, 1024], BF16)
        w_kv_v = w_mem_kv.rearrange("(k p) c -> p k c", p=128)
        for k in range(8):
            P.dma("pool", wkv[:, k, :], w_kv_v[:, k, :], wk=[KY(wkv, k)])
        wq = A.alloc("wmq", [128, 8, 512], BF16)
        for k in range(8):
            P.dma("pool", wq[:, k, :], w_in_v[:, k, 4104:4616], wk=[KY(wq, k)])
        KT = A.alloc("mKT", [128, 4, MEM], BF16)
        Vsb = A.alloc("mVsb", [128, 2, 512], BF16)
        kvo = A.alloc("mkvo", [128, 2, 2, 512], F32)
        for h in range(4):
            for k in range(8):
                P.mm(PS[h % 2][:, 0:MEM], wkv[:, k, h * 128:(h + 1) * 128], memnT[:, k, :], start=(k == 0), stop=(k == 7),
                     rk=[KY(wkv, k), memnT])
            P.copy("act", KT[:, h, :], PS[h % 2][:, 0:MEM])
        for mt in range(2):
            for kv in range(2):
                ps = PS[2 + kv]
                for k in range(8):
                    P.mm(ps[:, :], memnT[:, k, mt * 128:(mt + 1) * 128], wkv[:, k, kv * 512:(kv + 1) * 512], start=(k == 0), stop=(k == 7),
                         rk=[KY(wkv, k), memnT])
                P.copy("act", kvo[:, kv, mt, :], ps[:, :])
                if kv == 1:
                    P.copy("dve", Vsb[:, mt, :], kvo[:, 1, mt, :])
        P.dma("sp", o_mk.rearrange("(mt p) c -> p mt c", p=128), kvo[:, 0, :, :])
        P.dma("sp", o_mv.rearrange("(mt p) c -> p mt c", p=128), kvo[:, 1, :, :])
        if stop == "mem1":
            return
        qT = A.alloc("mqT", [128, 2, 512], BF16)
        ET = A.alloc("mET", [128, 2, 512], BF16)
        rden = A.alloc("mrden", [128, 512], F32)
        qTs = A.alloc("mqTs", [128, 4, 64], BF16)
        cnt = [0]
        for (tok0, W, ntile, TT, is_s) in STS:
            for h in range(4):
                par = cnt[0] % 2
                cnt[0] += 1
                for k in range(8):
                    P.mm(PS[par][:, 0:W], wq[:, k, h * 128:(h + 1) * 128], xnT[:, k, tok0:tok0 + W], start=(k == 0), stop=(k == 7),
                         rk=[KY(wq, k), xnT])
                if is_s:
                    P.act(qTs[:, h, :], PS[par][:, 0:64], AF.Copy, scale=128.0 ** -0.5)
                    continue
                P.act(qT[:, par, :], PS[par][:, :], AF.Copy, scale=128.0 ** -0.5)
                for c in range(2):
                    P.mm(PS[2 + c][:, :], KT[:, h, c * 128:(c + 1) * 128], qT[:, par, :])
                    P.act(ET[:, c, :], PS[2 + c][:, :], AF.Exp)
                for c in range(2):
                    P.mm(PS[4 + par][:, :], Vsb[:, c, h * 128:(h + 1) * 128], ET[:, c, :], start=(c == 0), stop=(c == 1))
                for c in range(2):
                    P.mm(PS[6 + par][:, :], ones_bf[:, :], ET[:, c, :], start=(c == 0), stop=(c == 1))
                P.recip(rden[:, :], PS[6 + par][:, :])
                P.tt("dve", omemT[:, h, tok0:tok0 + W], PS[4 + par][:, :], rden[:, :], ALU.mult)
        if stop == "mem2":
            return
        ETs = A.alloc("mETs", [128, 2, 4, 64], BF16)
        ck_v = cache_k.rearrange("i (c p) f -> p i c f", p=128)
        cv_v = cache_v.rearrange("i (c p) f -> p i c f", p=128)
        first = [True]
        for quarter in range(4):
            mS = A.mark()
            i0 = 4 * quarter
            Kc = A.alloc("mKc", [128, 4, 2, 512], BF16)
            KcT = A.alloc("mKcT", [128, 4, 8, 128], BF16)
            P.dma("pool", Kc[:, :, :, :], ck_v[:, i0:i0 + 4, :, :])
            for ii in range(4):
                pb = PSB[ii % 2]
                for c in range(2):
                    for h in range(4):
                        P.tr(pb[:, (c * 4 + h) * 128:(c * 4 + h + 1) * 128], Kc[:, ii, c, h * 128:(h + 1) * 128], ident_bf[:, :])
                P.copy("act" if ii % 2 == 0 else "dve", KcT[:, ii, :, :], pb[:, 0:1024].rearrange("p (a m) -> p a m", a=8))
            for ii in range(4):
                i = i0 + ii
                for c in range(2):
                    for h in range(4):
                        col = (c * 4 + h) * 64 + 4 * i
                        P.mm(PS[2][:, col:col + 4], KcT[:, ii, c * 4 + h, :], qTs[:, h, 4 * i:4 * i + 4], start=first[0], stop=(i == 15 and c == 1 and h == 3))
                        first[0] = False
            A.release(mS)
        if stop == "mem3":
            return
        P.act(ETs[:, :, :, :].rearrange("p c h t -> p (c h t)"), PS[2][:, :], AF.Exp)
        first = [True]
        for quarter in range(4):
            mS = A.mark()
            i0 = 4 * quarter
            Vc = A.alloc("mVc", [128, 4, 2, 512], BF16)
            P.dma("pool", Vc[:, :, :, :], cv_v[:, i0:i0 + 4, :, :])
            for ii in range(4):
                i = i0 + ii
                for h in range(4):
                    for c in range(2):
                        P.mm(PS[3][:, h * 64 + 4 * i:h * 64 + 4 * i + 4], Vc[:, ii, c, h * 128:(h + 1) * 128], ETs[:, c, h, 4 * i:4 * i + 4],
                             start=first[0], stop=(i == 15 and c == 1 and h == 3))
                        first[0] = False
            A.release(mS)
        for c in range(2):
            P.mm(PS[4][:, 0:256], ones_bf[:, :], ETs[:, c, :, :].rearrange("p h t -> p (h t)"), start=(c == 0), stop=(c == 1))
        P.recip(rden[:, 0:256], PS[4][:, 0:256])
        P.tt("dve", omemT[:, :, SEQ:SEQ + 64], PS[3][:, 0:256].rearrange("p (h t) -> p h t", h=4), rden[:, 0:256].rearrange("p (h t) -> p h t", h=4), ALU.mult)
        A.release(m0)

    phase_mem()
    if stop in ("mem", "mem1", "mem2", "mem3"):
        P.lower()
        return nc
    if debug:
        m0 = A.mark()
        dtmp = A.alloc("dbgtmp4", [128, 4, NTOK], F32)
        P.copy("dve", dtmp[:, :, :], omemT[:, :, :])
        P.dma("sp", dbg["omemT"], dtmp[:, :, :])
        A.release(m0)

    mrgT = A.alloc_at("mrgT", [128, 8, NTOK], BF16, OFF_MRG)
    TGS = [(0, 512), (512, 512), (1024, 512), (1536, 512), (2048, 64)]

    def phase_merge():
        A.off = OFF_MRG_END
        m0 = A.mark()
        wbr = A.alloc("wbr", [128, 3, 4, D], BF16)
        for b, wsrc in enumerate((w_br_hg, w_br_gdn, w_br_mem)):
            P.dma("pool", wbr[:, b, :, :], wsrc.rearrange("(k p) c -> p k c", p=128), wk=[KY(wbr, b)])
        wg = [A.alloc("wgate%d" % i, [128, 8, 3, 128], BF16) for i in range(3)]
        thb = [A.alloc("mgth%d" % i, [128, 512], F32) for i in range(2)]
        acc = A.alloc("mgacc", [128, 512], F32)
        term = A.alloc("mgterm", [128, 512], F32)
        obs = (ohgT, ogdnT, omemT)
        wgv = w_in[:, 4616:7688].rearrange("(k p) (b f) -> p k b f", p=128, b=3)
        cnt = 0
        def load_gate_w(fc):
            wgb_ = wg[fc % 3]
            for b in range(3):
                P.dma("pool", wgb_[:, :, b, :], wgv[:, :, b, fc * 128:(fc + 1) * 128], wk=[KY(wgb_, b)])

        load_gate_w(0)
        load_gate_w(1)
        for fc in range(8):
            wgb = wg[fc % 3]
            if fc + 2 < 8:
                load_gate_w(fc + 2)
            for (t0, W) in TGS:
                for b in range(3):
                    par = cnt % 2
                    cnt += 1
                    gps = PS[par]
                    bps = PS[2 + par]
                    for k in range(8):
                        P.mm(gps[:, 0:W], wgb[:, k, b, :], xnT[:, k, t0:t0 + W], start=(k == 0), stop=(k == 7), rk=[KY(wgb, b), xnT])
                    P.act(thb[par][:, 0:W], gps[:, 0:W], AF.Tanh, scale=0.5)
                    for kc in range(4):
                        P.mm(bps[:, 0:W], wbr[:, b, kc, fc * 128:(fc + 1) * 128], obs[b][:, kc, t0:t0 + W], start=(kc == 0), stop=(kc == 3),
                             rk=[KY(wbr, b), obs[b]])
                    if b == 0:
                        P.stt("dve", acc[:, 0:W], thb[par][:, 0:W], 1.0, bps[:, 0:W], ALU.add, ALU.mult)
                    else:
                        P.stt("dve", term[:, 0:W], thb[par][:, 0:W], 1.0, bps[:, 0:W], ALU.add, ALU.mult)
                        P.tt("pool", acc[:, 0:W], acc[:, 0:W], term[:, 0:W], ALU.add)
                P.act(mrgT[:, fc, t0:t0 + W], acc[:, 0:W], AF.Copy, scale=0.5)
        A.release(m0)

    phase_merge()
    if stop == "merge":
        P.lower()
        return nc
    if debug:
        m0 = A.mark()
        dtmp = A.alloc("dbgtmp5", [128, 8, NTOK], F32)
        P.copy("dve", dtmp[:, :, :], mrgT[:, :, :])
        P.dma("sp", dbg["mrgT"], dtmp[:, :, :])
        A.release(m0)

    OFF_H = OFF_XNT
    OFF_H_END = OFF_H + 17 * 4096
    assert OFF_H_END <= OFF_MRG
    OFF_HNT = OFF_MRG_END
    OFF_HNT_END = OFF_HNT + 33792
    hall = A.alloc_at("hall", [128, 17, D], F32, OFF_H)
    hnT = A.alloc_at("hnT", [128, 8, NTOK], BF16, OFF_HNT)

    def tile_rows(ti):
        return 128 if ti < 16 else 64

    def phase_out():
        A.off = OFF_HNT_END
        m0 = A.mark()
        wo = A.alloc("wo", [128, 8, D], BF16)
        for k in range(8):
            P.dma("pool", wo[:, k, :], w_out.rearrange("(k p) c -> p k c", p=128)[:, k, :], wk=[KY(wo, k)])
        gpm = A.alloc("gpm", [128, D], F32)
        P.dma("sp", gpm[:, :], g_post_mix.partition_broadcast(128))
        xt = [A.alloc("oxt%d" % i, [128, D], F32) for i in range(2)]
        jk = A.alloc("ojk", [128, D], BF16)
        junk = A.alloc("ojunk", [128, 512], BF16)
        hs = [A.alloc("ohs%d" % i, [128, D], BF16) for i in range(2)]
        stat = A.alloc("ostat", [128, 100], F32)
        P.memset("dve", stat[:, :], 0.0)
        for ti in range(17):
            rows = tile_rows(ti)
            tk0 = 128 * ti
            for half in range(2):
                ps = PS[(2 * ti + half) % 4]
                for k in range(8):
                    P.mm(ps[0:rows, :], mrgT[:, k, tk0:tk0 + rows], wo[:, k, half * 512:(half + 1) * 512], start=(k == 0), stop=(k == 7),
                         rk=[mrgT, KY(wo, k)])
                P.act(junk[0:rows, :], ps[0:rows, :], AF.Square, accum_out=stat[0:rows, 2 * ti + half:2 * ti + half + 1])
                P.copy("dve", hall[0:rows, ti, half * 512:(half + 1) * 512], ps[0:rows, :], wk=[KY(hall, ti)])
        sv = stat[:, 0:34].rearrange("p (t h) -> p t h", h=2)
        P.tt("dve", stat[:, 40:57], sv[:, :, 0], sv[:, :, 1], ALU.add)
        P.act(stat[:, 40:57], stat[:, 40:57], AF.Ln, scale=1.0 / D, bias=EPS)
        P.act(stat[:, 40:57], stat[:, 40:57], AF.Exp, scale=-0.5)
        for ti in range(17):
            rows = tile_rows(ti)
            P.dma("sp", xt[ti % 2][0:rows, :], x_rows(ti))
            P.stt("dve", hall[0:rows, ti, :], hall[0:rows, ti, :], stat[0:rows, 40 + ti:41 + ti], gpm[0:rows, :], ALU.mult, ALU.mult,
                  rk=[KY(hall, ti), stat, gpm], wk=[KY(hall, ti)])
            P.tt("pool", hall[0:rows, ti, :], hall[0:rows, ti, :], xt[ti % 2][0:rows, :], ALU.add, rk=[KY(hall, ti), xt[ti % 2]], wk=[KY(hall, ti), hall])
            P.act(jk[0:rows, :], hall[0:rows, ti, :], AF.Square, accum_out=stat[0:rows, 60 + ti:61 + ti], rk=[KY(hall, ti)], wk=[jk, KY(stat, "b")])
        P.act(stat[:, 80:97], stat[:, 60:77], AF.Ln, scale=1.0 / D, bias=EPS, rk=[KY(stat, "b"), stat], wk=[KY(stat, "c")])
        P.act(stat[:, 80:97], stat[:, 80:97], AF.Exp, scale=-0.5, rk=[KY(stat, "c")], wk=[KY(stat, "c")])
        for ti in range(17):
            rows = tile_rows(ti)
            tk0 = 128 * ti
            hsb = hs[ti % 2]
            P.act(hsb[0:rows, :], hall[0:rows, ti, :], AF.Copy, scale=stat[0:rows, 80 + ti:81 + ti], rk=[KY(hall, ti), KY(stat, "c")])
            pb = PSB[4 + ti % 2]
            for k in range(8):
                P.tr(pb[:, k * 128:k * 128 + rows], hsb[0:rows, k * 128:(k + 1) * 128], ident_bf[0:rows, 0:rows])
            src = pb[:, 0:1024].rearrange("p (k t) -> p k t", k=8)[:, :, 0:rows]
            P.tt("dve", hnT[:, :, tk0:tk0 + rows], src, bc_free(vc("g_pre_ffn", 0, 8), rows), ALU.mult)
        A.release(m0)

    phase_out()
    if stop == "out":
        P.lower()
        return nc
    if debug:
        P.dma("sp", dbg["h"][0:SEQ, :].rearrange("(t p) d -> p t d", p=128), hall[:, 0:16, :])
        P.dma("sp", dbg["h"][SEQ:NTOK, :], hall[0:64, 16, :])

    def phase_ffn():
        wfo = A.alloc_at("wfo", [128, 22, D], BF16, OFF_H_END)
        assert OFF_H_END + 22 * D * 2 <= OFF_HNT
        for j in range(22):
            P.dma("pool", wfo[:, j, :], w_ffn_out[128 * j:128 * j + 128, :], wk=[KY(wfo, j)])
        A.off = OFF_HNT_END
        actT = A.alloc("actT", [128, 22, 576], BF16)
        wfi = [A.alloc("wfi%d" % i, [128, 8, 2, 128], BF16) for i in range(3)]
        s2 = A.alloc("fs2", [128, 512], F32)
        A.off = SB_BASE
        gpf = A.alloc("gpf", [128, D], F32)
        P.dma("sp", gpf[:, :], g_post_ffn.partition_broadcast(128))
        thf_ = A.alloc("fth", [128, 512], F32)
        yt = [A.alloc("fyt%d" % i, [128, D], F32) for i in range(2)]
        junk = A.alloc("fjunk", [128, 512], BF16)
        stat = A.alloc("fstat", [128, 8], F32)
        wfv = w_ffn_in.rearrange("(k p) (u f) -> p k u f", p=128, u=2)
        PASSES = [(0, 512), (512, 512), (1024, 512), (1536, 576)]
        nblk = 0
        ycnt = 0
        for (p0, PW) in PASSES:
            subs = [(0, 512)] if PW == 512 else [(0, 512), (512, 64)]
            for j in range(22):
                wb = wfi[nblk % 3]
                nblk += 1
                for u in range(2):
                    P.dma("pool", wb[:, :, u, :], wfv[:, :, u, 128 * j:128 * j + 128], wk=[KY(wb, u)])
                for (s0, SW) in subs:
                    par = (nblk + (s0 > 0)) % 2
                    gps = PS[2 * par]
                    ups = PS[2 * par + 1]
                    for k in range(8):
                        P.mm(gps[:, 0:SW], wb[:, k, 0, :], hnT[:, k, p0 + s0:p0 + s0 + SW], start=(k == 0), stop=(k == 7), rk=[KY(wb, 0), hnT])
                    for k in range(8):
                        P.mm(ups[:, 0:SW], wb[:, k, 1, :], hnT[:, k, p0 + s0:p0 + s0 + SW], start=(k == 0), stop=(k == 7), rk=[KY(wb, 1), hnT])
                    P.act(thf_[:, 0:SW], gps[:, 0:SW], AF.Tanh, scale=0.5)
                    P.stt("dve", s2[:, 0:SW], thf_[:, 0:SW], 1.0, gps[:, 0:SW], ALU.add, ALU.mult)
                    P.stt("dve", actT[:, j, s0:s0 + SW], s2[:, 0:SW], 0.5, ups[:, 0:SW], ALU.mult, ALU.mult)
            ntl = PW // 128 + (1 if PW % 128 else 0)
            for tl in range(ntl):
                ti = p0 // 128 + tl
                rows = tile_rows(ti)
                c0 = 128 * tl
                yb = yt[ycnt % 2]
                ycnt += 1
                P.memset("dve", stat[:, :], 0.0)
                for half in range(2):
                    ps = PS[4 + 2 * (tl % 2) + half]
                    for j in range(22):
                        P.mm(ps[0:rows, :], actT[:, j, c0:c0 + rows], wfo[:, j, half * 512:(half + 1) * 512], start=(j == 0), stop=(j == 21),
                             rk=[actT, KY(wfo, j)])
                    P.act(junk[0:rows, :], ps[0:rows, :], AF.Square, accum_out=stat[0:rows, half:half + 1])
                P.tt("dve", stat[0:rows, 2:3], stat[0:rows, 0:1], stat[0:rows, 1:2], ALU.add)
                P.act(stat[0:rows, 3:4], stat[0:rows, 2:3], AF.Ln, scale=1.0 / D, bias=EPS)
                P.act(stat[0:rows, 3:4], stat[0:rows, 3:4], AF.Exp, scale=-0.5)
                for half in range(2):
                    ps = PS[4 + 2 * (tl % 2) + half]
                    P.stt("dve", yb[0:rows, half * 512:(half + 1) * 512], ps[0:rows, :], stat[0:rows, 3:4], gpf[0:rows, half * 512:(half + 1) * 512],
                          ALU.mult, ALU.mult)
                P.tt("pool", yb[0:rows, :], yb[0:rows, :], hall[0:rows, ti, :], ALU.add)
                if ti < 16:
                    P.dma("sp", y_prompt[128 * ti:128 * ti + 128, :], yb[:, :])
                else:
                    P.dma("sp", y_sample[:, :], yb[0:64, :])

    phase_ffn()

    P.lower()
    return nc


_CACHE = {}


def _get_prog(debug=False):
    if debug not in _CACHE:
        _CACHE[debug] = build_program(debug)
    return _CACHE[debug]


def kernel(**inputs):
    return run(inputs, debug=False)


def run(inputs, debug=False):
    nc = _get_prog(debug)
    f = lambda a: np.ascontiguousarray(np.asarray(a, dtype=np.float32))
    in_maps = []
    for c in range(NCORES):
        sl = slice(NS * c, NS * c + NS)
        m = {
            "x_prompt": f(inputs["x_prompt"][c]),
            "x_sample": f(inputs["x_sample"][sl]).reshape(NS * TS, D),
            "mem_prompt": f(inputs["mem_prompt"][c]),
            "cache_mem_k": f(inputs["cache_mem_k"][0, sl]).reshape(NS, MEM, 512),
            "cache_mem_v": f(inputs["cache_mem_v"][0, sl]).reshape(NS, MEM, 512),
            "state_hgrn": f(inputs["state_hgrn"][0, sl]),
            "state_gdn": f(inputs["state_gdn"][0, sl]),
            "state_gdn_conv": f(inputs["state_gdn_conv"][0, sl]),
            "hg_lb_logits": f(inputs["hg_lb_logits"]),
            "g_pre_mix": f(inputs["g_pre_mix"][0]),
            "w_in": f(inputs["w_in"][0]),
            "w_conv": f(inputs["w_conv"][0]),
            "a_log": f(inputs["a_log"][0]),
            "dt_bias": f(inputs["dt_bias"][0]),
            "g_hg_out": f(inputs["g_hg_out"][0]),
            "g_gdn_out": f(inputs["g_gdn_out"][0]),
            "g_mem": f(inputs["g_mem"][0]),
            "w_mem_kv": f(inputs["w_mem_kv"][0]),
            "w_br_hg": f(inputs["w_br_hg"][0]),
            "w_br_gdn": f(inputs["w_br_gdn"][0]),
            "w_br_mem": f(inputs["w_br_mem"][0]),
            "w_out": f(inputs["w_out"][0]),
            "g_post_mix": f(inputs["g_post_mix"][0]),
            "g_pre_ffn": f(inputs["g_pre_ffn"][0]),
            "w_ffn_in": f(inputs["w_ffn_in"][0]),
            "w_ffn_out": f(inputs["w_ffn_out"][0]),
            "g_post_ffn": f(inputs["g_post_ffn"][0]),
            "consts": CONST_ARR,
            "lmasks": LMASK_ARR,
        }
        in_maps.append(m)
    res = run_bass_kernel_spmd(nc, in_maps, core_ids=list(range(NCORES)))
    R = res.results
    if debug:
        return R
    yp = np.stack([R[c]["y_prompt"] for c in range(NCORES)], 0)
    ys = np.concatenate([R[c]["y_sample"].reshape(NS, TS, D) for c in range(NCORES)], 0)
    mk = np.stack([R[c]["new_mem_k"].reshape(MEM, 4, 128) for c in range(NCORES)], 0)[None]
    mv = np.stack([R[c]["new_mem_v"].reshape(MEM, 4, 128) for c in range(NCORES)], 0)[None]
    hgp = np.stack([R[c]["new_hg_p"] for c in range(NCORES)], 0)[None]
    gdp = np.stack([R[c]["new_gdn_p"] for c in range(NCORES)], 0)[None]
    cvp = np.stack([R[c]["new_conv_p"] for c in range(NCORES)], 0)[None]
    hgs = np.concatenate([R[c]["new_hg_s"] for c in range(NCORES)], 0)[None]
    gds = np.concatenate([R[c]["new_gdn_s"] for c in range(NCORES)], 0)[None]
    cvs = np.concatenate([R[c]["new_conv_s"] for c in range(NCORES)], 0)[None]
    return (yp, ys, mk, mv, hgp, gdp, cvp, hgs, gds, cvs)
```

```python
import os
import numpy as np
import ml_dtypes
import concourse.bass as bass
import concourse.mybir as mybir
from concourse.ap import AP
from concourse.bass_utils import run_bass_kernel_spmd

F32 = mybir.dt.float32
BF16 = mybir.dt.bfloat16
AF = mybir.ActivationFunctionType
ALU = mybir.AluOpType

NCORES = 8
D = 1024
SEQ = 2048
NS = 16
TS = 4
NTOK = SEQ + NS * TS
MEM = 256
FFH = 2816
INW = 7688
EPS = 1e-6
SB_BASE = 16512
SB_TOP = 229344


class Op:
    __slots__ = ("eng", "idx", "build", "deps", "dma", "sem", "semval", "mark")


class Prog:
    ENGS = ("pe", "act", "dve", "pool", "sp")

    def __init__(self, nc):
        self.nc = nc
        self.streams = {e: [] for e in self.ENGS}
        self.last_w = {}
        self.readers = {}
        self.alias = {}
        self.keys_of = {}

    @staticmethod
    def _keys(aps):
        out = []
        for a in aps:
            if a is None or isinstance(a, (int, float)):
                continue
            if isinstance(a, (str, tuple)):
                out.append(a)
            else:
                out.append(getattr(a, "tensor", a).name)
        return out

    def emit(self, eng, build, reads, writes, dma=False):
        op = Op()
        op.eng = eng
        op.build = build
        op.dma = dma
        op.mark = False
        op.sem = None
        op.semval = 0
        st = self.streams[eng]
        op.idx = len(st)
        st.append(op)
        rk = self._keys(reads)
        wk = self._keys(writes)
        deps = {}

        def add(d, kind):
            if d is None or d is op:
                return
            if (not d.dma) and d.eng == eng and not dma:
                if eng == "pe":
                    return
            deps[id(d)] = d

        for k in rk + wk:
            base = k if isinstance(k, str) else k[0]
            self.keys_of.setdefault(base, set()).add(k)
        for k in rk:
            add(self.last_w.get(k), "raw")
            base = k if isinstance(k, str) else k[0]
            if isinstance(base, str) and base.startswith("psb"):
                for r in self.readers.get(k, ()):
                    if r.eng != eng:
                        add(r, "rar")
        for k in wk:
            add(self.last_w.get(k), "waw")
            for r in self.readers.get(k, ()):
                add(r, "war")
            base = k if isinstance(k, str) else k[0]
            for o in self.alias.get(base, ()):
                for kk in self.keys_of.get(o, ()):
                    add(self.last_w.get(kk), "waw")
                    for r in self.readers.get(kk, ()):
                        add(r, "war")
        op.deps = list(deps.values())
        for k in rk:
            lst = self.readers.setdefault(k, [])
            if not dma:
                lst[:] = [r for r in lst if r.dma or r.eng != eng]
            lst.append(op)
        for k in wk:
            self.last_w[k] = op
            self.readers[k] = []
        return op

    def lower(self):
        nc = self.nc
        eng_sem = {e: nc.alloc_semaphore("cs_" + e) for e in self.ENGS}
        ndma = {"sp": 14, "pool": 14, "act": 4, "pe": 1, "dve": 1}
        dma_sems = {e: [nc.alloc_semaphore("ds_%s%d" % (e, i)) for i in range(ndma[e])] for e in ("sp", "pool", "act")}
        for e in self.ENGS:
            for op in self.streams[e]:
                for d in op.deps:
                    if not d.dma:
                        d.mark = True
        finals = {}
        for e in self.ENGS:
            cnt = 0
            m = 0
            for op in self.streams[e]:
                if op.dma:
                    sems = dma_sems[e]
                    op.sem = sems[m % len(sems)]
                    op.semval = 16 * (m // len(sems) + 1)
                    finals[op.sem] = op.semval
                    m += 1
                else:
                    if op.mark:
                        cnt += 1
                    op.sem = eng_sem[e]
                    op.semval = cnt
        streams = self.streams

        def run(name, h):
            waited = {}
            for op in streams[name]:
                waits = {}
                for d in op.deps:
                    if waits.get(d.sem, 0) < d.semval:
                        waits[d.sem] = d.semval
                if op.dma and op.semval > 16:
                    if waits.get(op.sem, 0) < op.semval - 16:
                        waits[op.sem] = op.semval - 16
                for sem, val in waits.items():
                    if waited.get(sem, 0) >= val:
                        continue
                    h.wait_ge(sem, val)
                    waited[sem] = val
                ins = op.build(h)
                if op.dma:
                    ins.then_inc(op.sem, 16)
                elif op.mark:
                    ins.then_inc(op.sem, 1)
            if name == "sp":
                for sem, val in finals.items():
                    if waited.get(sem, 0) < val:
                        h.wait_ge(sem, val)

        with nc.Block() as block:
            @block.tensor
            def _(h):
                run("pe", h)

            @block.scalar
            def _(h):
                run("act", h)

            @block.vector
            def _(h):
                run("dve", h)

            @block.gpsimd
            def _(h):
                run("pool", h)

            @block.sync
            def _(h):
                run("sp", h)

    def dma(self, eng, out, in_, rk=None, wk=None, nc_ok=False):
        nc = self.nc

        def b(h):
            if nc_ok:
                with nc.allow_non_contiguous_dma(reason="small strided constant load"):
                    return h.dma_start(out=out, in_=in_)
            return h.dma_start(out=out, in_=in_)
        return self.emit(eng, b, rk if rk is not None else [in_], wk if wk is not None else [out], dma=True)

    def mm(self, out, lhsT, rhs, start=True, stop=True, rk=None, wk=None):
        return self.emit("pe", lambda h: h.matmul(out, lhsT=lhsT, rhs=rhs, start=start, stop=stop),
                         rk if rk is not None else [lhsT, rhs], wk if wk is not None else [out])

    def tr(self, out, in_, ident, rk=None, wk=None):
        return self.emit("pe", lambda h: h.transpose(out=out, in_=in_, identity=ident),
                         rk if rk is not None else [in_, ident], wk if wk is not None else [out])

    def act(self, out, in_, func, scale=None, bias=None, accum_out=None, rk=None, wk=None):
        kw = {}
        if scale is not None:
            kw["scale"] = scale
        if bias is not None:
            kw["bias"] = bias
        if accum_out is not None:
            kw["accum_out"] = accum_out
        rd = [in_]
        if isinstance(scale, AP):
            rd.append(scale)
        if isinstance(bias, AP):
            rd.append(bias)
        return self.emit("act", lambda h: h.activation(out=out, in_=in_, func=func, **kw),
                         rk if rk is not None else rd, wk if wk is not None else [out, accum_out])

    def tt(self, eng, out, in0, in1, op, rk=None, wk=None):
        return self.emit(eng, lambda h: h.tensor_tensor(out=out, in0=in0, in1=in1, op=op),
                         rk if rk is not None else [in0, in1], wk if wk is not None else [out])

    def ts(self, eng, out, in0, s1, s2, op0, op1=None, rk=None, wk=None):
        rd = [in0]
        if isinstance(s1, AP):
            rd.append(s1)
        if isinstance(s2, AP):
            rd.append(s2)

        def b(h):
            if op1 is None:
                return h.tensor_scalar(out=out, in0=in0, scalar1=s1, scalar2=None, op0=op0)
            return h.tensor_scalar(out=out, in0=in0, scalar1=s1, scalar2=s2, op0=op0, op1=op1)
        return self.emit(eng, b, rk if rk is not None else rd, wk if wk is not None else [out])

    def stt(self, eng, out, in0, scalar, in1, op0, op1, rk=None, wk=None):
        rd = [in0, in1]
        if isinstance(scalar, AP):
            rd.append(scalar)
        return self.emit(eng, lambda h: h.scalar_tensor_tensor(out=out, in0=in0, scalar=scalar, in1=in1, op0=op0, op1=op1),
                         rk if rk is not None else rd, wk if wk is not None else [out])

    def copy(self, eng, out, in_, rk=None, wk=None):
        if eng == "act":
            return self.act(out, in_, AF.Copy, rk=rk, wk=wk)
        return self.emit(eng, lambda h: h.tensor_copy(out=out, in_=in_),
                         rk if rk is not None else [in_], wk if wk is not None else [out])

    def memset(self, eng, out, val, wk=None):
        return self.emit(eng, lambda h: h.memset(out, val), [], wk if wk is not None else [out])

    def recip(self, out, in_, rk=None, wk=None):
        return self.emit("dve", lambda h: h.reciprocal(out=out, in_=in_),
                         rk if rk is not None else [in_], wk if wk is not None else [out])

    def scan(self, out, d0, d1, init, op0, op1, rk=None, wk=None):
        return self.emit("dve", lambda h: h.tensor_tensor_scan(out=out, data0=d0, data1=d1, initial=init, op0=op0, op1=op1),
                         rk if rk is not None else [d0, d1], wk if wk is not None else [out])


class Arena:
    def __init__(self, nc, prog):
        self.nc = nc
        self.prog = prog
        self.off = SB_BASE
        self.n = 0
        self.peak = SB_BASE
        self.ranges = []

    def _register(self, t, off, per):
        al = self.prog.alias
        for (nm, o, e) in self.ranges:
            if o < off + per and off < e:
                al.setdefault(t.name, set()).add(nm)
                al.setdefault(nm, set()).add(t.name)
        self.ranges.append((t.name, off, off + per))

    def alloc(self, name, shape, dtype):
        esz = 2 if dtype == BF16 else 4
        per = esz
        for s in shape[1:]:
            per *= s
        off = (self.off + 31) // 32 * 32
        assert off + per <= SB_TOP, "SBUF overflow allocating %s (%d bytes at %d)" % (name, per, off)
        self.n += 1
        t = self.nc.alloc_sbuf_tensor_at("%s_%d" % (name, self.n), list(shape), dtype, offset=off)
        self._register(t, off, per)
        self.off = off + per
        self.peak = max(self.peak, self.off)
        return t

    def alloc_at(self, name, shape, dtype, off):
        esz = 2 if dtype == BF16 else 4
        per = esz
        for x in shape[1:]:
            per *= x
        assert off % 32 == 0 and off >= SB_BASE and off + per <= SB_TOP, "bad alloc_at %s" % name
        self.n += 1
        t = self.nc.alloc_sbuf_tensor_at("%s_%d" % (name, self.n), list(shape), dtype, offset=off)
        self._register(t, off, per)
        return t

    def mark(self):
        return self.off

    def release(self, m):
        self.off = m


def interleave(gens, width, stagger=0):
    gens = list(gens)
    live = []
    nxt = 0
    while live or nxt < len(gens):
        while len(live) < width and nxt < len(gens) and (not live or live[-1][1] >= stagger):
            live.append([gens[nxt], 0])
            nxt += 1
        for e in list(live):
            try:
                next(e[0])
                e[1] += 1
            except StopIteration:
                live.remove(e)


def bc_free(t_ap, n):
    ap = [list(x) for x in t_ap.ap]
    if len(ap) >= 2 and ap[-1][1] == 1:
        ap = ap[:-1]
    return AP(t_ap.tensor, t_ap.offset, ap + [[0, n]])


def make_consts():
    c = {}
    s = np.arange(128)[:, None]
    t = np.arange(128)[None, :]
    c["maskH_p"] = ((s <= t) & (s // 64 == t // 64)).astype(np.float32)
    s6 = np.arange(64)[:, None]
    t6 = np.arange(64)[None, :]
    m_s = ((s6 <= t6) & (s6 // 4 == t6 // 4)).astype(np.float32)
    c["maskB_s"] = np.zeros((128, 64), np.float32)
    c["maskB_s"][:64] = m_s
    c["maskG_p"] = (s <= t).astype(np.float32)
    c["maskGs_p"] = (s < t).astype(np.float32)
    ms = ((s6 < t6) & (s6 // 4 == t6 // 4)).astype(np.float32)
    c["maskBs_s"] = np.zeros((128, 64), np.float32)
    c["maskBs_s"][:64] = ms
    c["nmaskGs_p"] = -c["maskGs_p"]
    c["nmaskBs_s"] = -c["maskBs_s"]
    c["negG_p"] = np.where(s <= t, 0.0, -30000.0).astype(np.float32)
    ng = np.where((s6 <= t6) & (s6 // 4 == t6 // 4), 0.0, -30000.0).astype(np.float32)
    c["negB_s"] = np.zeros((128, 64), np.float32)
    c["negB_s"][:64] = ng
    tt = np.arange(512)[None, :]
    c["cmneg"] = np.zeros((128, 16 * 64), np.float32)
    cmn = -(np.arange(64)[None, :] // 4 == np.arange(16)[:, None]).astype(np.float32)
    c["cmneg"][:] = cmn.reshape(1, 1024)
    c["scan64"] = np.broadcast_to((tt % 64 != 0).astype(np.float32), (128, 512)).copy()
    c["scan128"] = np.broadcast_to((tt % 128 != 0).astype(np.float32), (128, 512)).copy()
    c["scan4"] = np.broadcast_to((np.arange(64)[None, :] % 4 != 0).astype(np.float32), (128, 64)).copy()
    bm = (np.arange(128)[:, None] // 4 == np.arange(16)[None, :]).astype(np.float32)
    c["bm"] = bm
    sel = np.zeros((128, 4 * 128), np.float32)
    for h in range(4):
        sel[h, h * 128:(h + 1) * 128] = 1.0
    c["sel4"] = sel
    c["ident"] = np.eye(128, dtype=np.float32)
    c["ones"] = np.ones((128, 128), np.float32)
    names = list(c.keys())
    offs = {}
    o = 0
    for k in names:
        offs[k] = (o, c[k].shape[1])
        o += c[k].shape[1]
    arr = np.concatenate([c[k] for k in names], axis=1).astype(np.float32)
    return arr, offs


CONST_ARR, CONST_OFFS = make_consts()
NCONST = CONST_ARR.shape[1]


def make_level_masks():
    t = np.arange(128)[:, None]
    u = np.arange(128)[None, :]
    ms = []
    for k in range(7):
        m = ((t >> (k + 1)) == (u >> (k + 1))) & ((t >> k) != (u >> k)) & (t > u)
        ms.append(m.astype(np.float32) + np.eye(128, dtype=np.float32))
    ms.append(ms[0].T.copy())
    return np.concatenate(ms, axis=1).astype(ml_dtypes.bfloat16)


LMASK_ARR = make_level_masks()


def build_program(debug=False):
    nc = bass.Bass("TRN2", target_bir_lowering=False)
    stop = os.environ.get("KSTOP", "") if debug else ""
    P = Prog(nc)
    A = Arena(nc, P)

    def din(name, shape, dt=F32):
        return nc.dram_tensor(name, list(shape), dt, kind="ExternalInput").ap()

    def dout(name, shape, dt=F32):
        return nc.dram_tensor(name, list(shape), dt, kind="ExternalOutput").ap()

    x_prompt = din("x_prompt", [SEQ, D])
    x_sample = din("x_sample", [NS * TS, D])
    mem_prompt = din("mem_prompt", [MEM, D])
    cache_k = din("cache_mem_k", [NS, MEM, 512])
    cache_v = din("cache_mem_v", [NS, MEM, 512])
    st_hg_in = din("state_hgrn", [NS, 4, 128, 128])
    st_gdn_in = din("state_gdn", [NS, 4, 128, 128])
    st_conv_in = din("state_gdn_conv", [NS, 3, 1536])
    lb_logits = din("hg_lb_logits", [2, 512])
    g_pre_mix = din("g_pre_mix", [D])
    w_in = din("w_in", [D, INW])
    w_conv = din("w_conv", [4, 1536])
    a_log = din("a_log", [4])
    dt_bias = din("dt_bias", [4])
    g_hg_out = din("g_hg_out", [512])
    g_gdn_out = din("g_gdn_out", [128])
    g_mem = din("g_mem", [D])
    w_mem_kv = din("w_mem_kv", [D, 1024])
    w_br_hg = din("w_br_hg", [512, D])
    w_br_gdn = din("w_br_gdn", [512, D])
    w_br_mem = din("w_br_mem", [512, D])
    w_out = din("w_out", [D, D])
    g_post_mix = din("g_post_mix", [D])
    g_pre_ffn = din("g_pre_ffn", [D])
    w_ffn_in = din("w_ffn_in", [D, 2 * FFH])
    w_ffn_out = din("w_ffn_out", [FFH, D])
    g_post_ffn = din("g_post_ffn", [D])
    consts_d = din("consts", [128, NCONST])
    lmask_d = din("lmasks", [128, 1024], BF16)

    y_prompt = dout("y_prompt", [SEQ, D])
    y_sample = dout("y_sample", [NS * TS, D])
    o_mk = dout("new_mem_k", [MEM, 512])
    o_mv = dout("new_mem_v", [MEM, 512])
    o_hg_p = dout("new_hg_p", [4, 128, 128])
    o_gdn_p = dout("new_gdn_p", [4, 128, 128])
    o_conv_p = dout("new_conv_p", [3, 1536])
    o_hg_s = dout("new_hg_s", [NS, 4, 128, 128])
    o_gdn_s = dout("new_gdn_s", [NS, 4, 128, 128])
    o_conv_s = dout("new_conv_s", [NS, 3, 1536])
    dbg = {}
    if debug:
        dbg["ohgT"] = dout("dbg_ohgT", [128, 4, NTOK])
        dbg["xnT"] = dout("dbg_xnT", [128, 8, NTOK])
        dbg["ogdnT"] = dout("dbg_ogdnT", [128, 4, NTOK])
        dbg["omemT"] = dout("dbg_omemT", [128, 4, NTOK])
        dbg["mrgT"] = dout("dbg_mrgT", [128, 8, NTOK])
        dbg["h"] = dout("dbg_h", [NTOK, D])

    PS = [nc.alloc_psum_tensor("psb%d" % i, [128, 512], F32) for i in range(8)]
    PSB = [p.bitcast(BF16) for p in PS]

    cst = A.alloc("cst", [128, NCONST], F32)
    P.dma("sp", cst[:, :], consts_d)

    def C(name, rows=128):
        o, w = CONST_OFFS[name]
        return cst[0:rows, o:o + w]

    lmk = A.alloc("lmk", [128, 8, 128], BF16)
    P.dma("sp", lmk[:, :, :].rearrange("p k t -> p (k t)"), lmask_d)
    ident_bf = A.alloc("identbf", [128, 128], BF16)
    ones_bf = A.alloc("onesbf", [128, 128], BF16)
    P.copy("dve", ident_bf[:, :], C("ident"))
    P.copy("dve", ones_bf[:, :], C("ones"))

    vec = A.alloc("vec", [128, 160], F32)
    vrow = A.alloc("vrow", [128, 128], F32)
    ident32 = C("ident")
    P.memset("dve", vrow[:, :], 0.0)
    VC = {}
    vcol = [0]
    vparts = []

    def load_cols(name, src, n):
        c0 = vcol[0]
        vcol[0] += n
        P.dma("sp", vrow[c0:c0 + n, 0:src.shape[-1]], src, rk=[vrow], wk=[("vrowpart", name)])
        VC[name] = c0
        vparts.append(("vrowpart", name))
        return c0

    load_cols("g_pre_mix", g_pre_mix.rearrange("(k p) -> k p", p=128), 8)
    load_cols("g_mem", g_mem.rearrange("(k p) -> k p", p=128), 8)
    load_cols("g_hg", g_hg_out.rearrange("(k p) -> k p", p=128), 4)
    load_cols("g_gdn", g_gdn_out.rearrange("(k p) -> k p", p=128), 1)
    load_cols("l0", lb_logits[0].rearrange("(k p) -> k p", p=128), 4)
    load_cols("l1", lb_logits[1].rearrange("(k p) -> k p", p=128), 4)
    load_cols("g_post_mix", g_post_mix.rearrange("(k p) -> k p", p=128), 8)
    load_cols("g_pre_ffn", g_pre_ffn.rearrange("(k p) -> k p", p=128), 8)
    load_cols("g_post_ffn", g_post_ffn.rearrange("(k p) -> k p", p=128), 8)
    load_cols("w_conv", w_conv.rearrange("j (k p) -> (j k) p", p=128), 48)
    load_cols("a_log", a_log.rearrange("(o h) -> o h", o=1), 1)
    load_cols("dt_bias", dt_bias.rearrange("(o h) -> o h", o=1), 1)
    assert vcol[0] <= 128
    vcol[0] = 128
    P.tr(PS[7][:, 0:128], vrow[:, :], ident32, rk=[vrow, cst] + vparts)
    P.copy("dve", vec[:, 0:128], PS[7][:, 0:128])

    def vc(name, k=0, n=1):
        c0 = VC[name] + k
        return vec[:, c0:c0 + n]

    VC["lb"] = vcol[0]; vcol[0] += 4
    VC["oml"] = vcol[0]; vcol[0] += 4
    VC["tmp4"] = vcol[0]; vcol[0] += 4
    VC["halfoml"] = vcol[0]; vcol[0] += 4
    VC["fbias"] = vcol[0]; vcol[0] += 4
    VC["noml"] = vcol[0]; vcol[0] += 4
    P.tt("dve", vc("tmp4", 0, 4), vc("l1", 0, 4), vc("l0", 0, 4), ALU.subtract)
    P.act(vc("tmp4", 0, 4), vc("tmp4", 0, 4), AF.Exp)
    P.ts("dve", vc("tmp4", 0, 4), vc("tmp4", 0, 4), 1.0, None, ALU.add)
    P.recip(vc("lb", 0, 4), vc("tmp4", 0, 4))
    P.ts("dve", vc("oml", 0, 4), vc("lb", 0, 4), -1.0, 1.0, ALU.mult, ALU.add)
    P.ts("dve", vc("halfoml", 0, 4), vc("oml", 0, 4), 0.5, None, ALU.mult)
    P.tt("dve", vc("fbias", 0, 4), vc("lb", 0, 4), vc("halfoml", 0, 4), ALU.add)
    P.ts("dve", vc("noml", 0, 4), vc("halfoml", 0, 4), -1.0, None, ALU.mult)

    OFF_XNT = 36864
    OFF_OHG = OFF_XNT + 33792
    OFF_OGDN = OFF_OHG + 16896
    OFF_OMEM = OFF_OGDN + 16896
    OFF_MRG = OFF_OMEM + 16896
    OFF_MRG_END = OFF_MRG + 33792
    assert A.off <= OFF_XNT, A.off
    xnT = A.alloc_at("xnT", [128, 8, NTOK], BF16, OFF_XNT)
    A.off = OFF_OHG

    def norm_transpose(src_rows_fn, ntiles, rows_fn, gname, dstT, tag):
        m0 = A.mark()
        xb = [A.alloc("xt%d" % i, [128, D], F32) for i in range(3)]
        xs = [A.alloc("xs%d" % i, [128, D], BF16) for i in range(3)]
        junk = [A.alloc("junk%d" % i, [128, D], BF16) for i in range(3)]
        stat = A.alloc("stat", [128, 3 * 32], F32)
        P.memset("dve", stat[:, :], 0.0)
        def chain(ti):
            rows = rows_fn(ti)
            xt = xb[ti % 3]
            P.dma("sp", xt[0:rows, :], src_rows_fn(ti))
            P.act(junk[ti % 3][0:rows, :], xt[0:rows, :], AF.Square, accum_out=stat[0:rows, ti:ti + 1])
            yield
            P.act(stat[0:rows, 32 + ti:33 + ti], stat[0:rows, ti:ti + 1], AF.Ln, scale=1.0 / D, bias=EPS)
            P.act(stat[0:rows, 64 + ti:65 + ti], stat[0:rows, 32 + ti:33 + ti], AF.Exp, scale=-0.5)
            xsb = xs[ti % 3]
            P.act(xsb[0:rows, :], xt[0:rows, :], AF.Copy, scale=stat[0:rows, 64 + ti:65 + ti])
            yield
            pb = PSB[ti % 3]
            for k in range(8):
                P.tr(pb[:, k * 128:k * 128 + rows], xsb[0:rows, k * 128:(k + 1) * 128], ident_bf[0:rows, 0:rows])
            tok0 = 128 * ti
            src = pb[:, 0:1024].rearrange("p (k t) -> p k t", k=8)[:, :, 0:rows]
            P.tt("dve", dstT[:, :, tok0:tok0 + rows], src, bc_free(vc(gname, 0, 8), rows), ALU.mult)
            yield

        interleave([chain(ti) for ti in range(ntiles)], 3, 1)
        A.release(m0)

    def x_rows(ti):
        if ti < 16:
            return x_prompt[128 * ti:128 * ti + 128, :]
        return x_sample[:, :]

    norm_transpose(x_rows, 17, lambda ti: 128 if ti < 16 else 64, "g_pre_mix", xnT, "x")

    if debug:
        m0 = A.mark()
        dtmp = A.alloc("dbgtmp", [128, 8, NTOK], F32)
        P.copy("dve", dtmp[:, :, :], xnT[:, :, :])
        P.dma("sp", dbg["xnT"], dtmp[:, :, :])
        A.release(m0)


    w_in_v = w_in.rearrange("(k p) c -> p k c", p=128)
    ogdnT = A.alloc_at("ogdnT", [128, 4, NTOK], BF16, OFF_OHG)
    ohgT = A.alloc_at("ohgT", [128, 4, NTOK], BF16, OFF_OGDN)
    omemT = A.alloc_at("omemT", [128, 4, NTOK], BF16, OFF_OMEM)

    def KY(t, *idx):
        return (t.name,) + tuple(idx)

    STS = [(512 * i, 512, 4, 128, False) for i in range(4)] + [(SEQ, 64, 1, 64, True)]

    def phase_hgrn():
        A.off = OFF_OMEM
        m0 = A.mark()
        whg = A.alloc("whg", [128, 8, 2048], BF16)
        for k in range(8):
            P.dma("pool", whg[:, k, :], w_in_v[:, k, 0:2048], wk=[KY(whg, k)])
        wkeys = [KY(whg, k) for k in range(8)]
        WH = 256
        v_sb_2 = [A.alloc("hv%d" % i, [128, 2, 512], BF16) for i in range(2)]
        thf = A.alloc("hthf", [128, 4, WH], F32)
        logf = thf
        G = A.alloc("hG", [128, 4, WH], F32)
        eG_2 = [A.alloc("heG%d" % i, [128, 4, WH], F32) for i in range(2)]
        eNG = thf
        kk = A.alloc("hkk", [128, 4, WH], F32)
        qtT_2 = [A.alloc("hqtT%d" % i, [128, 4, WH], BF16) for i in range(2)]
        ktT_2 = [A.alloc("hktT%d" % i, [128, 4, WH], BF16) for i in range(2)]
        thg_2 = [A.alloc("hthg%d" % i, [128, 4, WH], F32) for i in range(2)]
        ATm_2 = [A.alloc("hATm%d" % i, [128, 4, 128], BF16) for i in range(2)]
        ktok_2 = [A.alloc("hktok%d" % i, [128, 4, 128], BF16) for i in range(2)]
        S32 = A.alloc("hS32", [128, 4, 128], F32)
        Sbf = A.alloc("hSbf", [128, 4, 128], BF16)
        tmpS = A.alloc("htmpS", [128, 4, 128], F32)
        sq = A.alloc("hsq", [128, 512], BF16)
        lnv = A.alloc("hlnv", [128, 512], F32)
        rstd = A.alloc("hrstd", [128, 512], F32)
        t1 = A.alloc("ht1", [128, 512], F32)
        ghalf = A.alloc("hghalf", [128, 4], F32)
        P.ts("dve", ghalf[:, :], vc("g_hg", 0, 4), 0.5, None, ALU.mult)
        P.memset("dve", S32[:, :, :], 0.0)
        P.memset("dve", Sbf[:, :, :], 0.0)
        pj = [0]

        def proj_fm(col0, tok0, W):
            ps = PS[pj[0] % 2]
            pj[0] += 1
            for k in range(8):
                P.mm(ps[:, 0:W], whg[:, k, col0:col0 + 128], xnT[:, k, tok0:tok0 + W], start=(k == 0), stop=(k == 7),
                     rk=[wkeys[k], xnT])
            return ps

        HSTS = [(WH * i, WH, 2, 128, False) for i in range(8)] + [(SEQ, 64, 1, 64, True)]

        def stageA(sti):
            (tok0, W, ntile, TT, is_s) = HSTS[sti]
            (v_sb, eG, qtT, ktT, thg) = [x[sti % 2] for x in (v_sb_2, eG_2, qtT_2, ktT_2, thg_2)]
            sgT = thg
            for ti in range(ntile):
                t0 = tok0 + ti * TT
                psv = PS[pj[0] % 2]
                pj[0] += 1
                for k in range(8):
                    P.mm(psv[0:TT, :], xnT[:, k, t0:t0 + TT], whg[:, k, 1024:1536], start=(k == 0), stop=(k == 7),
                         rk=[wkeys[k], xnT])
                P.copy("act", v_sb[0:TT, ti, :], psv[0:TT, :])
                yield
            for h in range(4):
                ps = proj_fm(512 + h * 128, tok0, W)
                P.act(thf[:, h, 0:W], ps[:, 0:W], AF.Tanh, scale=0.5, wk=[KY(thf, h)])
            for h in range(4):
                ps = proj_fm(1536 + h * 128, tok0, W)
                P.act(thg[:, h, 0:W], ps[:, 0:W], AF.Tanh, scale=0.5)
                P.stt("dve", thg[:, h, 0:W], thg[:, h, 0:W], 1.0, ps[:, 0:W], ALU.add, ALU.mult)
                P.ts("dve", sgT[:, h, 0:W], thg[:, h, 0:W], ghalf[:, h:h + 1], None, ALU.mult)
            yield
            scanm = C("scan4")[:, 0:W] if is_s else C("scan64")[:, 0:W]
            for h in range(4):
                P.ts("dve", kk[:, h, 0:W], thf[:, h, 0:W], vc("noml", h), vc("halfoml", h), ALU.mult, ALU.add, rk=[KY(thf, h), vec], wk=[KY(kk, h)])
            yield
            for h in range(4):
                P.act(logf[:, h, 0:W], thf[:, h, 0:W], AF.Ln, scale=vc("halfoml", h), bias=vc("fbias", h), rk=[KY(thf, h), KY(kk, h), vec], wk=[KY(thf, h)])
            yield
            for h in range(4):
                P.scan(G[:, h, 0:W], scanm, logf[:, h, 0:W], 0.0, ALU.mult, ALU.add, rk=[KY(thf, h), cst], wk=[KY(G, h)])
            yield
            for h in range(4):
                P.act(eG[:, h, 0:W], G[:, h, 0:W], AF.Exp, rk=[KY(G, h)], wk=[KY(eG, h), eG])
                P.act(eNG[:, h, 0:W], G[:, h, 0:W], AF.Exp, scale=-1.0, rk=[KY(G, h)], wk=[KY(thf, h)])
            yield
            for h in range(4):
                ps = proj_fm(h * 128, tok0, W)
                P.tt("dve", qtT[:, h, 0:W], ps[:, 0:W], eG[:, h, 0:W], ALU.mult, rk=[ps, KY(eG, h)], wk=[KY(qtT, h), qtT])
                P.tt("pool", ktT[:, h, 0:W], kk[:, h, 0:W], eNG[:, h, 0:W], ALU.mult, rk=[KY(kk, h), KY(thf, h)], wk=[KY(ktT, h), ktT])
                yield

        def stageB1(sti, ti):
            (tok0, W, ntile, TT, is_s) = HSTS[sti]
            (v_sb, eG, qtT, ktT, thg) = [x[sti % 2] for x in (v_sb_2, eG_2, qtT_2, ktT_2, thg_2)]
            sgT = thg
            if True:
                c0 = ti * TT
                t0 = tok0 + c0
                gt = 2 * sti + ti
                oT = PS[6 + (gt % 2)]
                ATm = ATm_2[gt % 2]
                ktok = ktok_2[gt % 2]
                mask = C("maskB_s", 64) if is_s else C("maskH_p")
                for h in range(4):
                    P.mm(PS[3][0:TT, h * 128:h * 128 + TT], ktT[:, h, c0:c0 + TT], qtT[:, h, c0:c0 + TT])
                maskb = AP(mask.tensor, mask.offset, [list(mask.ap[0]), [0, 4], list(mask.ap[1])])
                P.tt("dve", ATm[0:TT, :, 0:TT], PS[3][0:TT, :].rearrange("p (h t) -> p h t", h=4)[:, :, 0:TT], maskb, ALU.mult)
                yield
                for h in range(4):
                    P.tr(PSB[4][0:TT, h * 128:(h + 1) * 128], ktT[:, h, c0:c0 + TT], ident_bf[:, :])
                P.copy("act", ktok[0:TT, :, :], PSB[4][0:TT, 0:512].rearrange("p (h d) -> p h d", h=4))
                yield
                for h in range(4):
                    P.mm(oT[:, h * TT:(h + 1) * TT], v_sb[0:TT, ti, h * 128:(h + 1) * 128], ATm[0:TT, h, 0:TT], start=(h == 0), stop=False)
                yield

        def stageB2(sti, ti):
            (tok0, W, ntile, TT, is_s) = HSTS[sti]
            (v_sb, eG, qtT, ktT, thg) = [x[sti % 2] for x in (v_sb_2, eG_2, qtT_2, ktT_2, thg_2)]
            sgT = thg
            if True:
                c0 = ti * TT
                t0 = tok0 + c0
                gt = 2 * sti + ti
                oT = PS[6 + (gt % 2)]
                ATm = ATm_2[gt % 2]
                ktok = ktok_2[gt % 2]
                if not is_s:
                    for b in range(2):
                        r0 = 64 * b
                        for h in range(4):
                            P.mm(oT[:, h * TT + r0:h * TT + r0 + 64], Sbf[:, h, :], qtT[:, h, c0 + r0:c0 + r0 + 64], start=False, stop=(b == 1 and h == 3))
                        for h in range(4):
                            P.mm(PS[5][:, h * 128:(h + 1) * 128], ktok[r0:r0 + 64, h, :], v_sb[r0:r0 + 64, ti, h * 128:(h + 1) * 128])
                        egcv = AP(eG, c0 + r0 + 63, [[4 * WH, 128], [WH, 4], [0, 128]])
                        P.tt("dve", S32[:, :, :], S32[:, :, :], PS[5][:, :].rearrange("p (h v) -> p h v", h=4), ALU.add)
                        P.tt("dve", Sbf[:, :, :], S32[:, :, :], egcv, ALU.mult)
                        P.tt("dve", S32[:, :, :], S32[:, :, :], egcv, ALU.mult)
                        yield
                else:
                    mS = A.mark()
                    S0_2 = [A.alloc("hS0%d" % q_, [128, 4, 4, 128], F32) for q_ in range(2)]
                    S0bf_2 = [A.alloc("hS0bf%d" % q_, [128, 4, 4, 128], BF16) for q_ in range(2)]
                    vm_2 = [A.alloc("hvm%d" % q_, [128, 2, 4, 128], BF16) for q_ in range(2)]
                    for grp in range(4):
                        i0 = 4 * grp
                        S0 = S0_2[grp % 2]
                        S0bf = S0bf_2[grp % 2]
                        vm = vm_2[grp % 2]
                        P.dma("sp", S0[:, :, :, :], st_hg_in[i0:i0 + 4].rearrange("i h d v -> d i h v"))
                        P.copy("act", S0bf[:, :, :, :], S0[:, :, :, :])
                        for h in range(4):
                            for ii in range(4):
                                i = i0 + ii
                                P.mm(oT[:, h * TT + 4 * i:h * TT + 4 * i + 4], S0bf[:, ii, h, :], qtT[:, h, 4 * i:4 * i + 4], start=False,
                                     stop=(grp == 3 and h == 3 and ii == 3))
                        egc = AP(eG, 4 * i0 + 3, [[4 * WH, 128], [4, 4], [WH, 4], [0, 128]])
                        P.tt("dve", S0[:, :, :, :], S0[:, :, :, :], egc, ALU.mult, rk=[S0, S0bf, eG])
                        for h in range(4):
                            vin = AP(v_sb, h * 128, [[2 * 512, 64], [0, 4], [1, 128]])
                            bmv = bc_free(C("bm", 64)[:, i0:i0 + 4], 128)
                            vmh = vm[0:64, h % 2, :, :]
                            P.tt("dve", vmh, vin, bmv, ALU.mult, rk=[v_sb, cst], wk=[KY(vm, h % 2)])
                            for ii in range(4):
                                P.mm(PS[5][:, ii * 128:(ii + 1) * 128], ktok[0:64, h, :], vm[0:64, h % 2, ii, :], rk=[ktok, KY(vm, h % 2)])
                            egc2 = AP(eG, h * WH + 4 * i0 + 3, [[4 * WH, 128], [4, 4], [0, 128]])
                            P.tt("dve", tmpS[:, :, :], PS[5][:, :].rearrange("p (i v) -> p i v", i=4), egc2, ALU.mult)
                            P.tt("dve", S0[:, :, h, :], tmpS[:, :, :], S0[:, :, h, :], ALU.add)
                        P.dma("sp", o_hg_s[i0:i0 + 4].rearrange("i h d v -> d i h v"), S0[:, :, :, :])
                        yield
                    A.release(mS)
                n = 4 * TT
                P.act(sq[:, 0:n], oT[:, 0:n], AF.Square)
                P.mm(PS[2][:, 0:n], ones_bf[:, :], sq[:, 0:n])
                P.act(lnv[:, 0:n], PS[2][:, 0:n], AF.Ln, scale=1.0 / 128, bias=EPS)
                P.act(rstd[:, 0:n], lnv[:, 0:n], AF.Exp, scale=-0.5)
                P.tt("dve", t1[:, 0:n], oT[:, 0:n], rstd[:, 0:n], ALU.mult)
                P.tt("dve", ohgT[:, :, t0:t0 + TT], t1[:, 0:n].rearrange("p (h t) -> p h t", h=4), sgT[:, :, c0:c0 + TT], ALU.mult)
                yield

        def drive(gens):
            alive = list(gens)
            while alive:
                for g in list(alive):
                    try:
                        next(g)
                    except StopIteration:
                        alive.remove(g)

        tl = [(sti, ti) for sti in range(len(HSTS)) for ti in range(HSTS[sti][2])]
        drive([stageA(0)])
        drive([stageB1(*tl[0])])
        for n, (sti, ti) in enumerate(tl):
            gs = [stageB2(sti, ti)]
            if n + 1 < len(tl):
                gs.append(stageB1(*tl[n + 1]))
            if ti == 0 and sti + 1 < len(HSTS):
                gs.append(stageA(sti + 1))
            drive(gs)
        P.dma("sp", o_hg_p.rearrange("h d v -> d h v"), S32[:, :, :])
        A.release(m0)


    def phase_gdn():
        A.off = OFF_OGDN
        m0 = A.mark()
        WG = 128
        wgd = A.alloc("wgd", [128, 8, 2056], BF16)
        for k in range(8):
            P.dma("pool", wgd[:, k, :], w_in_v[:, k, 2048:4104], wk=[KY(wgd, k)])
        wkeys = [KY(wgd, k) for k in range(8)]
        cext = A.alloc("gcext", [128, 12, 3 + WG], BF16)
        cact = A.alloc("gcact", [128, 8, WG], F32)
        th = A.alloc("gth", [128, 12, WG], BF16)
        sqb = A.alloc("gsq", [128, 8, WG], BF16)
        lnv = A.alloc("glnv", [128, 8, WG], F32)
        dgw = A.alloc("gdgw", [128, 48, 128], BF16)
        qnT_2 = [A.alloc("gqnT%d" % i, [128, 4, WG], BF16) for i in range(2)]
        knT_2 = [A.alloc("gknT%d" % i, [128, 4, WG], BF16) for i in range(2)]
        qhT_2 = [A.alloc("gqhT%d" % i, [128, 4, WG], BF16) for i in range(2)]
        vTb_2 = [A.alloc("gvTb%d" % i, [128, 4, WG], BF16) for i in range(2)]
        Gbc_2 = [A.alloc("gGbc%d" % i, [128, 4, WG], F32) for i in range(2)]
        eGbc_2 = [A.alloc("geGbc%d" % i, [128, 4, WG], F32) for i in range(2)]
        bbc_2 = [A.alloc("gbbc%d" % i, [128, 4, WG], F32) for i in range(2)]
        gT = A.alloc("ggT", [4, WG], F32)
        GT_2 = [A.alloc("gGT%d" % i, [4, WG], F32) for i in range(2)]
        GmT_2 = [A.alloc("gGmT%d" % i, [4, WG], F32) for i in range(2)]
        bT_2 = [A.alloc("gbT%d" % i, [4, WG], F32) for i in range(2)]
        zsg_2 = [A.alloc("gzsg%d" % i, [128, 4, WG], F32) for i in range(2)]
        tok = A.alloc("gtok", [128, 12], F32)
        etok = A.alloc("getok", [128, 8], F32)
        hb = A.alloc("ghb", [128, 4], F32)
        beg = A.alloc("gbeg", [128, 4], F32)
        D1 = A.alloc("gD1", [128, 4, 128], F32)
        AqkT = A.alloc("gAqkT", [128, 4, 128], BF16)
        T1 = A.alloc("gT1", [128, 4, 128], BF16)
        ATb = [A.alloc("gAT%d" % i, [128, 4, 128], BF16) for i in range(1)]
        Tn = [A.alloc("gTn%d" % i, [128, 4, 128], BF16) for i in range(2)]
        Vb = A.alloc("gVb", [128, 4, 128], BF16)
        PTb = [A.alloc("gPT%d" % i, [128, 4, 128], BF16) for i in range(2)]
        kbg = A.alloc("gkbg", [128, 4, 128], BF16)
        khat = A.alloc("gkhat", [128, 4, 128], BF16)
        vbt = A.alloc("gvbt", [128, 4, 128], BF16)
        nwT = A.alloc("gnwT", [128, 4, 128], BF16)
        vnew = A.alloc("gvnew", [128, 4, 128], BF16)
        S32 = A.alloc("gS32", [128, 4, 128], F32)
        Sbf = A.alloc("gSbf", [128, 4, 128], BF16)
        sq2 = A.alloc("gsq2", [128, 512], BF16)
        lnv2 = A.alloc("glnv2", [128, 512], F32)
        t1 = A.alloc("gt1", [128, 512], F32)
        ghalf = A.alloc("gghalf", [128, 1], F32)
        wcv = A.alloc("gwcv", [128, 4, 12], F32)
        P.copy("dve", wcv[:, :, :].rearrange("p j k -> p (j k)"), vc("w_conv", 0, 48))
        P.tt("dve", dgw[:, :, :], AP(ident_bf, 0, [[128, 128], [0, 48], [1, 128]]), AP(wcv, 0, [[48, 128], [1, 48], [0, 128]]), ALU.mult)
        P.ts("dve", ghalf[:, :], vc("g_gdn", 0, 1), 0.5, None, ALU.mult)
        P.memset("dve", S32[:, :, :], 0.0)
        P.memset("dve", Sbf[:, :, :], 0.0)
        P.memset("dve", cext[:, :, :], 0.0)
        nA = A.alloc("gnA", [4, 1], F32)
        P.act(nA[:, :], vec[0:4, VC["a_log"]:VC["a_log"] + 1], AF.Exp)
        P.ts("dve", nA[:, :], nA[:, :], -1.0, None, ALU.mult)
        dtb = vec[0:4, VC["dt_bias"]:VC["dt_bias"] + 1]
        sel = C("sel4", 4)
        pj = [0]

        def proj_fm(col0, tok0, W, M=128):
            ps = PS[pj[0] % 2]
            pj[0] += 1
            for k in range(8):
                P.mm(ps[0:M, 0:W], wgd[:, k, col0:col0 + M], xnT[:, k, tok0:tok0 + W], start=(k == 0), stop=(k == 7),
                     rk=[wkeys[k], xnT])
            return ps

        GSTS = [(WG * i, WG, 1, 128, False) for i in range(16)] + [(SEQ, 64, 1, 64, True)]
        def stageA(sti):
            (tok0, W, ntile, TT, is_s) = GSTS[sti]
            (qnT, knT, qhT, vTb, Gbc, eGbc, bbc, GT, GmT, bT, zsg) = [x[sti % 2] for x in (qnT_2, knT_2, qhT_2, vTb_2, Gbc_2, eGbc_2, bbc_2, GT_2, GmT_2, bT_2, zsg_2)]
            if is_s:
                cx = AP(cext, 0, [[12 * (3 + WG), 128], [3 + WG, 12], [7, NS], [1, 7]])
                mcr = A.mark()
                crow = A.alloc("gcrow", [48, 1536], F32)
                P.dma("sp", crow[:, :], st_conv_in.rearrange("i j c -> (i j) c"))
                for half in range(2):
                    for cc in range(6):
                        c = 6 * half + cc
                        P.tr(PS[half][:, cc * 48:(cc + 1) * 48], crow[0:48, c * 128:(c + 1) * 128], C("ident", 48)[:, 0:48])
                    P.copy("dve", cx[:, 6 * half:6 * half + 6, :, 0:3], PS[half][:, 0:288].rearrange("p (c i j) -> p c i j", c=6, i=NS))
                A.release(mcr)
            else:
                if tok0 > 0:
                    P.copy("dve", cext[:, :, 0:3], cext[:, :, WG:WG + 3])
            for c in range(12):
                ps = proj_fm(c * 128, tok0, W)
                if is_s:
                    P.copy("act", cx[:, c, :, 3:7], ps[:, 0:64].rearrange("p (i t) -> p i t", i=NS))
                else:
                    P.copy("act", cext[:, c, 3:3 + W], ps[:, 0:W])
                yield
            if is_s or tok0 + W == SEQ:
                mcv = A.mark()
                cvo = A.alloc("gcvo", [64, 1536], F32)
                rows = slice(tok0, tok0 + 64) if is_s else slice(SEQ - 3, SEQ)
                nr = 64 if is_s else 3
                for cb in range(3):
                    for k in range(8):
                        P.mm(PS[cb % 2][0:nr, :], xnT[:, k, rows], wgd[:, k, cb * 512:(cb + 1) * 512], start=(k == 0), stop=(k == 7),
                             rk=[wkeys[k], xnT])
                    P.copy("act", cvo[0:nr, cb * 512:(cb + 1) * 512], PS[cb % 2][0:nr, :])
                if is_s:
                    for j in range(3):
                        src = AP(cvo, (1 + j) * 1536, [[4 * 1536, NS], [1, 1536]])
                        P.dma("sp", o_conv_s[:, j, :], src)
                else:
                    P.dma("sp", o_conv_p, cvo[0:3, :])
                A.release(mcv)
            for g in range(3):
                bank = PS[g % 2]
                for cc in range(4):
                    c = 4 * g + cc
                    for j in range(4):
                        rhs = cx[:, c, :, j:j + 4] if is_s else cext[:, c, j:j + W]
                        P.mm(bank[:, cc * 128:cc * 128 + W], dgw[:, j * 12 + c, :], rhs, start=(j == 0), stop=(j == 3))
                bv = bank[:, :].rearrange("p (c t) -> p c t", c=4)[:, :, 0:W]
                P.act(th[:, 4 * g:4 * g + 4, 0:W], bv, AF.Tanh, scale=0.5)
                dst = cact[:, 4 * g:4 * g + 4, 0:W] if g < 2 else vTb[:, :, 0:W]
                P.stt("dve", dst, th[:, 4 * g:4 * g + 4, 0:W], 1.0, bv, ALU.add, ALU.mult)
            for h in range(4):
                ps = proj_fm(1536 + h * 128, tok0, W)
                P.act(zsg[:, h, 0:W], ps[:, 0:W], AF.Tanh, scale=0.5)
                P.stt("dve", zsg[:, h, 0:W], zsg[:, h, 0:W], 1.0, ps[:, 0:W], ALU.add, ALU.mult)
            P.ts("dve", zsg[:, :, 0:W], zsg[:, :, 0:W], ghalf[:, 0:1], None, ALU.mult)
            psb_ = proj_fm(2052, tok0, W, M=4)
            P.act(bT[:, 0:W], psb_[0:4, 0:W], AF.Tanh, scale=0.5)
            P.ts("dve", bT[:, 0:W], bT[:, 0:W], 0.5, 0.5, ALU.mult, ALU.add)
            yield
            psa = proj_fm(2048, tok0, W, M=4)
            P.act(gT[:, 0:W], psa[0:4, 0:W], AF.Exp, bias=dtb)
            P.act(gT[:, 0:W], gT[:, 0:W], AF.Ln, bias=1.0)
            P.ts("dve", gT[:, 0:W], gT[:, 0:W], nA[:, 0:1], None, ALU.mult)
            scanm = (C("scan4")[0:4, 0:W] if is_s else C("scan128")[0:4, 0:W])
            P.scan(GT[:, 0:W], scanm, gT[:, 0:W], 0.0, ALU.mult, ALU.add)
            if is_s:
                gcv = AP(GT, 3, [[WG, 4], [4, NS], [0, 4]])
                P.tt("dve", GmT[:, 0:64].rearrange("p (i t) -> p i t", i=NS), gcv, GT[:, 0:64].rearrange("p (i t) -> p i t", i=NS), ALU.subtract)
            else:
                gcv = AP(GT, 127, [[WG, 4], [128, ntile], [0, 128]])
                P.tt("dve", GmT[:, 0:W].rearrange("p (i t) -> p i t", i=ntile), gcv, GT[:, 0:W].rearrange("p (i t) -> p i t", i=ntile), ALU.subtract)
            for h in range(4):
                P.mm(PS[0][:, h * 128:h * 128 + W], sel[:, h * 128:(h + 1) * 128], GT[:, 0:W])
            for h in range(4):
                P.mm(PS[1][:, h * 128:h * 128 + W], sel[:, h * 128:(h + 1) * 128], bT[:, 0:W])
            gv = PS[0][:, :].rearrange("p (h t) -> p h t", h=4)[:, :, 0:W]
            P.copy("act", Gbc[:, :, 0:W], gv)
            P.act(eGbc[:, :, 0:W], gv, AF.Exp)
            P.copy("dve", bbc[:, :, 0:W], PS[1][:, :].rearrange("p (h t) -> p h t", h=4)[:, :, 0:W])
            yield
            P.act(sqb[:, :, 0:W], cact[:, 0:8, 0:W], AF.Square)
            for qk in range(2):
                for h in range(4):
                    P.mm(PS[qk][:, h * 128:h * 128 + W], ones_bf[:, :], sqb[:, 4 * qk + h, 0:W])
                P.act(lnv[:, 4 * qk:4 * qk + 4, 0:W], PS[qk][:, :].rearrange("p (h t) -> p h t", h=4)[:, :, 0:W], AF.Ln, bias=4.0 * EPS)
            P.act(lnv[:, :, 0:W], lnv[:, :, 0:W], AF.Exp, scale=-0.5)
            yield
            P.stt("dve", qnT[:, :, 0:W], cact[:, 0:4, 0:W], 128.0 ** -0.5, lnv[:, 0:4, 0:W], ALU.mult, ALU.mult)
            P.tt("dve", knT[:, :, 0:W], cact[:, 4:8, 0:W], lnv[:, 4:8, 0:W], ALU.mult)
            P.tt("dve", qhT[:, :, 0:W], qnT[:, :, 0:W], eGbc[:, :, 0:W], ALU.mult)
            yield

        def stageB(sti):
            (tok0, W, ntile, TT, is_s) = GSTS[sti]
            K = 2 if is_s else 7
            (qnT, knT, qhT, vTb, Gbc, eGbc, bbc, GT, GmT, bT, zsg) = [x[sti % 2] for x in (qnT_2, knT_2, qhT_2, vTb_2, Gbc_2, eGbc_2, bbc_2, GT_2, GmT_2, bT_2, zsg_2)]
            for ti in range(ntile):
                c0 = ti * TT
                t0 = tok0 + c0
                oT = PS[7]
                negm = C("negB_s", 64) if is_s else C("negG_p")
                nsm = C("nmaskBs_s", 64) if is_s else C("nmaskGs_p")
                P.tr(PS[2][0:TT, 0:4], GT[0:4, c0:c0 + TT], C("ident", 4)[:, 0:4])
                P.tr(PS[2][0:TT, 4:8], GmT[0:4, c0:c0 + TT], C("ident", 4)[:, 0:4])
                P.tr(PS[2][0:TT, 8:12], bT[0:4, c0:c0 + TT], C("ident", 4)[:, 0:4])
                P.copy("dve", tok[0:TT, :], PS[2][0:TT, 0:12])
                P.act(etok[0:TT, :], tok[0:TT, 0:8], AF.Exp)
                P.ts("dve", hb[0:TT, :], tok[0:TT, 8:12], 0.5, None, ALU.mult)
                P.tt("dve", beg[0:TT, :], tok[0:TT, 8:12], etok[0:TT, 0:4], ALU.mult)
                yield
                P.tt("dve", D1[0:TT, :, 0:TT], Gbc[0:TT, :, c0:c0 + TT], bc_free(tok[0:TT, 0:4], TT), ALU.subtract)
                negb = AP(negm.tensor, negm.offset, [list(negm.ap[0]), [0, 4], list(negm.ap[1])])
                P.tt("dve", D1[0:TT, :, 0:TT], D1[0:TT, :, 0:TT], negb, ALU.add)
                P.act(D1[0:TT, :, 0:TT], D1[0:TT, :, 0:TT], AF.Exp)
                nsb = AP(nsm.tensor, nsm.offset, [list(nsm.ap[0]), [0, 4], list(nsm.ap[1])])
                P.tt("pool", T1[0:TT, :, 0:TT], D1[0:TT, :, 0:TT], bbc[0:TT, :, c0:c0 + TT], ALU.mult)
                P.tt("pool", T1[0:TT, :, 0:TT], T1[0:TT, :, 0:TT], nsb, ALU.mult)
                yield
                for h in range(4):
                    P.mm(PS[2][0:TT, h * 128:h * 128 + TT], knT[:, h, c0:c0 + TT], knT[:, h, c0:c0 + TT], start=True, stop=True)
                for h in range(4):
                    P.mm(PS[3][0:TT, h * 128:h * 128 + TT], knT[:, h, c0:c0 + TT], qnT[:, h, c0:c0 + TT], start=True, stop=True)
                kkv = PS[2][0:TT, :].rearrange("p (h t) -> p h t", h=4)[:, :, 0:TT]
                qkv = PS[3][0:TT, :].rearrange("p (h t) -> p h t", h=4)[:, :, 0:TT]
                NT = ATb[0]
                P.tt("dve", NT[0:TT, :, 0:TT], kkv, T1[0:TT, :, 0:TT], ALU.mult)
                P.tt("dve", AqkT[0:TT, :, 0:TT], qkv, D1[0:TT, :, 0:TT], ALU.mult)
                yield
                for h in range(4):
                    P.tr(PSB[5][0:TT, h * 128:(h + 1) * 128], knT[:, h, c0:c0 + TT], ident_bf[:, :])
                for h in range(4):
                    P.act(kbg[0:TT, h, :], PSB[5][0:TT, h * 128:(h + 1) * 128], AF.Copy, scale=beg[0:TT, h:h + 1])
                    P.act(khat[0:TT, h, :], PSB[5][0:TT, h * 128:(h + 1) * 128], AF.Copy, scale=etok[0:TT, 4 + h:5 + h])
                for h in range(4):
                    P.tr(PSB[6][0:TT, h * 128:(h + 1) * 128], vTb[:, h, c0:c0 + TT], ident_bf[:, :])
                P.tt("dve", vbt[0:TT, :, :], PSB[6][0:TT, 0:512].rearrange("p (h d) -> p h d", h=4), bc_free(hb[0:TT, 0:4], 128), ALU.mult)
                yield
                idt = ident_bf[0:TT, 0:TT]

                def lmI(k):
                    return AP(lmk, k * 128, [[8 * 128, TT], [0, 2], [1, TT]])
                for hp in range(2):
                    for hh in range(2):
                        h = 2 * hp + hh
                        P.mm(PS[4 + hp][0:TT, hh * 128:hh * 128 + TT], NT[0:TT, h, 0:TT], idt, start=True, stop=False)
                        P.mm(PS[4 + hp][0:TT, hh * 128:hh * 128 + TT], idt, idt, start=False, stop=True)
                        P.mm(PS[6 + hp][0:TT, hh * 128:hh * 128 + TT], idt, NT[0:TT, h, 0:TT], start=True, stop=False)
                        P.mm(PS[6 + hp][0:TT, hh * 128:hh * 128 + TT], idt, idt, start=False, stop=True)
                for hp in range(2):
                    P.tt("dve", Tn[0][0:TT, 2 * hp:2 * hp + 2, 0:TT], PS[4 + hp][0:TT, 0:256].rearrange("p (h t) -> p h t", h=2)[:, :, 0:TT], lmI(0),
                         ALU.mult, wk=[KY(Tn[0], hp)])
                    P.tt("dve", PTb[0][0:TT, 2 * hp:2 * hp + 2, 0:TT], PS[6 + hp][0:TT, 0:256].rearrange("p (h t) -> p h t", h=2)[:, :, 0:TT], lmI(7),
                         ALU.mult, wk=[KY(PTb[0], hp)])
                yield
                cur = 0
                for k in range(1, K):
                    nxt = 1 - cur
                    for hp in range(2):
                        vb_ = PS[2 + hp]
                        for hh in range(2):
                            h = 2 * hp + hh
                            P.mm(vb_[0:TT, hh * 128:hh * 128 + TT], NT[0:TT, h, 0:TT], Tn[cur][0:TT, h, 0:TT], start=True, stop=False,
                                 rk=[NT, KY(Tn[cur], hp)])
                            P.mm(vb_[0:TT, hh * 128:hh * 128 + TT], ident_bf[0:TT, 0:TT], ident_bf[0:TT, 0:TT], start=False, stop=True)
                    for hp in range(2):
                        P.tt("dve", Vb[0:TT, 2 * hp:2 * hp + 2, 0:TT], PS[2 + hp][0:TT, 0:256].rearrange("p (h t) -> p h t", h=2)[:, :, 0:TT], lmI(k),
                             ALU.mult, wk=[KY(Vb, hp)])
                    for hp in range(2):
                        if k < K - 1:
                            for hh in range(2):
                                h = 2 * hp + hh
                                P.mm(PS[4 + hp][0:TT, hh * 128:hh * 128 + TT], PTb[cur][0:TT, h, 0:TT], Vb[0:TT, h, 0:TT], start=True, stop=True,
                                     rk=[KY(PTb[cur], hp), KY(Vb, hp)])
                        for hh in range(2):
                            h = 2 * hp + hh
                            P.mm(PS[6 + hp][0:TT, hh * 128:hh * 128 + TT], Vb[0:TT, h, 0:TT], PTb[cur][0:TT, h, 0:TT], start=True, stop=True,
                                 rk=[KY(PTb[cur], hp), KY(Vb, hp)])
                    for hp in range(2):
                        if k < K - 1:
                            P.copy("act", Tn[nxt][0:TT, 2 * hp:2 * hp + 2, 0:TT], PS[4 + hp][0:TT, 0:256].rearrange("p (h t) -> p h t", h=2)[:, :, 0:TT],
                                   wk=[KY(Tn[nxt], hp)])
                        P.copy("act", PTb[nxt][0:TT, 2 * hp:2 * hp + 2, 0:TT],
                               PS[6 + hp][0:TT, 0:256].rearrange("p (h t) -> p h t", h=2)[:, :, 0:TT], wk=[KY(PTb[nxt], hp)])
                    cur = nxt
                    yield
                PT = PTb[cur]
                for h in range(4):
                    P.mm(PS[4][:, h * 128:h * 128 + TT], kbg[0:TT, h, :], PT[0:TT, h, 0:TT], start=True, stop=True, rk=[kbg, KY(PT, h // 2)])
                P.act(nwT[:, :, 0:TT], PS[4][:, :].rearrange("p (h t) -> p h t", h=4)[:, :, 0:TT], AF.Copy, scale=-1.0)
                yield
                if not is_s:
                    for h in range(4):
                        P.mm(PS[5][0:TT, h * 128:(h + 1) * 128], PT[0:TT, h, 0:TT], vbt[0:TT, h, :], start=True, stop=False, rk=[vbt, KY(PT, h // 2)])
                        P.mm(PS[5][0:TT, h * 128:(h + 1) * 128], nwT[:, h, 0:TT], Sbf[:, h, :], start=False, stop=True)
                    P.copy("act", vnew[0:TT, :, :], PS[5][0:TT, :].rearrange("p (h v) -> p h v", h=4))
                    yield
                    for h in range(4):
                        P.mm(oT[:, h * TT:(h + 1) * TT], vnew[0:TT, h, :], AqkT[0:TT, h, 0:TT], start=True, stop=False)
                        P.mm(oT[:, h * TT:(h + 1) * TT], Sbf[:, h, :], qhT[:, h, c0:c0 + TT], start=False, stop=True)
                    for h in range(4):
                        P.mm(PS[6][:, h * 128:(h + 1) * 128], khat[0:TT, h, :], vnew[0:TT, h, :], start=True, stop=True)
                    egcv = AP(eGbc, c0 + TT - 1, [[4 * WG, 128], [WG, 4], [0, 128]])
                    P.tt("dve", S32[:, :, :], S32[:, :, :], egcv, ALU.mult)
                    P.tt("dve", Sbf[:, :, :], S32[:, :, :], PS[6][:, :].rearrange("p (h v) -> p h v", h=4), ALU.add)
                    P.tt("dve", S32[:, :, :], S32[:, :, :], PS[6][:, :].rearrange("p (h v) -> p h v", h=4), ALU.add)
                else:
                    for h in range(4):
                        P.mm(PS[5][0:TT, h * 128:(h + 1) * 128], PT[0:TT, h, 0:TT], vbt[0:TT, h, :], start=True, stop=True, rk=[vbt, KY(PT, h // 2)])
                    wS = A.alloc("gwS", [64, 4, 128], F32)
                    for quarter in range(4):
                        mS = A.mark()
                        i0 = 4 * quarter
                        S0 = A.alloc("gS0", [128, 4, 4, 128], F32)
                        S0bf = A.alloc("gS0bf", [128, 4, 4, 128], BF16)
                        nwm = A.alloc("gnwm", [128, 4, 4, 64], BF16)
                        vnm = A.alloc("gvnm", [64, 4, 4, 128], BF16)
                        P.dma("sp", S0[:, :, :, :], st_gdn_in[i0:i0 + 4].rearrange("i h d v -> d i h v"))
                        P.copy("act", S0bf[:, :, :, :], S0[:, :, :, :])
                        cmv = AP(cst, CONST_OFFS["cmneg"][0] + i0 * 64, [[NCONST, 128], [0, 4], [64, 4], [1, 64]])
                        nwv = AP(nwT, 0, [[4 * 128, 128], [128, 4], [0, 4], [1, 64]])
                        P.tt("dve", nwm[:, :, :, :], nwv, cmv, ALU.mult)
                        for h in range(4):
                            for ii in range(4):
                                i = i0 + ii
                                P.mm(PS[6][0:TT, h * 128:(h + 1) * 128], nwm[:, h, ii, :], S0bf[:, ii, h, :],
                                     start=(ii == 0), stop=(ii == 3))
                        if quarter == 0:
                            P.copy("dve", wS[0:TT, :, :], PS[6][0:TT, :].rearrange("p (h v) -> p h v", h=4))
                        else:
                            P.tt("dve", wS[0:TT, :, :], wS[0:TT, :, :], PS[6][0:TT, :].rearrange("p (h v) -> p h v", h=4), ALU.add)
                        A.release(mS)
                    P.tt("dve", vnew[0:TT, :, :], PS[5][0:TT, :].rearrange("p (h v) -> p h v", h=4), wS[0:TT, :, :], ALU.subtract)
                    for h in range(4):
                        P.mm(oT[:, h * TT:(h + 1) * TT], vnew[0:TT, h, :], AqkT[0:TT, h, 0:TT], start=(h == 0), stop=False)
                    for quarter in range(4):
                        mS = A.mark()
                        i0 = 4 * quarter
                        S0 = A.alloc("gS0b", [128, 4, 4, 128], F32)
                        S0bf = A.alloc("gS0bfb", [128, 4, 4, 128], BF16)
                        vnm = A.alloc("gvnm", [64, 4, 4, 128], BF16)
                        P.dma("sp", S0[:, :, :, :], st_gdn_in[i0:i0 + 4].rearrange("i h d v -> d i h v"))
                        P.copy("dve", S0bf[:, :, :, :], S0[:, :, :, :])
                        for h in range(4):
                            for ii in range(4):
                                i = i0 + ii
                                P.mm(oT[:, h * TT + 4 * i:h * TT + 4 * i + 4], S0bf[:, ii, h, :], qhT[:, h, 4 * i:4 * i + 4], start=False, stop=(quarter == 3 and h == 3 and ii == 3))
                        vnv = AP(vnew, 0, [[4 * 128, 64], [128, 4], [0, 4], [1, 128]])
                        bmv = AP(cst, CONST_OFFS["bm"][0] + i0, [[NCONST, 64], [0, 4], [1, 4], [0, 128]])
                        P.tt("dve", vnm[:, :, :, :], vnv, bmv, ALU.mult)
                        for ii in range(4):
                            i = i0 + ii
                            for h in range(4):
                                P.mm(PS[6][:, h * 128:(h + 1) * 128], khat[0:TT, h, :], vnm[:, h, ii, :], start=True, stop=True)
                            egc = AP(eGbc, 4 * i + 3, [[4 * WG, 128], [WG, 4], [0, 128]])
                            P.tt("dve", S0[:, ii, :, :], S0[:, ii, :, :], egc, ALU.mult)
                            P.tt("dve", S0[:, ii, :, :], S0[:, ii, :, :], PS[6][:, :].rearrange("p (h v) -> p h v", h=4), ALU.add)
                        P.dma("sp", o_gdn_s[i0:i0 + 4].rearrange("i h d v -> d i h v"), S0[:, :, :, :])
                        A.release(mS)
                n = 4 * TT
                P.act(sq2[:, 0:n], oT[:, 0:n], AF.Square)
                P.mm(PS[3][:, 0:n], ones_bf[:, :], sq2[:, 0:n])
                P.act(lnv2[:, 0:n], PS[3][:, 0:n], AF.Ln, scale=1.0 / 128, bias=EPS)
                P.act(lnv2[:, 0:n], lnv2[:, 0:n], AF.Exp, scale=-0.5)
                P.tt("dve", t1[:, 0:n], oT[:, 0:n], lnv2[:, 0:n], ALU.mult)
                P.tt("dve", ogdnT[:, :, t0:t0 + TT], t1[:, 0:n].rearrange("p (h t) -> p h t", h=4), zsg[:, :, c0:c0 + TT], ALU.mult)
            yield

        def drive(gens):
            alive = list(gens)
            while alive:
                for g in list(alive):
                    try:
                        next(g)
                    except StopIteration:
                        alive.remove(g)

        drive([stageA(0)])
        for sti in range(len(GSTS)):
            gs = [stageB(sti)]
            if sti + 1 < len(GSTS):
                gs.append(stageA(sti + 1))
            drive(gs)
        P.dma("sp", o_gdn_p.rearrange("h d v -> d h v"), S32[:, :, :])
        A.release(m0)

    phase_gdn()
    if debug:
        m0 = A.mark()
        dtmp = A.alloc("dbgtmp3", [128, 4, NTOK], F32)
        P.copy("dve", dtmp[:, :, :], ogdnT[:, :, :])
        P.dma("sp", dbg["ogdnT"], dtmp[:, :, :])
        A.release(m0)

    phase_hgrn()
    if debug:
        m0 = A.mark()
        dtmpx = A.alloc("dbgtmp2b", [128, 4, NTOK], F32)
        P.copy("dve", dtmpx[:, :, :], ohgT[:, :, :])
        P.dma("sp", dbg["ohgT"], dtmpx[:, :, :])
        A.release(m0)

    def phase_mem():
        A.off = OFF_MRG
        m0 = A.mark()
        memnT = A.alloc("memnT", [128, 8, MEM], BF16)
        norm_transpose(lambda ti: mem_prompt[128 * ti:128 * ti + 128, :], 2, lambda ti: 128, "g_mem", memnT, "m")
        wkv = A.alloc("wkv", [128, 8, 1024], BF16)
        w_kv_v = w_mem_kv.rearrange("(k p) c -> p k c", p=128)
        for k in range(8):
            P.dma("pool", wkv[:, k, :], w_kv_v[:, k, :], wk=[KY(wkv, k)])
        wq = A.alloc("wmq", [128, 8, 512], BF16)
        for k in range(8):
            P.dma("pool", wq[:, k, :], w_in_v[:, k, 4104:4616], wk=[KY(wq, k)])
        KT = A.alloc("mKT", [128, 4, MEM], BF16)
        Vsb = A.alloc("mVsb", [128, 2, 512], BF16)
        kvo = A.alloc("mkvo", [128, 2, 2, 512], F32)
        for h in range(4):
            for k in range(8):
                P.mm(PS[h % 2][:, 0:MEM], wkv[:, k, h * 128:(h + 1) * 128], memnT[:, k, :], start=(k == 0), stop=(k == 7),
                     rk=[KY(wkv, k), memnT])
            P.copy("act", KT[:, h, :], PS[h % 2][:, 0:MEM])
        for mt in range(2):
            for kv in range(2):
                ps = PS[2 + kv]
                for k in range(8):
                    P.mm(ps[:, :], memnT[:, k, mt * 128:(mt + 1) * 128], wkv[:, k, kv * 512:(kv + 1) * 512], start=(k == 0), stop=(k == 7),
                         rk=[KY(wkv, k), memnT])
                P.copy("act", kvo[:, kv, mt, :], ps[:, :])
                if kv == 1:
                    P.copy("dve", Vsb[:, mt, :], kvo[:, 1, mt, :])
        P.dma("sp", o_mk.rearrange("(mt p) c -> p mt c", p=128), kvo[:, 0, :, :])
        P.dma("sp", o_mv.rearrange("(mt p) c -> p mt c", p=128), kvo[:, 1, :, :])
        if stop == "mem1":
            return
        qT = A.alloc("mqT", [128, 2, 512], BF16)
        ET = A.alloc("mET", [128, 2, 512], BF16)
        rden = A.alloc("mrden", [128, 512], F32)
        qTs = A.alloc("mqTs", [128, 4, 64], BF16)
        cnt = [0]
        for (tok0, W, ntile, TT, is_s) in STS:
            for h in range(4):
                par = cnt[0] % 2
                cnt[0] += 1
                for k in range(8):
                    P.mm(PS[par][:, 0:W], wq[:, k, h * 128:(h + 1) * 128], xnT[:, k, tok0:tok0 + W], start=(k == 0), stop=(k == 7),
                         rk=[KY(wq, k), xnT])
                if is_s:
                    P.act(qTs[:, h, :], PS[par][:, 0:64], AF.Copy, scale=128.0 ** -0.5)
                    continue
                P.act(qT[:, par, :], PS[par][:, :], AF.Copy, scale=128.0 ** -0.5)
                for c in range(2):
                    P.mm(PS[2 + c][:, :], KT[:, h, c * 128:(c + 1) * 128], qT[:, par, :])
                    P.act(ET[:, c, :], PS[2 + c][:, :], AF.Exp)
                for c in range(2):
                    P.mm(PS[4 + par][:, :], Vsb[:, c, h * 128:(h + 1) * 128], ET[:, c, :], start=(c == 0), stop=(c == 1))
                for c in range(2):
                    P.mm(PS[6 + par][:, :], ones_bf[:, :], ET[:, c, :], start=(c == 0), stop=(c == 1))
                P.recip(rden[:, :], PS[6 + par][:, :])
                P.tt("dve", omemT[:, h, tok0:tok0 + W], PS[4 + par][:, :], rden[:, :], ALU.mult)
        if stop == "mem2":
            return
        ETs = A.alloc("mETs", [128, 2, 4, 64], BF16)
        ck_v = cache_k.rearrange("i (c p) f -> p i c f", p=128)
        cv_v = cache_v.rearrange("i (c p) f -> p i c f", p=128)
        first = [True]
        for quarter in range(4):
            mS = A.mark()
            i0 = 4 * quarter
            Kc = A.alloc("mKc", [128, 4, 2, 512], BF16)
            KcT = A.alloc("mKcT", [128, 4, 8, 128], BF16)
            P.dma("pool", Kc[:, :, :, :], ck_v[:, i0:i0 + 4, :, :])
            for ii in range(4):
                pb = PSB[ii % 2]
                for c in range(2):
                    for h in range(4):
                        P.tr(pb[:, (c * 4 + h) * 128:(c * 4 + h + 1) * 128], Kc[:, ii, c, h * 128:(h + 1) * 128], ident_bf[:, :])
                P.copy("act" if ii % 2 == 0 else "dve", KcT[:, ii, :, :], pb[:, 0:1024].rearrange("p (a m) -> p a m", a=8))
            for ii in range(4):
                i = i0 + ii
                for c in range(2):
                    for h in range(4):
                        col = (c * 4 + h) * 64 + 4 * i
                        P.mm(PS[2][:, col:col + 4], KcT[:, ii, c * 4 + h, :], qTs[:, h, 4 * i:4 * i + 4], start=first[0], stop=(i == 15 and c == 1 and h == 3))
                        first[0] = False
            A.release(mS)
        if stop == "mem3":
            return
        P.act(ETs[:, :, :, :].rearrange("p c h t -> p (c h t)"), PS[2][:, :], AF.Exp)
        first = [True]
        for quarter in range(4):
            mS = A.mark()
            i0 = 4 * quarter
            Vc = A.alloc("mVc", [128, 4, 2, 512], BF16)
            P.dma("pool", Vc[:, :, :, :], cv_v[:, i0:i0 + 4, :, :])
            for ii in range(4):
                i = i0 + ii
                for h in range(4):
                    for c in range(2):
                        P.mm(PS[3][:, h * 64 + 4 * i:h * 64 + 4 * i + 4], Vc[:, ii, c, h * 128:(h + 1) * 128], ETs[:, c, h, 4 * i:4 * i + 4],
                             start=first[0], stop=(i == 15 and c == 1 and h == 3))
                        first[0] = False
            A.release(mS)
        for c in range(2):
            P.mm(PS[4][:, 0:256], ones_bf[:, :], ETs[:, c, :, :].rearrange("p h t -> p (h t)"), start=(c == 0), stop=(c == 1))
        P.recip(rden[:, 0:256], PS[4][:, 0:256])
        P.tt("dve", omemT[:, :, SEQ:SEQ + 64], PS[3][:, 0:256].rearrange("p (h t) -> p h t", h=4), rden[:, 0:256].rearrange("p (h t) -> p h t", h=4), ALU.mult)
        A.release(m0)

    phase_mem()
    if stop in ("mem", "mem1", "mem2", "mem3"):
        P.lower()
        return nc
    if debug:
        m0 = A.mark()
        dtmp = A.alloc("dbgtmp4", [128, 4, NTOK], F32)
        P.copy("dve", dtmp[:, :, :], omemT[:, :, :])
        P.dma("sp", dbg["omemT"], dtmp[:, :, :])
        A.release(m0)

    mrgT = A.alloc_at("mrgT", [128, 8, NTOK], BF16, OFF_MRG)
    TGS = [(0, 512), (512, 512), (1024, 512), (1536, 512), (2048, 64)]

    def phase_merge():
        A.off = OFF_MRG_END
        m0 = A.mark()
        wbr = A.alloc("wbr", [128, 3, 4, D], BF16)
        for b, wsrc in enumerate((w_br_hg, w_br_gdn, w_br_mem)):
            P.dma("pool", wbr[:, b, :, :], wsrc.rearrange("(k p) c -> p k c", p=128), wk=[KY(wbr, b)])
        wg = [A.alloc("wgate%d" % i, [128, 8, 3, 128], BF16) for i in range(3)]
        thb = [A.alloc("mgth%d" % i, [128, 512], F32) for i in range(2)]
        acc = A.alloc("mgacc", [128, 512], F32)
        term = A.alloc("mgterm", [128, 512], F32)
        obs = (ohgT, ogdnT, omemT)
        wgv = w_in[:, 4616:7688].rearrange("(k p) (b f) -> p k b f", p=128, b=3)
        cnt = 0
        def load_gate_w(fc):
            wgb_ = wg[fc % 3]
            for b in range(3):
                P.dma("pool", wgb_[:, :, b, :], wgv[:, :, b, fc * 128:(fc + 1) * 128], wk=[KY(wgb_, b)])

        load_gate_w(0)
        load_gate_w(1)
        for fc in range(8):
            wgb = wg[fc % 3]
            if fc + 2 < 8:
                load_gate_w(fc + 2)
            for (t0, W) in TGS:
                for b in range(3):
                    par = cnt % 2
                    cnt += 1
                    gps = PS[par]
                    bps = PS[2 + par]
                    for k in range(8):
                        P.mm(gps[:, 0:W], wgb[:, k, b, :], xnT[:, k, t0:t0 + W], start=(k == 0), stop=(k == 7), rk=[KY(wgb, b), xnT])
                    P.act(thb[par][:, 0:W], gps[:, 0:W], AF.Tanh, scale=0.5)
                    for kc in range(4):
                        P.mm(bps[:, 0:W], wbr[:, b, kc, fc * 128:(fc + 1) * 128], obs[b][:, kc, t0:t0 + W], start=(kc == 0), stop=(kc == 3),
                             rk=[KY(wbr, b), obs[b]])
                    if b == 0:
                        P.stt("dve", acc[:, 0:W], thb[par][:, 0:W], 1.0, bps[:, 0:W], ALU.add, ALU.mult)
                    else:
                        P.stt("dve", term[:, 0:W], thb[par][:, 0:W], 1.0, bps[:, 0:W], ALU.add, ALU.mult)
                        P.tt("pool", acc[:, 0:W], acc[:, 0:W], term[:, 0:W], ALU.add)
                P.act(mrgT[:, fc, t0:t0 + W], acc[:, 0:W], AF.Copy, scale=0.5)
        A.release(m0)

    phase_merge()
    if stop == "merge":
        P.lower()
        return nc
    if debug:
        m0 = A.mark()
        dtmp = A.alloc("dbgtmp5", [128, 8, NTOK], F32)
        P.copy("dve", dtmp[:, :, :], mrgT[:, :, :])
        P.dma("sp", dbg["mrgT"], dtmp[:, :, :])
        A.release(m0)

    OFF_H = OFF_XNT
    OFF_H_END = OFF_H + 17 * 4096
    assert OFF_H_END <= OFF_MRG
    OFF_HNT = OFF_MRG_END
    OFF_HNT_END = OFF_HNT + 33792
    hall = A.alloc_at("hall", [128, 17, D], F32, OFF_H)
    hnT = A.alloc_at("hnT", [128, 8, NTOK], BF16, OFF_HNT)

    def tile_rows(ti):
        return 128 if ti < 16 else 64

    def phase_out():
        A.off = OFF_HNT_END
        m0 = A.mark()
        wo = A.alloc("wo", [128, 8, D], BF16)
        for k in range(8):
            P.dma("pool", wo[:, k, :], w_out.rearrange("(k p) c -> p k c", p=128)[:, k, :], wk=[KY(wo, k)])
        gpm = A.alloc("gpm", [128, D], F32)
        P.dma("sp", gpm[:, :], g_post_mix.partition_broadcast(128))
        xt = [A.alloc("oxt%d" % i, [128, D], F32) for i in range(2)]
        jk = A.alloc("ojk", [128, D], BF16)
        junk = A.alloc("ojunk", [128, 512], BF16)
        hs = [A.alloc("ohs%d" % i, [128, D], BF16) for i in range(2)]
        stat = A.alloc("ostat", [128, 100], F32)
        P.memset("dve", stat[:, :], 0.0)
        for ti in range(17):
            rows = tile_rows(ti)
            tk0 = 128 * ti
            for half in range(2):
                ps = PS[(2 * ti + half) % 4]
                for k in range(8):
                    P.mm(ps[0:rows, :], mrgT[:, k, tk0:tk0 + rows], wo[:, k, half * 512:(half + 1) * 512], start=(k == 0), stop=(k == 7),
                         rk=[mrgT, KY(wo, k)])
                P.act(junk[0:rows, :], ps[0:rows, :], AF.Square, accum_out=stat[0:rows, 2 * ti + half:2 * ti + half + 1])
                P.copy("dve", hall[0:rows, ti, half * 512:(half + 1) * 512], ps[0:rows, :], wk=[KY(hall, ti)])
        sv = stat[:, 0:34].rearrange("p (t h) -> p t h", h=2)
        P.tt("dve", stat[:, 40:57], sv[:, :, 0], sv[:, :, 1], ALU.add)
        P.act(stat[:, 40:57], stat[:, 40:57], AF.Ln, scale=1.0 / D, bias=EPS)
        P.act(stat[:, 40:57], stat[:, 40:57], AF.Exp, scale=-0.5)
        for ti in range(17):
            rows = tile_rows(ti)
            P.dma("sp", xt[ti % 2][0:rows, :], x_rows(ti))
            P.stt("dve", hall[0:rows, ti, :], hall[0:rows, ti, :], stat[0:rows, 40 + ti:41 + ti], gpm[0:rows, :], ALU.mult, ALU.mult,
                  rk=[KY(hall, ti), stat, gpm], wk=[KY(hall, ti)])
            P.tt("pool", hall[0:rows, ti, 0:512], hall[0:rows, ti, 0:512], xt[ti % 2][0:rows, 0:512], ALU.add, rk=[KY(hall, ti), xt[ti % 2]],
                 wk=[KY(hall, ti, "a")])
            P.tt("dve", hall[0:rows, ti, 512:1024], hall[0:rows, ti, 512:1024], xt[ti % 2][0:rows, 512:1024], ALU.add, rk=[KY(hall, ti), xt[ti % 2]],
                 wk=[KY(hall, ti, "b"), hall])
            P.act(jk[0:rows, :], hall[0:rows, ti, :], AF.Square, accum_out=stat[0:rows, 60 + ti:61 + ti],
                  rk=[KY(hall, ti), KY(hall, ti, "a"), KY(hall, ti, "b")], wk=[jk, KY(stat, "b")])
        P.act(stat[:, 80:97], stat[:, 60:77], AF.Ln, scale=1.0 / D, bias=EPS, rk=[KY(stat, "b"), stat], wk=[KY(stat, "c")])
        P.act(stat[:, 80:97], stat[:, 80:97], AF.Exp, scale=-0.5, rk=[KY(stat, "c")], wk=[KY(stat, "c")])
        for ti in range(17):
            rows = tile_rows(ti)
            tk0 = 128 * ti
            hsb = hs[ti % 2]
            P.act(hsb[0:rows, :], hall[0:rows, ti, :], AF.Copy, scale=stat[0:rows, 80 + ti:81 + ti],
                  rk=[KY(hall, ti), KY(hall, ti, "a"), KY(hall, ti, "b"), KY(stat, "c")])
            pb = PSB[4 + ti % 2]
            for k in range(8):
                P.tr(pb[:, k * 128:k * 128 + rows], hsb[0:rows, k * 128:(k + 1) * 128], ident_bf[0:rows, 0:rows])
            src = pb[:, 0:1024].rearrange("p (k t) -> p k t", k=8)[:, :, 0:rows]
            P.tt("dve", hnT[:, :, tk0:tk0 + rows], src, bc_free(vc("g_pre_ffn", 0, 8), rows), ALU.mult)
        A.release(m0)

    phase_out()
    if stop == "out":
        P.lower()
        return nc
    if debug:
        P.dma("sp", dbg["h"][0:SEQ, :].rearrange("(t p) d -> p t d", p=128), hall[:, 0:16, :])
        P.dma("sp", dbg["h"][SEQ:NTOK, :], hall[0:64, 16, :])

    def phase_ffn():
        wfo = A.alloc_at("wfo", [128, 22, D], BF16, OFF_H_END)
        assert OFF_H_END + 22 * D * 2 <= OFF_HNT
        for j in range(22):
            P.dma("pool", wfo[:, j, :], w_ffn_out[128 * j:128 * j + 128, :], wk=[KY(wfo, j)])
        A.off = OFF_HNT_END
        actT = A.alloc("actT", [128, 22, 576], BF16)
        wfi = [A.alloc("wfi%d" % i, [128, 8, 2, 128], BF16) for i in range(3)]
        s2 = A.alloc("fs2", [128, 512], F32)
        A.off = SB_BASE
        gpf = A.alloc("gpf", [128, D], F32)
        P.dma("sp", gpf[:, :], g_post_ffn.partition_broadcast(128))
        thf_ = A.alloc("fth", [128, 512], F32)
        yt = [A.alloc("fyt%d" % i, [128, D], F32) for i in range(2)]
        junk = A.alloc("fjunk", [128, 512], BF16)
        stat = A.alloc("fstat", [128, 8], F32)
        wfv = w_ffn_in.rearrange("(k p) (u f) -> p k u f", p=128, u=2)
        PASSES = [(0, 512), (512, 512), (1024, 512), (1536, 576)]
        nblk = 0
        ycnt = 0
        for (p0, PW) in PASSES:
            subs = [(0, 512)] if PW == 512 else [(0, 512), (512, 64)]
            for j in range(22):
                wb = wfi[nblk % 3]
                nblk += 1
                for u in range(2):
                    P.dma("pool", wb[:, :, u, :], wfv[:, :, u, 128 * j:128 * j + 128], wk=[KY(wb, u)])
                for (s0, SW) in subs:
                    par = (nblk + (s0 > 0)) % 2
                    gps = PS[2 * par]
                    ups = PS[2 * par + 1]
                    for k in range(8):
                        P.mm(gps[:, 0:SW], wb[:, k, 0, :], hnT[:, k, p0 + s0:p0 + s0 + SW], start=(k == 0), stop=(k == 7), rk=[KY(wb, 0), hnT])
                    for k in range(8):
                        P.mm(ups[:, 0:SW], wb[:, k, 1, :], hnT[:, k, p0 + s0:p0 + s0 + SW], start=(k == 0), stop=(k == 7), rk=[KY(wb, 1), hnT])
                    P.act(thf_[:, 0:SW], gps[:, 0:SW], AF.Tanh, scale=0.5)
                    P.stt("dve", s2[:, 0:SW], thf_[:, 0:SW], 1.0, gps[:, 0:SW], ALU.add, ALU.mult)
                    P.stt("dve", actT[:, j, s0:s0 + SW], s2[:, 0:SW], 0.5, ups[:, 0:SW], ALU.mult, ALU.mult)
            ntl = PW // 128 + (1 if PW % 128 else 0)
            for tl in range(ntl):
                ti = p0 // 128 + tl
                rows = tile_rows(ti)
                c0 = 128 * tl
                yb = yt[ycnt % 2]
                ycnt += 1
                P.memset("dve", stat[:, :], 0.0)
                for half in range(2):
                    ps = PS[4 + 2 * (tl % 2) + half]
                    for j in range(22):
                        P.mm(ps[0:rows, :], actT[:, j, c0:c0 + rows], wfo[:, j, half * 512:(half + 1) * 512], start=(j == 0), stop=(j == 21),
                             rk=[actT, KY(wfo, j)])
                    P.act(junk[0:rows, :], ps[0:rows, :], AF.Square, accum_out=stat[0:rows, half:half + 1])
                P.tt("dve", stat[0:rows, 2:3], stat[0:rows, 0:1], stat[0:rows, 1:2], ALU.add)
                P.act(stat[0:rows, 3:4], stat[0:rows, 2:3], AF.Ln, scale=1.0 / D, bias=EPS)
                P.act(stat[0:rows, 3:4], stat[0:rows, 3:4], AF.Exp, scale=-0.5)
                for half in range(2):
                    ps = PS[4 + 2 * (tl % 2) + half]
                    P.stt("dve", yb[0:rows, half * 512:(half + 1) * 512], ps[0:rows, :], stat[0:rows, 3:4], gpf[0:rows, half * 512:(half + 1) * 512],
                          ALU.mult, ALU.mult)
                P.tt("pool", yb[0:rows, :], yb[0:rows, :], hall[0:rows, ti, :], ALU.add)
                if ti < 16:
                    P.dma("sp", y_prompt[128 * ti:128 * ti + 128, :], yb[:, :])
                else:
                    P.dma("sp", y_sample[:, :], yb[0:64, :])

    phase_ffn()

    P.lower()
    return nc


_CACHE = {}


def _get_prog(debug=False):
    if debug not in _CACHE:
        _CACHE[debug] = build_program(debug)
    return _CACHE[debug]


def kernel(**inputs):
    return run(inputs, debug=False)


def run(inputs, debug=False):
    nc = _get_prog(debug)
    f = lambda a: np.ascontiguousarray(np.asarray(a, dtype=np.float32))
    in_maps = []
    for c in range(NCORES):
        sl = slice(NS * c, NS * c + NS)
        m = {
            "x_prompt": f(inputs["x_prompt"][c]),
            "x_sample": f(inputs["x_sample"][sl]).reshape(NS * TS, D),
            "mem_prompt": f(inputs["mem_prompt"][c]),
            "cache_mem_k": f(inputs["cache_mem_k"][0, sl]).reshape(NS, MEM, 512),
            "cache_mem_v": f(inputs["cache_mem_v"][0, sl]).reshape(NS, MEM, 512),
            "state_hgrn": f(inputs["state_hgrn"][0, sl]),
            "state_gdn": f(inputs["state_gdn"][0, sl]),
            "state_gdn_conv": f(inputs["state_gdn_conv"][0, sl]),
            "hg_lb_logits": f(inputs["hg_lb_logits"]),
            "g_pre_mix": f(inputs["g_pre_mix"][0]),
            "w_in": f(inputs["w_in"][0]),
            "w_conv": f(inputs["w_conv"][0]),
            "a_log": f(inputs["a_log"][0]),
            "dt_bias": f(inputs["dt_bias"][0]),
            "g_hg_out": f(inputs["g_hg_out"][0]),
            "g_gdn_out": f(inputs["g_gdn_out"][0]),
            "g_mem": f(inputs["g_mem"][0]),
            "w_mem_kv": f(inputs["w_mem_kv"][0]),
            "w_br_hg": f(inputs["w_br_hg"][0]),
            "w_br_gdn": f(inputs["w_br_gdn"][0]),
            "w_br_mem": f(inputs["w_br_mem"][0]),
            "w_out": f(inputs["w_out"][0]),
            "g_post_mix": f(inputs["g_post_mix"][0]),
            "g_pre_ffn": f(inputs["g_pre_ffn"][0]),
            "w_ffn_in": f(inputs["w_ffn_in"][0]),
            "w_ffn_out": f(inputs["w_ffn_out"][0]),
            "g_post_ffn": f(inputs["g_post_ffn"][0]),
            "consts": CONST_ARR,
            "lmasks": LMASK_ARR,
        }
        in_maps.append(m)
    res = run_bass_kernel_spmd(nc, in_maps, core_ids=list(range(NCORES)))
    R = res.results
    if debug:
        return R
    yp = np.stack([R[c]["y_prompt"] for c in range(NCORES)], 0)
    ys = np.concatenate([R[c]["y_sample"].reshape(NS, TS, D) for c in range(NCORES)], 0)
    mk = np.stack([R[c]["new_mem_k"].reshape(MEM, 4, 128) for c in range(NCORES)], 0)[None]
    mv = np.stack([R[c]["new_mem_v"].reshape(MEM, 4, 128) for c in range(NCORES)], 0)[None]
    hgp = np.stack([R[c]["new_hg_p"] for c in range(NCORES)], 0)[None]
    gdp = np.stack([R[c]["new_gdn_p"] for c in range(NCORES)], 0)[None]
    cvp = np.stack([R[c]["new_conv_p"] for c in range(NCORES)], 0)[None]
    hgs = np.concatenate([R[c]["new_hg_s"] for c in range(NCORES)], 0)[None]
    gds = np.concatenate([R[c]["new_gdn_s"] for c in range(NCORES)], 0)[None]
    cvs = np.concatenate([R[c]["new_conv_s"] for c in range(NCORES)], 0)[None]
    return (yp, ys, mk, mv, hgp, gdp, cvp, hgs, gds, cvs)
```
